# Optimizing a Trainium2 kernel written in Bass

```python
import jax, jax.numpy as jnp
from jax import lax
import numpy as np

D_MODEL = 1024
BATCH = 1
SEQ = 16384
DEPTH = 1

N_HEADS = 8
QK_NOPE_DIM = 64
QK_ROPE_DIM = 32
V_HEAD_DIM = 64
Q_LORA_RANK = 384
KV_LORA_RANK = 256
QK_HEAD_DIM = QK_NOPE_DIM + QK_ROPE_DIM
ATTN_WIDTH = N_HEADS * V_HEAD_DIM
ROPE_BASE = 10000.0
Q_BLOCK = 128

POOL_WINDOWS = (2, 4, 8, 16)
POOL_GROUPS = len(POOL_WINDOWS)
POOL_GROUP_DIM = 128
POOL_WIDTH = POOL_GROUPS * POOL_GROUP_DIM

D_FF = ((8 * D_MODEL + 3 * 256 - 1) // (3 * 256)) * 256

NORM_EPS = 1e-6

IN_SPLITS = (Q_LORA_RANK, KV_LORA_RANK, QK_ROPE_DIM, POOL_WIDTH, D_MODEL, D_MODEL)
IN_WIDTH = sum(IN_SPLITS)

kernel_name = "hybrid_mla_pool_gated_encoder_block"


def rms_norm(x, g):
    xf = x.astype(jnp.float32)
    y = xf * lax.rsqrt(jnp.mean(xf * xf, axis=-1, keepdims=True) + NORM_EPS)
    return (y * g.astype(jnp.float32)).astype(x.dtype)


def rope_tables(positions):
    inv = 1.0 / (ROPE_BASE ** (jnp.arange(0, QK_ROPE_DIM, 2, dtype=jnp.float32) / QK_ROPE_DIM))
    ang = positions.astype(jnp.float32)[..., None] * inv
    return jnp.cos(ang), jnp.sin(ang)


def apply_rope(t, cos, sin):
    tf = t.astype(jnp.float32)
    t1, t2 = jnp.split(tf, 2, axis=-1)
    out = jnp.concatenate([t1 * cos - t2 * sin, t2 * cos + t1 * sin], axis=-1)
    return out.astype(t.dtype)


def split_cols(p):
    offs = np.cumsum((0,) + IN_SPLITS)
    return [p[..., int(offs[i]):int(offs[i + 1])] for i in range(len(IN_SPLITS))]


def mla_attention(q_nope, q_rope, k_nope, k_rope, v):
    B, S, H, _ = q_nope.shape
    nb = S // Q_BLOCK
    scale = QK_HEAD_DIM ** -0.5

    def block(args):
        qn, qr = args
        s = (jnp.einsum('bqhd,bkhd->bhqk', qn, k_nope)
             + jnp.einsum('bqhr,bkr->bhqk', qr, k_rope))
        p = jax.nn.softmax(s.astype(jnp.float32) * scale, axis=-1).astype(v.dtype)
        return jnp.einsum('bhqk,bkhd->bqhd', p, v)

    qn_b = q_nope.reshape(B, nb, Q_BLOCK, H, QK_NOPE_DIM).transpose(1, 0, 2, 3, 4)
    qr_b = q_rope.reshape(B, nb, Q_BLOCK, H, QK_ROPE_DIM).transpose(1, 0, 2, 3, 4)
    out = lax.map(block, (qn_b, qr_b))
    return out.transpose(1, 0, 2, 3, 4).reshape(B, S, H * V_HEAD_DIM)


def multiscale_pool(u, w_pool_group, pool_scale):
    B, S, _ = u.shape
    uf = u.astype(jnp.float32)
    t = jnp.arange(S)
    outs = []
    for g, w in enumerate(POOL_WINDOWS):
        left = w // 2
        right = w - left - 1
        seg = uf[..., g * POOL_GROUP_DIM:(g + 1) * POOL_GROUP_DIM]
        cs = jnp.cumsum(jnp.pad(seg, ((0, 0), (left + 1, right), (0, 0))), axis=1)
        win_sum = cs[:, w:w + S] - cs[:, :S]
        cnt = (jnp.minimum(t + right, S - 1) - jnp.maximum(t - left, 0) + 1).astype(jnp.float32)
        outs.append(win_sum / cnt[None, :, None] - seg)
    pooled = jnp.stack(outs, axis=2).astype(u.dtype)
    mixed = jnp.einsum('bsgc,gcd->bsgd', pooled, w_pool_group).reshape(B, S, POOL_WIDTH)
    return mixed * pool_scale


def setup_inputs(seed: int = 0) -> dict:
    key = jax.random.key(seed)
    ks = jax.random.split(key, 20)
    f32 = jnp.float32

    def w(k, shape, fan_in):
        return jax.random.normal(k, shape, f32) * (fan_in ** -0.5)

    def gain(k, n):
        return 1.0 + 0.02 * jax.random.normal(k, (n,), f32)

    x = jax.random.normal(ks[0], (BATCH, SEQ, D_MODEL), f32)
    positions = jnp.broadcast_to(jnp.arange(SEQ, dtype=jnp.int32)[None, :], (BATCH, SEQ))
    return {
        "x": x,
        "positions": positions,
        "g_mix_pre": gain(ks[1], D_MODEL),
        "w_in": w(ks[2], (D_MODEL, IN_WIDTH), D_MODEL),
        "g_q_lat": gain(ks[3], Q_LORA_RANK),
        "w_uq": w(ks[4], (Q_LORA_RANK, N_HEADS * QK_HEAD_DIM), Q_LORA_RANK),
        "g_kv_lat": gain(ks[5], KV_LORA_RANK),
        "w_ukv": w(ks[6], (KV_LORA_RANK, N_HEADS * (QK_NOPE_DIM + V_HEAD_DIM)), KV_LORA_RANK),
        "w_o_attn": w(ks[7], (ATTN_WIDTH, D_MODEL), ATTN_WIDTH),
        "w_pool_group": w(ks[8], (POOL_GROUPS, POOL_GROUP_DIM, POOL_GROUP_DIM), POOL_GROUP_DIM),
        "pool_scale": gain(ks[9], POOL_WIDTH),
        "w_o_pool": w(ks[10], (POOL_WIDTH, D_MODEL), POOL_WIDTH),
        "w_out": w(ks[11], (D_MODEL, D_MODEL), D_MODEL),
        "g_mix_post": gain(ks[12], D_MODEL),
        "g_ffn_pre": gain(ks[13], D_MODEL),
        "w_gate_up": w(ks[14], (D_MODEL, 2 * D_FF), D_MODEL),
        "w_down": w(ks[15], (D_FF, D_MODEL), D_FF),
        "g_ffn_post": gain(ks[16], D_MODEL),
    }


def reference(x, positions, g_mix_pre, w_in, g_q_lat, w_uq, g_kv_lat, w_ukv, w_o_attn,
              w_pool_group, pool_scale, w_o_pool, w_out, g_mix_post, g_ffn_pre,
              w_gate_up, w_down, g_ffn_post):
    B, S, _ = x.shape
    cos, sin = rope_tables(positions)
    for _layer in range(DEPTH):
        h = rms_norm(x, g_mix_pre)
        c_q, c_kv, k_rope, u_pool, gate_a, gate_b = split_cols(h @ w_in)

        q = (rms_norm(c_q, g_q_lat) @ w_uq).reshape(B, S, N_HEADS, QK_HEAD_DIM)
        q_nope = q[..., :QK_NOPE_DIM]
        q_rope = apply_rope(q[..., QK_NOPE_DIM:], cos[:, :, None, :], sin[:, :, None, :])
        kv = (rms_norm(c_kv, g_kv_lat) @ w_ukv).reshape(B, S, N_HEADS, QK_NOPE_DIM + V_HEAD_DIM)
        k_nope = kv[..., :QK_NOPE_DIM]
        v = kv[..., QK_NOPE_DIM:]
        k_rope = apply_rope(k_rope, cos, sin)
        attn = mla_attention(q_nope, q_rope, k_nope, k_rope, v)

        pool = multiscale_pool(u_pool, w_pool_group, pool_scale)

        merged = (jax.nn.sigmoid(gate_a) * (attn @ w_o_attn)
                  + jax.nn.sigmoid(gate_b) * (pool @ w_o_pool))
        x = x + rms_norm(merged @ w_out, g_mix_post)

        hf = rms_norm(x, g_ffn_pre)
        gu = hf @ w_gate_up
        ff = (jax.nn.silu(gu[..., :D_FF]) * gu[..., D_FF:]) @ w_down
        x = x + rms_norm(ff, g_ffn_post)
    return x
```

```python
import math
from contextlib import ExitStack

import numpy as np
import concourse.bass as bass
import concourse.mybir as mybir
from concourse.bass_utils import run_bass_kernel_spmd

ENGS = ("pe", "act", "dve", "pool", "sp")


class Buf:
    __slots__ = ("name", "writers", "readers")

    def __init__(self, name):
        self.name = name
        self.writers = []
        self.readers = []


class DSem:
    __slots__ = ("sem", "count", "final")

    def __init__(self, sem, final=False):
        self.sem = sem
        self.count = 0
        self.final = final


class Op:
    __slots__ = ("eng", "fn", "deps", "signal", "sigval", "dsem", "dval", "idx")

    def __init__(self, eng, fn):
        self.eng = eng
        self.fn = fn
        self.deps = []
        self.signal = False
        self.sigval = None
        self.dsem = None
        self.dval = None


class Sched:
    def __init__(self, nc, stack):
        self.nc = nc
        self.stack = stack
        self.q = {e: [] for e in ENGS}
        self.esem = {e: stack.enter_context(nc.semaphore("es_" + e)) for e in ENGS}
        self.dmas = []
        self.marks = {}
        self.nds = 0

    def mark(self, name):
        self.marks[name] = {e: len(self.q[e]) for e in ENGS}

    def cut(self, name):
        for e in ENGS:
            del self.q[e][self.marks[name][e]:]

    def dsem(self, final=False):
        self.nds += 1
        return DSem(self.stack.enter_context(self.nc.semaphore("ds%d" % self.nds)), final)

    def _track(self, op, reads, writes, group_dma_writes=False):
        deps = set()
        for b in reads:
            for w in b.writers:
                deps.add((w, True))
        for b in writes:
            for w in b.writers:
                if group_dma_writes and w.dsem is not None and not b.readers and w.dsem is op.dsem:
                    continue
                deps.add((w, False))
            for r in b.readers:
                deps.add((r, False))
        for d, raw in deps:
            if d is op:
                continue
            if d.dsem is None and op.dsem is None and d.eng == op.eng:
                if not (raw and op.eng in ("act", "dve", "pool")):
                    continue
            op.deps.append(d)
            if d.dsem is None:
                d.signal = True
        for b in reads:
            b.readers.append(op)
        for b in writes:
            if group_dma_writes and b.writers and all(
                    w.dsem is not None and w.dsem is op.dsem for w in b.writers) and not b.readers:
                b.writers.append(op)
            else:
                b.writers = [op]
                b.readers = []

    def op(self, eng, fn, reads=(), writes=()):
        o = Op(eng, fn)
        self._track(o, reads, writes)
        self.q[eng].append(o)
        return o

    def dma(self, eng, out, in_, dsem, reads=(), writes=()):
        def fn(e, out=out, in_=in_):
            return e.dma_start(out=out, in_=in_)
        o = Op(eng, fn)
        o.dsem = dsem
        dsem.count += 16
        o.dval = dsem.count
        self._track(o, reads, writes, group_dma_writes=True)
        self.q[eng].append(o)
        self.dmas.append(o)
        return o

    def barrier(self):
        lasts = []
        for e in ENGS:
            for o in reversed(self.q[e]):
                if o.dsem is None and o.fn is not None:
                    o.signal = True
                    lasts.append(o)
                    break
        dm = {}
        for o in self.dmas:
            dm[id(o.dsem)] = o
        self.dmas = []
        for e in ENGS:
            def fn(eng):
                return None
            b = Op(e, None)
            for l in lasts:
                if l.eng != e:
                    b.deps.append(l)
            b.deps.extend(dm.values())
            self.q[e].append(b)

    def emit(self, engines):
        for e in ENGS:
            c = 0
            for o in self.q[e]:
                if o.dsem is None and o.signal and o.fn is not None:
                    c += 1
                    o.sigval = c

    def _waits(self, o):
        waits = {}
        for d in o.deps:
            if d.dsem is not None:
                key = id(d.dsem)
                val = d.dsem.count if d.dsem.final else d.dval
                sem = d.dsem.sem
            else:
                key = d.eng
                val = d.sigval
                sem = self.esem[d.eng]
            assert val is not None, (o.eng, d.eng, d.fn)
            if key not in waits or waits[key][1] < val:
                waits[key] = (sem, val)
        return waits

    def check(self):
        cnt = {}
        ptr = {e: 0 for e in ENGS}
        total = sum(len(self.q[e]) for e in ENGS)
        done = 0
        while done < total:
            prog = False
            for e in ENGS:
                while ptr[e] < len(self.q[e]):
                    o = self.q[e][ptr[e]]
                    ok = all(cnt.get(k, 0) >= v for k, (s_, v) in self._waits(o).items())
                    if not ok:
                        break
                    if o.fn is not None:
                        if o.dsem is not None:
                            cnt[id(o.dsem)] = cnt.get(id(o.dsem), 0) + 16
                        elif o.signal:
                            cnt[e] = cnt.get(e, 0) + 1
                    ptr[e] += 1
                    done += 1
                    prog = True
            if not prog:
                st = {e: (ptr[e], len(self.q[e])) for e in ENGS}
                raise RuntimeError("schedule deadlock: %s" % st)
        return True

    def play(self, e, eng):
        seen = {}
        last_real = None
        for o in self.q[e]:
            waits = {}
            for d in o.deps:
                if d.dsem is not None:
                    key = id(d.dsem)
                    val = d.dsem.count if d.dsem.final else d.dval
                    sem = d.dsem.sem
                else:
                    key = d.eng
                    val = d.sigval
                    sem = self.esem[d.eng]
                if key not in waits or waits[key][1] < val:
                    waits[key] = (sem, val)
            for key, (sem, val) in waits.items():
                if seen.get(key, 0) >= val:
                    continue
                seen[key] = val
                eng.wait_ge(sem, val)
            if o.fn is None:
                continue
            ins = o.fn(eng)
            if o.dsem is not None:
                ins.then_inc(o.dsem.sem, 16)
            elif o.signal:
                ins.then_inc(self.esem[e], 1)


F32 = mybir.dt.float32
BF16 = mybir.dt.bfloat16
I32 = mybir.dt.int32
AF = mybir.ActivationFunctionType
ALU = mybir.AluOpType

S_TOK = 16384
OWN = 2048
NCORES = 8
EPS = 1e-6
G_PRE, G_Q, G_KV, G_PS, G_POST, G_FPRE, G_FPOST = 0, 8, 11, 13, 17, 25, 33
POOL_W = (2, 4, 8, 16)


def build_program(debug=False, cut=None, nta=None, nchunks=None):
    nc = bass.Bass("TRN2", target_bir_lowering=False)

    def din(name, shape, dt=F32):
        return nc.dram_tensor(name, shape, dt, kind="ExternalInput").ap()

    xT = din("xT", [1024, S_TOK])
    xh = din("xh", [1024, 16])
    pos = din("pos", [1, S_TOK], I32)
    cst = din("cst", [128, 4])
    gains = din("gains", [128, 41])
    invcnt = din("invcnt", [4, OWN])
    w_in = din("w_in", [1024, 3232])
    w_uq = din("w_uq", [384, 768])
    w_ukv = din("w_ukv", [256, 1024])
    w_o_attn = din("w_o_attn", [512, 1024])
    w_pool_group = din("w_pool_group", [4, 128, 128])
    w_o_pool = din("w_o_pool", [512, 1024])
    w_out = din("w_out", [1024, 1024])
    w_gate_up = din("w_gate_up", [1024, 5632])
    w_down = din("w_down", [2816, 1024])
    outT = nc.dram_tensor("outT", [1024, OWN], F32, kind="ExternalOutput").ap()
    skind = dict(kind="ExternalOutput") if debug else {}
    kS = nc.dram_tensor("kS", [512, S_TOK], BF16, **skind).ap()
    krS = nc.dram_tensor("krS", [32, S_TOK], BF16, **skind).ap()
    vS = nc.dram_tensor("vS", [8, 128, 128, 65], BF16, **skind).ap()
    cosS = nc.dram_tensor("cosS", [32, S_TOK], F32, **skind).ap()
    sinS = nc.dram_tensor("sinS", [32, S_TOK], F32, **skind).ap()
    x1S = nc.dram_tensor("x1S", [1024, OWN], F32, **skind).ap()
    if debug:
        qtD = nc.dram_tensor("qtD", [128, 8, OWN], BF16, kind="ExternalOutput").ap()
        atD = nc.dram_tensor("atD", [128, 8, OWN], BF16, kind="ExternalOutput").ap()
        pmD = nc.dram_tensor("pmD", [128, 4, OWN], BF16, kind="ExternalOutput").ap()

    xTv = xT.rearrange("(k p) n -> p k n", p=128)
    xhv = xh.rearrange("(k p) n -> p k n", p=128)
    w_inv = w_in.rearrange("(k p) n -> p k n", p=128)
    outTv = outT.rearrange("(k p) n -> p k n", p=128)
    x1Sv = x1S.rearrange("(k p) n -> p k n", p=128)
    kSv = kS.rearrange("(j q) t -> q j t", q=128)
    vSv = vS.rearrange("h p t d -> p h t d")

    with ExitStack() as gst:
        S = Sched(nc, gst)
        ps = gst.enter_context(nc.psum_tensor("ps", [128, 4096], F32))
        PB = [Buf("pb%d" % b) for b in range(8)]

        def bk(b):
            return ps[:, 512 * b:512 * (b + 1)]

        class T:
            def __init__(self, st, name, shape, dt, nb=1):
                self.t = st.enter_context(nc.sbuf_tensor("sb_" + name, shape, dt))
                self.b = [Buf(name + str(i)) for i in range(nb)]

            def __getitem__(self, k):
                return self.t[k]

            @property
            def B(self):
                return self.b[0]

        def pe(mms, reads, writes):
            def fn(e, mms=mms):
                ins = None
                for (o, l, r, st_, sp_) in mms:
                    ins = e.matmul(o, l, r, start=st_, stop=sp_)
                return ins
            return S.op("pe", fn, reads, writes)

        def act(out, in_, func, reads, writes, scale=1.0, bias=0.0):
            return S.op("act", lambda e: e.activation(out, in_, func, bias=bias, scale=scale), reads, writes)

        def dve(fn, reads, writes):
            return S.op("dve", fn, reads, writes)

        def tt(out, a, b, op, reads, writes, eng="dve"):
            return S.op(eng, lambda e: e.tensor_tensor(out, a, b, op), reads, writes)

        def stt(out, in0, sc, in1, op0, op1, reads, writes):
            return S.op("dve", lambda e: e.scalar_tensor_tensor(out, in0, sc, in1, op0, op1), reads, writes)

        def ts(out, in0, s1, s2, op0, op1, reads, writes):
            if s2 is None:
                return S.op("dve", lambda e: e.tensor_scalar(out, in0, s1, None, op0), reads, writes)
            return S.op("dve", lambda e: e.tensor_scalar(out, in0, s1, s2, op0, op1), reads, writes)

        cstT = T(gst, "cst", [128, 4], F32)
        gT = T(gst, "gains", [128, 41], F32)
        ones = T(gst, "ones", [128, 128], BF16)
        onesf = T(gst, "onesf", [128, 128], F32)
        dC = S.dsem(final=True)
        S.dma("sp", cstT[:], cst, dC, writes=[cstT.B])
        S.dma("sp", gT[:], gains, dC, writes=[gT.B])
        S.op("pool", lambda e: e.memset(ones[:], 1.0), writes=[ones.B])
        S.op("pool", lambda e: e.memset(onesf[:], 1.0), writes=[onesf.B])
        epsb = cstT[:, 3:4]

        def gcol(off, k, p0=0, p1=128):
            return gT[p0:p1, off + k:off + k + 1]

        def rstd_ops(out_ap, out_B, ss_ap, ss_B, sc):
            act(out_ap, ss_ap, AF.Sqrt, [ss_B, cstT.B], [out_B], scale=sc, bias=epsb)
            dve(lambda e: e.reciprocal(out_ap, out_ap), [out_B], [out_B])

        def tot_ops(out_ap, out_B, ss_ap, ss_B, a_ap, a_B, nf):
            stt(out_ap, ss_ap, 1.0 / nf, a_ap, ALU.mult, ALU.add, [ss_B, a_B], [out_B])
            act(out_ap, out_ap, AF.Sqrt, [out_B], [out_B])
            dve(lambda e: e.reciprocal(out_ap, out_ap), [out_B], [out_B])

        dStage = [S.dsem(), S.dsem()]
        stage_ctr = [0]

        def load_folded(st, stg, dst_fn, src_fn, ncols, goff):
            for k in range(8):
                i = stage_ctr[0] % 2
                stage_ctr[0] += 1
                S.dma("sp", stg[i][:, 0:ncols], src_fn(k), dStage[i], writes=[stg[i].B])
                ts(dst_fn(k), stg[i][:, 0:ncols], gcol(goff, k), None, ALU.mult, None,
                   [stg[i].B, gT.B], [])

        with ExitStack() as st:
            I = T(st, "ti", [128, 4096], I32)
            X = T(st, "tx", [128, 4096], F32)
            Y = T(st, "ty", [128, 4096], F32)
            Z = T(st, "tz", [128, 4096], F32)
            W = T(st, "tw", [128, 4096], F32)
            d0 = S.dsem(final=True)
            for s in range(4):
                S.dma("sp", I[32 * s:32 * s + 32, :],
                      bass.AP(pos.tensor, 4096 * s, [[0, 32], [1, 4096]]),
                      d0, writes=[I.B])
            C1 = 6.28125
            C2 = 2 * math.pi - C1
            dve(lambda e: e.tensor_copy(X[:], I[:]), [I.B], [X.B])
            ts(X[:], X[:], cstT[:, 0:1], None, ALU.mult, None, [X.B, cstT.B], [X.B])
            ts(I[:], X[:], 1.0 / (2 * math.pi), None, ALU.mult, None, [X.B], [I.B])
            dve(lambda e: e.tensor_copy(Y[:], I[:]), [I.B], [Y.B])
            stt(Z[:], Y[:], -C1, X[:], ALU.mult, ALU.add, [Y.B, X.B], [Z.B])
            stt(Z[:], Y[:], -C2, Z[:], ALU.mult, ALU.add, [Y.B, Z.B], [Z.B])
            ts(Y[:], Z[:], math.pi, -2 * math.pi, ALU.is_gt, ALU.mult, [Z.B], [Y.B])
            tt(Z[:], Z[:], Y[:], ALU.add, [Z.B, Y.B], [Z.B])
            act(X[:], Z[:], AF.Sin, [Z.B, cstT.B], [X.B], scale=cstT[:, 1:2])
            dT = S.dsem(final=True)
            BtabS = Buf("tabS")
            for s in range(4):
                S.dma("sp", sinS[:, 4096 * s:4096 * (s + 1)], X[32 * s:32 * s + 32, :], dT, reads=[X.B], writes=[BtabS])
            ts(Y[:], Z[:], math.pi / 2, None, ALU.add, None, [Z.B], [Y.B])
            ts(W[:], Y[:], math.pi, -2 * math.pi, ALU.is_gt, ALU.mult, [Y.B], [W.B])
            tt(Y[:], Y[:], W[:], ALU.add, [Y.B, W.B], [Y.B])
            act(W[:], Y[:], AF.Sin, [Y.B], [W.B])
            for s in range(4):
                S.dma("sp", cosS[:, 4096 * s:4096 * (s + 1)], W[32 * s:32 * s + 32, :], dT, reads=[W.B], writes=[BtabS])
            S.barrier()
            S.mark("p0")

        def prep_tile(xb, sq, r1, a1, src_ap, n, dX, need_a=True):
            S.dma("pool", xb[:, :, 0:n], src_ap, dX, writes=[xb.B])
            act(sq[:, :, 0:n], xb[:, :, 0:n], AF.Square, [xb.B], [sq.B])
            pe([(bk(0)[:, 0:n], ones[:], sq[:, k, 0:n], k == 0, k == 7) for k in range(8)],
               [sq.B, ones.B], [PB[0]])
            rstd_ops(r1[:, 0:n], r1.B, bk(0)[:, 0:n], PB[0], 1.0 / 1024)
            if need_a:
                ts(a1[:, 0:n], bk(0)[:, 0:n], EPS / 1024, EPS * EPS, ALU.mult, ALU.add, [PB[0], r1.B], [a1.B])

        with ExitStack() as bq:
            AT = T(bq, "AT", [128, 8, OWN], BF16)
            PM = T(bq, "PM", [128, 4, OWN], BF16)
            qsc = ExitStack()
            QT = T(qsc, "QT", [128, 8, OWN], BF16)

            with ExitStack() as st:
                stg = [T(st, "stg%d" % i, [128, 512], F32) for i in range(2)]
                wkv = T(st, "wkv", [128, 8, 320], BF16)
                wk = T(st, "wk", [128, 2, 512], BF16)
                wv = T(st, "wv", [128, 2, 512], BF16)
                xb = [T(st, "xb%d" % i, [128, 8, 512], BF16) for i in range(2)]
                sq = [T(st, "sq%d" % i, [128, 8, 512], BF16) for i in range(2)]
                r1 = [T(st, "r1%d" % i, [128, 512], F32) for i in range(2)]
                a1 = [T(st, "a1%d" % i, [128, 512], F32) for i in range(2)]
                sqr = [T(st, "sqr%d" % i, [128, 2, 512], BF16) for i in range(2)]
                tot = T(st, "tot", [128, 512], F32)
                ckvn = [T(st, "ckvn%d" % i, [128, 2, 512], BF16) for i in range(2)]
                kst = [T(st, "kst%d" % i, [128, 4, 512], BF16) for i in range(2)]
                vst = [T(st, "vst%d" % i, [128, 8, 4, 65], BF16) for i in range(2)]
                ct = [T(st, "ct%d" % i, [32, 512], F32) for i in range(2)]
                sn = [T(st, "sn%d" % i, [32, 512], F32) for i in range(2)]
                krs = [T(st, "krs%d" % i, [32, 512], BF16) for i in range(2)]
                t1 = T(st, "t1", [32, 512], F32)
                t2 = T(st, "t2", [32, 512], F32)
                dW = S.dsem(final=True)
                dX = [S.dsem(), S.dsem()]
                dTab = [S.dsem(), S.dsem()]
                dTabS = [S.dsem(), S.dsem()]
                dKs = [S.dsem(), S.dsem()]
                dVs = [S.dsem(), S.dsem()]
                dKr = [S.dsem(), S.dsem()]
                BscrK, BscrV, BscrR = Buf("scrK"), Buf("scrV"), Buf("scrR")

                load_folded(st, stg, lambda k: wkv[:, k, 0:288], lambda k: w_inv[:, k, 384:672], 288, G_PRE)
                load_folded(st, stg, lambda k: wkv[:, k, 288:304], lambda k: w_inv[:, k, 656:672], 16, G_PRE)
                load_folded(st, stg, lambda k: wkv[:, k, 304:320], lambda k: w_inv[:, k, 640:656], 16, G_PRE)
                wkv.B.writers = [S.q["dve"][-1]]
                ukv = w_ukv.rearrange("(k p) (h two d) -> p k h two d", p=128, two=2, d=64)
                for kc in range(2):
                    S.dma("pool", wk[:, kc, :].rearrange("p (h d) -> p h d", d=64), ukv[:, kc, :, 0, :], dW, writes=[wk.B])
                    S.dma("pool", wv[:, kc, :].rearrange("p (h d) -> p h d", d=64), ukv[:, kc, :, 1, :], dW, writes=[wv.B])
                for i in range(2):
                    S.op("pool", lambda e, i=i: e.memset(vst[i][:], 1.0), writes=[vst[i].B])

                NT = nta or (S_TOK // 512)

                def stage1(i):
                    p = i % 2
                    c0 = 512 * i
                    prep_tile(xb[p], sq[p], r1[p], a1[p], xTv[:, :, c0:c0 + 512], 512, dX[p])
                    S.dma("sp", ct[p][:], cosS[:, c0:c0 + 512], dTab[p], reads=[BtabS], writes=[ct[p].B])
                    S.dma("sp", sn[p][:], sinS[:, c0:c0 + 512], dTabS[p], reads=[BtabS], writes=[sn[p].B])
                    for c in range(2):
                        pe([(bk(1 + c), wkv[:, k, 128 * c:128 * c + 128], xb[p][:, k, :], k == 0, k == 7) for k in range(8)],
                           [wkv.B, xb[p].B], [PB[1 + c]])
                        act(sqr[p][:, c, :], bk(1 + c), AF.Square, [PB[1 + c]], [sqr[p].B])
                    pe([(bk(3)[0:32, :], wkv[:, k, 256:288], xb[p][:, k, :], k == 0, k == 7) for k in range(8)],
                       [wkv.B, xb[p].B], [PB[3]])
                    pe([(bk(4)[0:32, :], wkv[:, k, 288:320], xb[p][:, k, :], k == 0, k == 7) for k in range(8)],
                       [wkv.B, xb[p].B], [PB[4]])
                    tt(t1[:], bk(3)[0:32, :], ct[p][:], ALU.mult, [PB[3], ct[p].B], [t1.B])
                    tt(t2[:], bk(4)[0:32, :], sn[p][:], ALU.mult, [PB[4], sn[p].B], [t2.B])
                    tt(t1[:], t1[:], t2[:], ALU.add, [t1.B, t2.B], [t1.B])
                    tt(krs[p][:], t1[:], r1[p][0:32, :], ALU.mult, [t1.B, r1[p].B], [krs[p].B])
                    S.dma("sp", krS[:, c0:c0 + 512], krs[p][:], dKr[p], reads=[krs[p].B], writes=[BscrR])

                def stage2(i):
                    p = i % 2
                    pe([(bk(0), ones[:], sqr[p][:, c, :], c == 0, c == 1) for c in range(2)],
                       [ones.B, sqr[p].B], [PB[0]])
                    tot_ops(tot[:], tot.B, bk(0), PB[0], a1[p][:], a1[p].B, 256)
                    for c in range(2):
                        stt(ckvn[p][:, c, :], bk(1 + c), gcol(G_KV, c), tot[:], ALU.mult, ALU.mult,
                            [PB[1 + c], gT.B, tot.B], [ckvn[p].B])

                rot = [0]

                def rb():
                    b = 5 + rot[0] % 3
                    rot[0] += 1
                    return b

                def stage3(i):
                    p = i % 2
                    c0 = 512 * i
                    for j in range(4):
                        b = rb()
                        pe([(bk(b), wk[:, kc, 128 * j:128 * j + 128], ckvn[p][:, kc, :], kc == 0, kc == 1) for kc in range(2)],
                           [wk.B, ckvn[p].B], [PB[b]])
                        act(kst[p][:, j, :], bk(b), AF.Copy, [PB[b]], [kst[p].B])
                    for s in range(4):
                        b = rb()
                        pe([(bk(b), ckvn[p][:, kc, 128 * s:128 * s + 128], wv[:, kc, :], kc == 0, kc == 1) for kc in range(2)],
                           [wv.B, ckvn[p].B], [PB[b]])
                        dve(lambda e, b=b, s=s, p=p: e.tensor_copy(vst[p][:, :, s, 0:64], bk(b).rearrange("p (h d) -> p h d", d=64)),
                            [PB[b]], [vst[p].B])
                    S.dma("sp", kSv[:, :, c0:c0 + 512], kst[p][:], dKs[p], reads=[kst[p].B], writes=[BscrK])
                    S.dma("sp", vSv[:, :, 4 * i:4 * i + 4, :], vst[p][:], dVs[p], reads=[vst[p].B], writes=[BscrV])

                def dbg_mark(nm):
                    if cut == nm:
                        S.barrier()
                        S.mark(nm)
                dbg_mark("Aw")
                stage1(0)
                dbg_mark("A1")
                for i in range(NT):
                    stage2(i)
                    if i == 0:
                        dbg_mark("A2")
                    if i + 1 < NT:
                        stage1(i + 1)
                    stage3(i)
                    if i == 0:
                        dbg_mark("A3")
                S.barrier()
                S.mark("A")

            with ExitStack() as st:
                stg = [T(st, "stgq%d" % i, [128, 512], F32) for i in range(2)]
                wcq = T(st, "wcq", [128, 8, 384], BF16)
                wu = T(st, "wu", [128, 8, 512], BF16)
                wq = T(st, "wq", [128, 3, 768], BF16)
                wqr = T(st, "wqr", [128, 3, 8, 96], BF16)
                wpg = T(st, "wpg", [128, 4, 128], BF16)
                xb = T(st, "xbq", [128, 8, 512], BF16)
                sq = T(st, "sqq", [128, 8, 512], BF16)
                r1 = T(st, "r1q", [128, 512], F32)
                a1 = T(st, "a1q", [128, 512], F32)
                uT = T(st, "uT", [128, 4, OWN + 16], F32)
                st1 = ExitStack()
                cqraw = T(st1, "cqraw", [128, 3, 512], F32)
                sqcq = T(st1, "sqcq", [128, 3, 512], BF16)
                totq = T(st1, "totq", [128, 512], F32)
                cqn = T(st1, "cqn", [128, 3, 512], BF16)
                ctq = T(st1, "ctq", [128, 512], F32)
                snq = T(st1, "snq", [128, 512], F32)
                tq1 = [T(st1, "tq1%d" % i, [128, 512], F32) for i in range(2)]
                tq2 = [T(st1, "tq2%d" % i, [128, 512], F32) for i in range(2)]
                dW = S.dsem(final=True)
                dX = S.dsem()
                dTab = S.dsem()
                dTabS = S.dsem()
                dIc = S.dsem()

                load_folded(st, stg, lambda k: wcq[:, k, :], lambda k: w_inv[:, k, 0:384], 384, G_PRE)
                wcq.B.writers = [S.q["dve"][-1]]
                load_folded(st, stg, lambda k: wu[:, k, :], lambda k: w_inv[:, k, 672:1184], 512, G_PRE)
                wu.B.writers = [S.q["dve"][-1]]
                S.dma("pool", wq[:], w_uq.rearrange("(k p) n -> p k n", p=128), dW, writes=[wq.B])
                S.op("pool", lambda e: e.memset(wqr[:], 0.0), writes=[wqr.B])
                uqv = w_uq.rearrange("(k p) (h d) -> p k h d", p=128, d=96)
                for c in range(3):
                    S.dma("pool", wqr[:, c, :, 64:80], uqv[:, c, :, 80:96], dW, writes=[wqr.B])
                    S.dma("pool", wqr[:, c, :, 80:96], uqv[:, c, :, 64:80], dW, writes=[wqr.B])
                S.dma("pool", wpg[:], w_pool_group.rearrange("g c d -> c g d"), dW, writes=[wpg.B])

                rot = [0]

                def rb():
                    b = 1 + rot[0] % 7
                    rot[0] += 1
                    return b

                def u_part(n, dst_fns):
                    for g in range(4):
                        b = rb()
                        pe([(bk(b)[:, 0:n], wu[:, k, 128 * g:128 * g + 128], xb[:, k, 0:n], k == 0, k == 7) for k in range(8)],
                           [wu.B, xb.B], [PB[b]])
                        for (lo, hi, dfn) in dst_fns:
                            tt(dfn(g), bk(b)[:, lo:hi], r1[:, lo:hi], ALU.mult, [PB[b], r1.B], [uT.B])

                def dbg_markq(nm):
                    if cut == nm:
                        S.barrier()
                        S.mark(nm)
                dbg_markq("Qw")
                prep_tile(xb, sq, r1, a1, xhv, 16, dX, need_a=False)
                u_part(16, [(0, 8, lambda g: uT[:, g, 0:8]), (8, 16, lambda g: uT[:, g, OWN + 8:OWN + 16])])

                dbg_markq("Qh")
                for i in range(4):
                    if i == 1:
                        dbg_markq("Q0")
                    c0 = 512 * i
                    prep_tile(xb, sq, r1, a1, xTv[:, :, c0:c0 + 512], 512, dX)
                    S.dma("sp", ctq[64:96, :], cosS[:, c0:c0 + 512], dTab, reads=[BtabS], writes=[ctq.B])
                    S.dma("sp", snq[64:96, :], sinS[:, c0:c0 + 512], dTabS, reads=[BtabS], writes=[snq.B])
                    u_part(512, [(0, 512, lambda g, c0=c0: uT[:, g, 8 + c0:8 + c0 + 512])])
                    for c in range(3):
                        b = rb()
                        pe([(bk(b), wcq[:, k, 128 * c:128 * c + 128], xb[:, k, :], k == 0, k == 7) for k in range(8)],
                           [wcq.B, xb.B], [PB[b]])
                        act(sqcq[:, c, :], bk(b), AF.Square, [PB[b]], [sqcq.B])
                        dve(lambda e, b=b, c=c: e.tensor_copy(cqraw[:, c, :], bk(b)), [PB[b], sqcq.B], [cqraw.B])
                    b = rb()
                    pe([(bk(b), ones[:], sqcq[:, c, :], c == 0, c == 2) for c in range(3)], [ones.B, sqcq.B], [PB[b]])
                    tot_ops(totq[:], totq.B, bk(b), PB[b], a1[:], a1.B, 384)
                    for c in range(3):
                        stt(cqn[:, c, :], cqraw[:, c, :], gcol(G_Q, c), totq[:], ALU.mult, ALU.mult,
                            [cqraw.B, gT.B, totq.B], [cqn.B])
                    for h in range(8):
                        b1 = rb()
                        pe([(bk(b1)[0:96, :], wq[:, c, 96 * h:96 * h + 96], cqn[:, c, :], c == 0, c == 2) for c in range(3)],
                           [wq.B, cqn.B], [PB[b1]])
                        b2 = rb()
                        pe([(bk(b2)[0:96, :], wqr[:, c, h, :], cqn[:, c, :], c == 0, c == 2) for c in range(3)],
                           [wqr.B, cqn.B], [PB[b2]])
                        x1_, x2_ = tq1[h % 2], tq2[h % 2]
                        act(x1_[0:96, :], bk(b1)[0:96, :], AF.Copy, [PB[b1]], [x1_.B])
                        act(x2_[0:96, :], bk(b2)[0:96, :], AF.Copy, [PB[b2]], [x2_.B])
                        S.op("pool", lambda e, x1_=x1_, h=h, c0=c0: e.tensor_copy(QT[0:64, h, c0:c0 + 512], x1_[0:64, :]), [x1_.B], [QT.B])
                        tt(x1_[64:96, :], x1_[64:96, :], ctq[64:96, :], ALU.mult, [x1_.B, ctq.B], [x1_.B])
                        tt(x2_[64:96, :], x2_[64:96, :], snq[64:96, :], ALU.mult, [x2_.B, snq.B], [x2_.B])
                        tt(QT[64:96, h, c0:c0 + 512], x1_[64:96, :], x2_[64:96, :], ALU.add, [x1_.B, x2_.B], [QT.B])

                S.barrier()
                S.mark("Q1")
                st1.close()
                TA = T(st, "TA", [128, OWN + 16], F32)
                TB = T(st, "TB", [128, OWN + 16], F32)
                invc = T(st, "invc", [128, OWN], F32)
                pooled = T(st, "pooled", [128, 4, OWN], BF16)
                L = OWN + 16

                def sh(dst, src, lo, hi, d1, d2):
                    return lambda e: e.tensor_tensor(dst[:, lo:hi], src[:, lo + d1:hi + d1], src[:, lo + d2:hi + d2], ALU.add)

                for g in range(4):
                    ug = uT.t[:, g, :]
                    S.dma("sp", invc[:], bass.AP(invcnt.tensor, OWN * g, [[0, 128], [1, OWN]]), dIc, writes=[invc.B])
                    S.op("pool", sh(TA, ug, 1, L, -1, 0), [uT.B], [TA.B])
                    win = TA
                    if g >= 1:
                        S.op("pool", sh(TB, TA, 2, L - 1, -1, 1), [TA.B], [TB.B])
                        win = TB
                    if g >= 2:
                        S.op("pool", sh(TA, TB, 4, L - 3, -2, 2), [TB.B], [TA.B])
                        win = TA
                    if g >= 3:
                        S.op("pool", sh(TB, TA, 8, L - 7, -4, 4), [TA.B], [TB.B])
                        win = TB
                    other = TB if win is TA else TA
                    tt(other[:, 8:8 + OWN], win[:, 8:8 + OWN], invc[:], ALU.mult, [win.B, invc.B], [other.B])
                    tt(pooled[:, g, :], other[:, 8:8 + OWN], ug[:, 8:8 + OWN], ALU.subtract, [other.B, uT.B], [pooled.B])
                    for nt in range(4):
                        b = rb()
                        pe([(bk(b), wpg[:, g, :], pooled[:, g, 512 * nt:512 * nt + 512], True, True)], [wpg.B, pooled.B], [PB[b]])
                        act(PM[:, g, 512 * nt:512 * nt + 512], bk(b), AF.Identity, [PB[b], gT.B], [PM.B], scale=gcol(G_PS, g))
                if debug:
                    dD = S.dsem(final=True)
                    S.dma("sp", qtD[0:96], QT[0:96], dD, reads=[QT.B])
                    S.dma("sp", pmD, PM[:], dD, reads=[PM.B])
                S.barrier()
                S.mark("Q")

            with ExitStack() as st:
                NSL = 4
                kc = [T(st, "kc%d" % i, [128, 2048], BF16) for i in range(NSL)]
                vc = [T(st, "vc%d" % i, [128, 16, 65], BF16) for i in range(NSL)]
                Pb = [T(st, "P%d" % i, [128, 1024], BF16) for i in range(3)]
                rs = T(st, "rs", [128, OWN], F32)
                rc = [T(st, "rc%d" % i, [64, 512], F32) for i in range(2)]
                dSl = [S.dsem() for _ in range(NSL)]
                dSlV = [S.dsem() for _ in range(NSL)]
                scale = 96 ** -0.5
                chunks = [(h, c) for h in range(8) for c in range(8)]
                if nchunks:
                    chunks = chunks[:nchunks]

                def load_chunk(n):
                    h, c = chunks[n]
                    s = n % NSL
                    S.dma("sp", kc[s][0:64, :], kS[64 * h:64 * h + 64, 2048 * c:2048 * (c + 1)], dSl[s], reads=[BscrK], writes=[kc[s].B])
                    S.dma("sp", kc[s][64:96, :], krS[:, 2048 * c:2048 * (c + 1)], dSl[s], reads=[BscrR], writes=[kc[s].B])
                    S.dma("sp", vc[s][:], vS[h, :, 16 * c:16 * c + 16, :], dSlV[s], reads=[BscrV], writes=[vc[s].B])

                for n in range(3):
                    load_chunk(n)
                step = [0]
                pend = [None]

                def emit_pv(pv):
                    (s, kt, j, pb, first, last) = pv
                    pe([(ps[0:65, 512 * (2 * j + q):512 * (2 * j + q + 1)], vc[s][:, kt, :], Pb[pb][:, 512 * q:512 * q + 512], first, last)
                        for q in range(2)],
                       [vc[s].B, Pb[pb].B], [PB[2 * j], PB[2 * j + 1]])

                for n, (h, c) in enumerate(chunks):
                    if n + 3 < len(chunks):
                        load_chunk(n + 3)
                    s = n % NSL
                    for kt in range(16):
                        for j in range(2):
                            sb_ = step[0] % 2
                            pb = step[0] % 3
                            step[0] += 1
                            b0 = 4 + 2 * sb_
                            pe([(bk(b0 + q), kc[s][0:96, 128 * kt:128 * kt + 128], QT[0:96, h, 1024 * j + 512 * q:1024 * j + 512 * q + 512], True, True)
                                for q in range(2)],
                               [kc[s].B, QT.B], [PB[b0], PB[b0 + 1]])
                            act(Pb[pb][:], ps[:, 512 * b0:512 * b0 + 1024], AF.Exp, [PB[b0], PB[b0 + 1]], [Pb[pb].B], scale=scale)
                            if pend[0] is not None:
                                emit_pv(pend[0])
                            pend[0] = (s, kt, j, pb, (c == 0 and kt == 0), (c == 7 and kt == 15))
                    if c == 7:
                        emit_pv(pend[0])
                        pend[0] = None
                        for qb in range(4):
                            act(rs[64:65, 512 * qb:512 * qb + 512], bk(qb)[64:65, :], AF.Copy, [PB[qb]], [rs.B])
                            pe([(bk(4 + qb)[0:64, :], onesf[64:65, 0:64], rs[64:65, 512 * qb:512 * qb + 512], True, True)],
                               [onesf.B, rs.B], [PB[4 + qb]])
                            r_ = rc[qb % 2]
                            dve(lambda e, r_=r_, qb=qb: e.reciprocal(r_[:], bk(4 + qb)[0:64, :]), [PB[4 + qb]], [r_.B])
                            tt(AT[0:64, h, 512 * qb:512 * qb + 512], bk(qb)[0:64, :], r_[:], ALU.mult, [PB[qb], r_.B], [AT.B])
                if debug:
                    dD2 = S.dsem(final=True)
                    S.dma("sp", atD[0:64], AT[0:64], dD2, reads=[AT.B])
                S.barrier()
                S.mark("B")

            qsc.close()
            with ExitStack() as st:
                stg = [T(st, "stgc%d" % i, [128, 1024], F32) for i in range(2)]
                wg = T(st, "wg", [128, 8, 2048], BF16)
                woa = T(st, "woa", [64, 8, 1024], BF16)
                wob = T(st, "wob", [128, 4, 1024], BF16)
                wout = T(st, "wout", [128, 8, 1024], BF16)
                xb = T(st, "xbc", [128, 8, 512], BF16)
                sq = T(st, "sqc", [128, 8, 512], BF16)
                xf = T(st, "xfc", [128, 8, 512], F32)
                r1 = T(st, "r1c", [128, 512], F32)
                r2 = T(st, "r2c", [128, 512], F32)
                zA = [T(st, "zA%d" % i, [128, 512], F32) for i in range(2)]
                zB = [T(st, "zB%d" % i, [128, 512], F32) for i in range(2)]
                m = T(st, "m", [128, 8, 512], BF16)
                y = T(st, "y", [128, 8, 512], F32)
                sqy = [T(st, "sqy%d" % i, [128, 512], BF16) for i in range(2)]
                dW = S.dsem(final=True)
                dX = S.dsem()
                dXf = S.dsem()
                dO = S.dsem()
                Bx1S = Buf("x1S")
                load_folded(st, stg, lambda k: wg[:, k, 0:1024], lambda k: w_inv[:, k, 1184:2208], 1024, G_PRE)
                load_folded(st, stg, lambda k: wg[:, k, 1024:2048], lambda k: w_inv[:, k, 2208:3232], 1024, G_PRE)
                wg.B.writers = [S.q["dve"][-1]]
                S.dma("pool", woa[:], w_o_attn.rearrange("(h d) n -> d h n", d=64), dW, writes=[woa.B])
                S.dma("pool", wob[:], w_o_pool.rearrange("(c p) n -> p c n", p=128), dW, writes=[wob.B])
                for k in range(8):
                    S.dma("pool", wout[:, k, :], w_out[128 * k:128 * k + 128, :], dW, writes=[wout.B])
                for i in range(4):
                    c0 = 512 * i
                    S.dma("sp", xf[:], xTv[:, :, c0:c0 + 512], dXf, writes=[xf.B])
                    prep_tile(xb, sq, r1, None, xTv[:, :, c0:c0 + 512], 512, dX, need_a=False)
                    for j in range(8):
                        bA, bB, ba, bb = (1, 2, 3, 4) if j % 2 == 0 else (5, 6, 7, 4)
                        z1, z2 = zA[j % 2], zB[j % 2]
                        pe([(bk(bA), wg[:, k, 128 * j:128 * j + 128], xb[:, k, :], k == 0, k == 7) for k in range(8)],
                           [wg.B, xb.B], [PB[bA]])
                        pe([(bk(bB), wg[:, k, 1024 + 128 * j:1024 + 128 * j + 128], xb[:, k, :], k == 0, k == 7) for k in range(8)],
                           [wg.B, xb.B], [PB[bB]])
                        pe([(bk(ba), woa[0:64, h, 128 * j:128 * j + 128], AT[0:64, h, c0:c0 + 512], h == 0, h == 7) for h in range(8)],
                           [woa.B, AT.B], [PB[ba]])
                        tt(z1[:], bk(bA), r1[:], ALU.mult, [PB[bA], r1.B], [z1.B])
                        act(z1[:], z1[:], AF.Sigmoid, [z1.B], [z1.B])
                        tt(z1[:], z1[:], bk(ba), ALU.mult, [z1.B, PB[ba]], [z1.B])
                        pe([(bk(bb), wob[:, c, 128 * j:128 * j + 128], PM[:, c, c0:c0 + 512], c == 0, c == 3) for c in range(4)],
                           [wob.B, PM.B], [PB[bb]])
                        tt(z2[:], bk(bB), r1[:], ALU.mult, [PB[bB], r1.B], [z2.B])
                        act(z2[:], z2[:], AF.Sigmoid, [z2.B], [z2.B])
                        tt(z2[:], z2[:], bk(bb), ALU.mult, [z2.B, PB[bb]], [z2.B])
                        tt(m[:, j, :], z1[:], z2[:], ALU.add, [z1.B, z2.B], [m.B])
                    for j in range(8):
                        b = 1 + j % 6
                        pe([(bk(b), wout[:, k, 128 * j:128 * j + 128], m[:, k, :], k == 0, k == 7) for k in range(8)],
                           [wout.B, m.B], [PB[b]])
                        act(y[:, j, :], bk(b), AF.Copy, [PB[b]], [y.B])
                        sy = sqy[j % 2]
                        act(sy[:], bk(b), AF.Square, [PB[b]], [sy.B])
                        pe([(bk(7), ones[:], sy[:], j == 0, j == 7)], [ones.B, sy.B], [PB[7]])
                    rstd_ops(r2[:], r2.B, bk(7), PB[7], 1.0 / 1024)
                    for j in range(8):
                        stt(y[:, j, :], y[:, j, :], gcol(G_POST, j), r2[:], ALU.mult, ALU.mult, [y.B, gT.B, r2.B], [y.B])
                        tt(y[:, j, :], y[:, j, :], xf[:, j, :], ALU.add, [y.B, xf.B], [y.B])
                    S.dma("sp", x1Sv[:, :, c0:c0 + 512], y[:], dO, reads=[y.B], writes=[Bx1S])
                S.barrier()
                S.mark("C1")

        with ExitStack() as st:
            x1h = T(st, "x1h", [128, 8, 1024], F32)
            hfb = T(st, "hfb", [128, 8, 1024], BF16)
            actT = T(st, "actT", [128, 22, 1024], BF16)
            y2 = T(st, "y2", [128, 8, 1024], F32)
            wgu = [T(st, "wgu%d" % i, [128, 8, 256], BF16) for i in range(3)]
            wd = [T(st, "wd%d" % i, [128, 22, 128], BF16) for i in range(2)]
            sqc = [T(st, "sqf%d" % i, [128, 512], BF16) for i in range(2)]
            sg = [T(st, "sg%d" % i, [128, 512], F32) for i in range(2)]
            r3 = T(st, "r3", [128, 1024], F32)
            r4 = T(st, "r4", [128, 1024], F32)
            dX1 = S.dsem()
            dGU = [S.dsem() for _ in range(3)]
            dWD = [S.dsem() for _ in range(2)]
            dOut = S.dsem()
            wguv = w_gate_up.rearrange("(k p) n -> p k n", p=128)
            wdv = w_down.rearrange("(k p) n -> p k n", p=128)

            def load_gu(hf, j):
                s = (hf * 22 + j) % 3
                S.dma("pool", wgu[s][:, :, 0:128], wguv[:, :, 128 * j:128 * j + 128], dGU[s], writes=[wgu[s].B])
                S.dma("pool", wgu[s][:, :, 128:256], wguv[:, :, 2816 + 128 * j:2816 + 128 * j + 128], dGU[s], writes=[wgu[s].B])

            def load_wd(hf, i):
                s = (hf * 8 + i) % 2
                S.dma("pool", wd[s][:], wdv[:, :, 128 * i:128 * i + 128], dWD[s], writes=[wd[s].B])

            for hf in range(2):
                h0 = 1024 * hf
                S.dma("sp", x1h[:], x1Sv[:, :, h0:h0 + 1024], dX1, reads=[Bx1S], writes=[x1h.B])
                load_gu(hf, 0)
                load_gu(hf, 1)
                for nt in range(2):
                    n0 = 512 * nt
                    for k in range(8):
                        sc_ = sqc[k % 2]
                        act(sc_[:], x1h[:, k, n0:n0 + 512], AF.Square, [x1h.B], [sc_.B])
                        pe([(bk(6 + nt), ones[:], sc_[:], k == 0, k == 7)], [ones.B, sc_.B], [PB[6 + nt]])
                    rstd_ops(r3[:, n0:n0 + 512], r3.B, bk(6 + nt), PB[6 + nt], 1.0 / 1024)
                    for k in range(8):
                        stt(hfb[:, k, n0:n0 + 512], x1h[:, k, n0:n0 + 512], gcol(G_FPRE, k), r3[:, n0:n0 + 512],
                            ALU.mult, ALU.mult, [x1h.B, gT.B, r3.B], [hfb.B])
                cnt = 0
                for j in range(22):
                    if j + 2 < 22:
                        load_gu(hf, j + 2)
                    if j == 20:
                        load_wd(hf, 0)
                    if j == 21:
                        load_wd(hf, 1)
                    s = (hf * 22 + j) % 3
                    for nt in range(2):
                        n0 = 512 * nt
                        bg, bu = (0, 1) if cnt % 2 == 0 else (2, 3)
                        sg_ = sg[cnt % 2]
                        cnt += 1
                        pe([(bk(bg), wgu[s][:, k, 0:128], hfb[:, k, n0:n0 + 512], k == 0, k == 7) for k in range(8)],
                           [wgu[s].B, hfb.B], [PB[bg]])
                        pe([(bk(bu), wgu[s][:, k, 128:256], hfb[:, k, n0:n0 + 512], k == 0, k == 7) for k in range(8)],
                           [wgu[s].B, hfb.B], [PB[bu]])
                        act(sg_[:], bk(bg), AF.Silu, [PB[bg]], [sg_.B])
                        tt(actT[:, j, n0:n0 + 512], sg_[:], bk(bu), ALU.mult, [sg_.B, PB[bu]], [actT.B])
                cnt = 0
                for i in range(8):
                    s = (hf * 8 + i) % 2
                    for nt in range(2):
                        n0 = 512 * nt
                        b = cnt % 4
                        sc_ = sqc[cnt % 2]
                        cnt += 1
                        pe([(bk(b), wd[s][:, k, :], actT[:, k, n0:n0 + 512], k == 0, k == 21) for k in range(22)],
                           [wd[s].B, actT.B], [PB[b]])
                        act(y2[:, i, n0:n0 + 512], bk(b), AF.Copy, [PB[b]], [y2.B])
                        act(sc_[:], bk(b), AF.Square, [PB[b]], [sc_.B])
                        pe([(bk(4 + nt), ones[:], sc_[:], i == 0, i == 7)], [ones.B, sc_.B], [PB[4 + nt]])
                    if i + 2 < 8:
                        load_wd(hf, i + 2)
                for nt in range(2):
                    n0 = 512 * nt
                    rstd_ops(r4[:, n0:n0 + 512], r4.B, bk(4 + nt), PB[4 + nt], 1.0 / 1024)
                    for i in range(8):
                        stt(y2[:, i, n0:n0 + 512], y2[:, i, n0:n0 + 512], gcol(G_FPOST, i), r4[:, n0:n0 + 512],
                            ALU.mult, ALU.mult, [y2.B, gT.B, r4.B], [y2.B])
                        tt(y2[:, i, n0:n0 + 512], y2[:, i, n0:n0 + 512], x1h[:, i, n0:n0 + 512], ALU.add, [y2.B, x1h.B], [y2.B])
                S.dma("sp", outTv[:, :, h0:h0 + 1024], y2[:], dOut, reads=[y2.B])
            S.barrier()
            S.mark("C2")

        if cut:
            S.cut(cut)
        S.emit(None)
        S.check()
        with nc.Block() as block:
            @block.tensor
            def _(e):
                S.play("pe", e)

            @block.scalar
            def _(e):
                S.play("act", e)

            @block.vector
            def _(e):
                S.play("dve", e)

            @block.gpsimd
            def _(e):
                S.play("pool", e)

            @block.sync
            def _(e):
                S.play("sp", e)
    return nc


def make_inputs(inputs, core):
    x = np.asarray(inputs["x"], np.float32)[0]
    pos = np.asarray(inputs["positions"], np.int32)[0]
    o0 = core * OWN
    xr = np.roll(x, -o0, axis=0)
    xTc = np.ascontiguousarray(xr.T)
    posr = np.ascontiguousarray(np.roll(pos, -o0)[None, :])
    xh = np.zeros((16, 1024), np.float32)
    if o0 - 8 >= 0:
        xh[0:8] = x[o0 - 8:o0]
    if o0 + OWN + 8 <= S_TOK:
        xh[8:16] = x[o0 + OWN:o0 + OWN + 8]
    xhT = np.ascontiguousarray(xh.T)
    return xTc, xhT, posr


_CONSTS = {}


def const_tables(core):
    if core in _CONSTS:
        return _CONSTS[core]
    inv = (1.0 / (10000.0 ** (np.arange(0, 32, 2, dtype=np.float32) / np.float32(32)))).astype(np.float32)
    cst = np.zeros((128, 4), np.float32)
    for p in range(128):
        cst[p, 0] = inv[p % 16]
        cst[p, 1] = -1.0 if (p % 32) < 16 else 1.0
        cst[p, 2] = 0.0
        cst[p, 3] = EPS
    t = np.arange(core * OWN, (core + 1) * OWN)
    invcnt = np.zeros((4, OWN), np.float32)
    for g, w in enumerate(POOL_W):
        left = w // 2
        right = w - left - 1
        cnt = (np.minimum(t + right, S_TOK - 1) - np.maximum(t - left, 0) + 1).astype(np.float32)
        invcnt[g] = (1.0 / cnt).astype(np.float32)
    _CONSTS[core] = (cst, invcnt)
    return _CONSTS[core]


def chunk_cols(v):
    v = np.asarray(v, np.float32)
    return np.ascontiguousarray(v.reshape(-1, 128).T)


_NC = {}


def kernel(x, positions, g_mix_pre, w_in, g_q_lat, w_uq, g_kv_lat, w_ukv, w_o_attn,
           w_pool_group, pool_scale, w_o_pool, w_out, g_mix_post, g_ffn_pre,
           w_gate_up, w_down, g_ffn_post, _debug=False):
    inputs = {"x": x, "positions": positions}
    gains = np.concatenate([chunk_cols(g_mix_pre), chunk_cols(g_q_lat), chunk_cols(g_kv_lat),
                            chunk_cols(pool_scale), chunk_cols(g_mix_post), chunk_cols(g_ffn_pre),
                            chunk_cols(g_ffn_post)], axis=1)
    gains = np.ascontiguousarray(gains, dtype=np.float32)
    assert gains.shape == (128, 41)
    f = lambda a: np.ascontiguousarray(np.asarray(a, np.float32))
    common = dict(gains=gains, w_in=f(w_in), w_uq=f(w_uq), w_ukv=f(w_ukv), w_o_attn=f(w_o_attn),
                  w_pool_group=f(w_pool_group), w_o_pool=f(w_o_pool), w_out=f(w_out),
                  w_gate_up=f(w_gate_up), w_down=f(w_down))
    in_maps = []
    for c in range(NCORES):
        xTc, xhT, posr = make_inputs(inputs, c)
        cst, invcnt = const_tables(c)
        m = dict(common)
        m.update(xT=xTc, xh=xhT, pos=posr, cst=cst, invcnt=invcnt)
        in_maps.append(m)
    key = bool(_debug)
    if key not in _NC:
        _NC[key] = build_program(debug=key)
    nc = _NC[key]
    res = run_bass_kernel_spmd(nc, in_maps, core_ids=list(range(NCORES)))
    if _debug:
        return res
    outT = np.concatenate([np.asarray(r["outT"]) for r in res.results], axis=1)
    return np.ascontiguousarray(outT.T)[None, :, :].astype(np.float32)
```

```python
import math
from contextlib import ExitStack

import numpy as np
import concourse.bass as bass
import concourse.mybir as mybir
from concourse.bass_utils import run_bass_kernel_spmd

ENGS = ("pe", "act", "dve", "pool", "sp")


class Buf:
    __slots__ = ("name", "writers", "readers")

    def __init__(self, name):
        self.name = name
        self.writers = []
        self.readers = []


class DSem:
    __slots__ = ("sem", "count", "final")

    def __init__(self, sem, final=False):
        self.sem = sem
        self.count = 0
        self.final = final


class Op:
    __slots__ = ("eng", "fn", "deps", "signal", "sigval", "dsem", "dval", "idx")

    def __init__(self, eng, fn):
        self.eng = eng
        self.fn = fn
        self.deps = []
        self.signal = False
        self.sigval = None
        self.dsem = None
        self.dval = None


class Sched:
    def __init__(self, nc, stack):
        self.nc = nc
        self.stack = stack
        self.q = {e: [] for e in ENGS}
        self.esem = {e: stack.enter_context(nc.semaphore("es_" + e)) for e in ENGS}
        self.dmas = []
        self.marks = {}
        self.nds = 0

    def mark(self, name):
        self.marks[name] = {e: len(self.q[e]) for e in ENGS}

    def cut(self, name):
        for e in ENGS:
            del self.q[e][self.marks[name][e]:]

    def dsem(self, final=False):
        self.nds += 1
        return DSem(self.stack.enter_context(self.nc.semaphore("ds%d" % self.nds)), final)

    def _track(self, op, reads, writes, group_dma_writes=False):
        deps = set()
        for b in reads:
            for w in b.writers:
                deps.add((w, True))
        for b in writes:
            for w in b.writers:
                if group_dma_writes and w.dsem is not None and not b.readers and w.dsem is op.dsem:
                    continue
                deps.add((w, False))
            for r in b.readers:
                deps.add((r, False))
        for d, raw in deps:
            if d is op:
                continue
            if d.dsem is None and op.dsem is None and d.eng == op.eng:
                if not (raw and op.eng in ("act", "dve", "pool")):
                    continue
            op.deps.append(d)
            if d.dsem is None:
                d.signal = True
        for b in reads:
            b.readers.append(op)
        for b in writes:
            if group_dma_writes and b.writers and all(
                    w.dsem is not None and w.dsem is op.dsem for w in b.writers) and not b.readers:
                b.writers.append(op)
            else:
                b.writers = [op]
                b.readers = []

    def op(self, eng, fn, reads=(), writes=()):
        o = Op(eng, fn)
        self._track(o, reads, writes)
        self.q[eng].append(o)
        return o

    def dma(self, eng, out, in_, dsem, reads=(), writes=()):
        def fn(e, out=out, in_=in_):
            return e.dma_start(out=out, in_=in_)
        o = Op(eng, fn)
        o.dsem = dsem
        dsem.count += 16
        o.dval = dsem.count
        self._track(o, reads, writes, group_dma_writes=True)
        self.q[eng].append(o)
        self.dmas.append(o)
        return o

    def barrier(self):
        lasts = []
        for e in ENGS:
            for o in reversed(self.q[e]):
                if o.dsem is None and o.fn is not None:
                    o.signal = True
                    lasts.append(o)
                    break
        dm = {}
        for o in self.dmas:
            dm[id(o.dsem)] = o
        self.dmas = []
        for e in ENGS:
            def fn(eng):
                return None
            b = Op(e, None)
            for l in lasts:
                if l.eng != e:
                    b.deps.append(l)
            b.deps.extend(dm.values())
            self.q[e].append(b)

    def emit(self, engines):
        for e in ENGS:
            c = 0
            for o in self.q[e]:
                if o.dsem is None and o.signal and o.fn is not None:
                    c += 1
                    o.sigval = c

    def _waits(self, o):
        waits = {}
        for d in o.deps:
            if d.dsem is not None:
                key = id(d.dsem)
                val = d.dsem.count if d.dsem.final else d.dval
                sem = d.dsem.sem
            else:
                key = d.eng
                val = d.sigval
                sem = self.esem[d.eng]
            assert val is not None, (o.eng, d.eng, d.fn)
            if key not in waits or waits[key][1] < val:
                waits[key] = (sem, val)
        return waits

    def check(self):
        cnt = {}
        ptr = {e: 0 for e in ENGS}
        total = sum(len(self.q[e]) for e in ENGS)
        done = 0
        while done < total:
            prog = False
            for e in ENGS:
                while ptr[e] < len(self.q[e]):
                    o = self.q[e][ptr[e]]
                    ok = all(cnt.get(k, 0) >= v for k, (s_, v) in self._waits(o).items())
                    if not ok:
                        break
                    if o.fn is not None:
                        if o.dsem is not None:
                            cnt[id(o.dsem)] = cnt.get(id(o.dsem), 0) + 16
                        elif o.signal:
                            cnt[e] = cnt.get(e, 0) + 1
                    ptr[e] += 1
                    done += 1
                    prog = True
            if not prog:
                st = {e: (ptr[e], len(self.q[e])) for e in ENGS}
                raise RuntimeError("schedule deadlock: %s" % st)
        return True

    def play(self, e, eng):
        seen = {}
        last_real = None
        for o in self.q[e]:
            waits = {}
            for d in o.deps:
                if d.dsem is not None:
                    key = id(d.dsem)
                    val = d.dsem.count if d.dsem.final else d.dval
                    sem = d.dsem.sem
                else:
                    key = d.eng
                    val = d.sigval
                    sem = self.esem[d.eng]
                if key not in waits or waits[key][1] < val:
                    waits[key] = (sem, val)
            for key, (sem, val) in waits.items():
                if seen.get(key, 0) >= val:
                    continue
                seen[key] = val
                eng.wait_ge(sem, val)
            if o.fn is None:
                continue
            ins = o.fn(eng)
            if o.dsem is not None:
                ins.then_inc(o.dsem.sem, 16)
            elif o.signal:
                ins.then_inc(self.esem[e], 1)


F32 = mybir.dt.float32
BF16 = mybir.dt.bfloat16
I32 = mybir.dt.int32
AF = mybir.ActivationFunctionType
ALU = mybir.AluOpType

S_TOK = 16384
OWN = 2048
NCORES = 8
EPS = 1e-6
G_PRE, G_Q, G_KV, G_PS, G_POST, G_FPRE, G_FPOST = 0, 8, 11, 13, 17, 25, 33
POOL_W = (2, 4, 8, 16)


def build_program(debug=False, cut=None, nta=None, nchunks=None):
    nc = bass.Bass("TRN2", target_bir_lowering=False)

    def din(name, shape, dt=F32):
        return nc.dram_tensor(name, shape, dt, kind="ExternalInput").ap()

    xT = din("xT", [1024, S_TOK])
    xh = din("xh", [1024, 16])
    pos = din("pos", [1, S_TOK], I32)
    cst = din("cst", [128, 4])
    gains = din("gains", [128, 41])
    invcnt = din("invcnt", [4, OWN])
    w_in = din("w_in", [1024, 3232])
    w_uq = din("w_uq", [384, 768])
    w_ukv = din("w_ukv", [256, 1024])
    w_o_attn = din("w_o_attn", [512, 1024])
    w_pool_group = din("w_pool_group", [4, 128, 128])
    w_o_pool = din("w_o_pool", [512, 1024])
    w_out = din("w_out", [1024, 1024])
    w_gate_up = din("w_gate_up", [1024, 5632])
    w_down = din("w_down", [2816, 1024])
    outT = nc.dram_tensor("outT", [1024, OWN], F32, kind="ExternalOutput").ap()
    skind = dict(kind="ExternalOutput") if debug else {}
    kS = nc.dram_tensor("kS", [512, S_TOK], BF16, **skind).ap()
    krS = nc.dram_tensor("krS", [32, S_TOK], BF16, **skind).ap()
    vS = nc.dram_tensor("vS", [8, 128, 128, 65], BF16, **skind).ap()
    cosS = nc.dram_tensor("cosS", [32, S_TOK], F32, **skind).ap()
    sinS = nc.dram_tensor("sinS", [32, S_TOK], F32, **skind).ap()
    x1S = nc.dram_tensor("x1S", [1024, OWN], F32, **skind).ap()
    if debug:
        qtD = nc.dram_tensor("qtD", [128, 8, OWN], BF16, kind="ExternalOutput").ap()
        atD = nc.dram_tensor("atD", [128, 8, OWN], BF16, kind="ExternalOutput").ap()
        pmD = nc.dram_tensor("pmD", [128, 4, OWN], BF16, kind="ExternalOutput").ap()

    xTv = xT.rearrange("(k p) n -> p k n", p=128)
    xhv = xh.rearrange("(k p) n -> p k n", p=128)
    w_inv = w_in.rearrange("(k p) n -> p k n", p=128)
    outTv = outT.rearrange("(k p) n -> p k n", p=128)
    x1Sv = x1S.rearrange("(k p) n -> p k n", p=128)
    kSv = kS.rearrange("(j q) t -> q j t", q=128)
    vSv = vS.rearrange("h p t d -> p h t d")

    with ExitStack() as gst:
        S = Sched(nc, gst)
        ps = gst.enter_context(nc.psum_tensor("ps", [128, 4096], F32))
        PB = [Buf("pb%d" % b) for b in range(8)]

        def bk(b):
            return ps[:, 512 * b:512 * (b + 1)]

        class T:
            def __init__(self, st, name, shape, dt, nb=1, side=None):
                if side:
                    self.t = st.enter_context(nc.sbuf_tensor("sb_" + name, shape, dt, side=side))
                else:
                    self.t = st.enter_context(nc.sbuf_tensor("sb_" + name, shape, dt))
                self.b = [Buf(name + str(i)) for i in range(nb)]

            def __getitem__(self, k):
                return self.t[k]

            @property
            def B(self):
                return self.b[0]

        def pe(mms, reads, writes):
            def fn(e, mms=mms):
                ins = None
                for (o, l, r, st_, sp_) in mms:
                    ins = e.matmul(o, l, r, start=st_, stop=sp_)
                return ins
            return S.op("pe", fn, reads, writes)

        def act(out, in_, func, reads, writes, scale=1.0, bias=0.0):
            return S.op("act", lambda e: e.activation(out, in_, func, bias=bias, scale=scale), reads, writes)

        def dve(fn, reads, writes):
            return S.op("dve", fn, reads, writes)

        def tt(out, a, b, op, reads, writes, eng="dve"):
            return S.op(eng, lambda e: e.tensor_tensor(out, a, b, op), reads, writes)

        def stt(out, in0, sc, in1, op0, op1, reads, writes):
            return S.op("dve", lambda e: e.scalar_tensor_tensor(out, in0, sc, in1, op0, op1), reads, writes)

        def ts(out, in0, s1, s2, op0, op1, reads, writes):
            if s2 is None:
                return S.op("dve", lambda e: e.tensor_scalar(out, in0, s1, None, op0), reads, writes)
            return S.op("dve", lambda e: e.tensor_scalar(out, in0, s1, s2, op0, op1), reads, writes)

        cstT = T(gst, "cst", [128, 4], F32)
        gT = T(gst, "gains", [128, 41], F32)
        ones = T(gst, "ones", [128, 128], BF16)
        onesf = T(gst, "onesf", [128, 128], F32)
        dC = S.dsem(final=True)
        S.dma("sp", cstT[:], cst, dC, writes=[cstT.B])
        S.dma("sp", gT[:], gains, dC, writes=[gT.B])
        S.op("pool", lambda e: e.memset(ones[:], 1.0), writes=[ones.B])
        S.op("pool", lambda e: e.memset(onesf[:], 1.0), writes=[onesf.B])
        epsb = cstT[:, 3:4]

        def gcol(off, k, p0=0, p1=128):
            return gT[p0:p1, off + k:off + k + 1]

        def rstd_ops(out_ap, out_B, ss_ap, ss_B, sc):
            act(out_ap, ss_ap, AF.Sqrt, [ss_B, cstT.B], [out_B], scale=sc, bias=epsb)
            dve(lambda e: e.reciprocal(out_ap, out_ap), [out_B], [out_B])

        def tot_ops(out_ap, out_B, ss_ap, ss_B, a_ap, a_B, nf):
            stt(out_ap, ss_ap, 1.0 / nf, a_ap, ALU.mult, ALU.add, [ss_B, a_B], [out_B])
            act(out_ap, out_ap, AF.Sqrt, [out_B], [out_B])
            dve(lambda e: e.reciprocal(out_ap, out_ap), [out_B], [out_B])

        dStage = [S.dsem(), S.dsem()]
        stage_ctr = [0]

        def load_folded(st, stg, dst_fn, src_fn, ncols, goff, q="sp", dsems=None):
            dsems = dsems or dStage
            for k in range(8):
                i = stage_ctr[0] % 2
                stage_ctr[0] += 1
                S.dma(q, stg[i][:, 0:ncols], src_fn(k), dsems[i], writes=[stg[i].B])
                ts(dst_fn(k), stg[i][:, 0:ncols], gcol(goff, k), None, ALU.mult, None,
                   [stg[i].B, gT.B], [])

        with ExitStack() as st:
            I = T(st, "ti", [128, 4096], I32)
            X = T(st, "tx", [128, 4096], F32)
            Y = T(st, "ty", [128, 4096], F32)
            Z = T(st, "tz", [128, 4096], F32)
            W = T(st, "tw", [128, 4096], F32)
            d0 = S.dsem(final=True)
            for s in range(4):
                S.dma("sp", I[32 * s:32 * s + 32, :],
                      bass.AP(pos.tensor, 4096 * s, [[0, 32], [1, 4096]]),
                      d0, writes=[I.B])
            C1 = 6.28125
            C2 = 2 * math.pi - C1
            dve(lambda e: e.tensor_copy(X[:], I[:]), [I.B], [X.B])
            ts(X[:], X[:], cstT[:, 0:1], None, ALU.mult, None, [X.B, cstT.B], [X.B])
            ts(I[:], X[:], 1.0 / (2 * math.pi), None, ALU.mult, None, [X.B], [I.B])
            dve(lambda e: e.tensor_copy(Y[:], I[:]), [I.B], [Y.B])
            stt(Z[:], Y[:], -C1, X[:], ALU.mult, ALU.add, [Y.B, X.B], [Z.B])
            stt(Z[:], Y[:], -C2, Z[:], ALU.mult, ALU.add, [Y.B, Z.B], [Z.B])
            ts(Y[:], Z[:], math.pi, -2 * math.pi, ALU.is_gt, ALU.mult, [Z.B], [Y.B])
            tt(Z[:], Z[:], Y[:], ALU.add, [Z.B, Y.B], [Z.B])
            act(X[:], Z[:], AF.Sin, [Z.B, cstT.B], [X.B], scale=cstT[:, 1:2])
            dT = S.dsem(final=True)
            BtabS = Buf("tabS")
            for s in range(4):
                S.dma("sp", sinS[:, 4096 * s:4096 * (s + 1)], X[32 * s:32 * s + 32, :], dT, reads=[X.B], writes=[BtabS])
            ts(Y[:], Z[:], math.pi / 2, None, ALU.add, None, [Z.B], [Y.B])
            ts(W[:], Y[:], math.pi, -2 * math.pi, ALU.is_gt, ALU.mult, [Y.B], [W.B])
            tt(Y[:], Y[:], W[:], ALU.add, [Y.B, W.B], [Y.B])
            act(W[:], Y[:], AF.Sin, [Y.B], [W.B])
            for s in range(4):
                S.dma("sp", cosS[:, 4096 * s:4096 * (s + 1)], W[32 * s:32 * s + 32, :], dT, reads=[W.B], writes=[BtabS])
            S.barrier()
            S.mark("p0")

        def prep_tile(xb, sq, r1, a1, src_ap, n, dX, need_a=True):
            S.dma("pool", xb[:, :, 0:n], src_ap, dX, writes=[xb.B])
            act(sq[:, :, 0:n], xb[:, :, 0:n], AF.Square, [xb.B], [sq.B])
            pe([(bk(0)[:, 0:n], ones[:], sq[:, k, 0:n], k == 0, k == 7) for k in range(8)],
               [sq.B, ones.B], [PB[0]])
            rstd_ops(r1[:, 0:n], r1.B, bk(0)[:, 0:n], PB[0], 1.0 / 1024)
            if need_a:
                ts(a1[:, 0:n], bk(0)[:, 0:n], EPS / 1024, EPS * EPS, ALU.mult, ALU.add, [PB[0], r1.B], [a1.B])

        with ExitStack() as bq:
            AT = T(bq, "AT", [128, 8, OWN], BF16)
            PM = T(bq, "PM", [128, 4, OWN], BF16)
            qsc = ExitStack()
            QT = T(qsc, "QT", [128, 8, OWN], BF16)

            qw = ExitStack()
            stgq2 = [T(qw, "stgq2%d" % i, [128, 512], F32, side="right") for i in range(2)]
            wcq = T(qw, "wcq", [128, 8, 384], BF16, side="right")
            wu = T(qw, "wu", [128, 8, 512], BF16, side="right")
            wq = T(qw, "wq", [128, 3, 768], BF16, side="right")
            wqr = T(qw, "wqr", [128, 3, 8, 96], BF16, side="right")
            wpg = T(qw, "wpg", [128, 4, 128], BF16, side="right")
            dWq = S.dsem(final=True)
            dStq = [S.dsem(), S.dsem()]

            def load_q_weights():
                st = None
                stg = stgq2
                dW = dWq
                load_folded(st, stg, lambda k: wcq[:, k, :], lambda k: w_inv[:, k, 0:384], 384, G_PRE, dsems=dStq)
                wcq.B.writers = [S.q["dve"][-1]]
                load_folded(st, stg, lambda k: wu[:, k, :], lambda k: w_inv[:, k, 672:1184], 512, G_PRE, dsems=dStq)
                wu.B.writers = [S.q["dve"][-1]]
                S.dma("pool", wq[:], w_uq.rearrange("(k p) n -> p k n", p=128), dW, writes=[wq.B])
                S.op("pool", lambda e: e.memset(wqr[:], 0.0), writes=[wqr.B])
                uqv = w_uq.rearrange("(k p) (h d) -> p k h d", p=128, d=96)
                for c in range(3):
                    S.dma("pool", wqr[:, c, :, 64:80], uqv[:, c, :, 80:96], dW, writes=[wqr.B])
                    S.dma("pool", wqr[:, c, :, 80:96], uqv[:, c, :, 64:80], dW, writes=[wqr.B])
                S.dma("pool", wpg[:], w_pool_group.rearrange("g c d -> c g d"), dW, writes=[wpg.B])


            with ExitStack() as st:
                stg = [T(st, "stg%d" % i, [128, 512], F32) for i in range(2)]
                wkv = T(st, "wkv", [128, 8, 320], BF16)
                wk = T(st, "wk", [128, 2, 512], BF16)
                wv = T(st, "wv", [128, 2, 512], BF16)
                xb = [T(st, "xb%d" % i, [128, 8, 512], BF16) for i in range(2)]
                sq = [T(st, "sq%d" % i, [128, 8, 512], BF16) for i in range(2)]
                r1 = [T(st, "r1%d" % i, [128, 512], F32) for i in range(2)]
                a1 = [T(st, "a1%d" % i, [128, 512], F32) for i in range(2)]
                sqr = [T(st, "sqr%d" % i, [128, 2, 512], BF16) for i in range(2)]
                tot = T(st, "tot", [128, 512], F32)
                ckvn = [T(st, "ckvn%d" % i, [128, 2, 512], BF16) for i in range(2)]
                kst = [T(st, "kst%d" % i, [128, 4, 512], BF16) for i in range(2)]
                vst = [T(st, "vst%d" % i, [128, 8, 4, 65], BF16) for i in range(2)]
                ct = [T(st, "ct%d" % i, [32, 512], F32) for i in range(2)]
                sn = [T(st, "sn%d" % i, [32, 512], F32) for i in range(2)]
                krs = [T(st, "krs%d" % i, [32, 512], BF16) for i in range(2)]
                t1 = T(st, "t1", [32, 512], F32)
                t2 = T(st, "t2", [32, 512], F32)
                dW = S.dsem(final=True)
                dX = [S.dsem(), S.dsem()]
                dTab = [S.dsem(), S.dsem()]
                dTabS = [S.dsem(), S.dsem()]
                dKs = [S.dsem(), S.dsem()]
                dVs = [S.dsem(), S.dsem()]
                dKr = [S.dsem(), S.dsem()]
                BscrK, BscrV, BscrR = Buf("scrK"), Buf("scrV"), Buf("scrR")

                load_folded(st, stg, lambda k: wkv[:, k, 0:288], lambda k: w_inv[:, k, 384:672], 288, G_PRE)
                load_folded(st, stg, lambda k: wkv[:, k, 288:304], lambda k: w_inv[:, k, 656:672], 16, G_PRE)
                load_folded(st, stg, lambda k: wkv[:, k, 304:320], lambda k: w_inv[:, k, 640:656], 16, G_PRE)
                wkv.B.writers = [S.q["dve"][-1]]
                ukv = w_ukv.rearrange("(k p) (h two d) -> p k h two d", p=128, two=2, d=64)
                for kc in range(2):
                    S.dma("pool", wk[:, kc, :].rearrange("p (h d) -> p h d", d=64), ukv[:, kc, :, 0, :], dW, writes=[wk.B])
                    S.dma("pool", wv[:, kc, :].rearrange("p (h d) -> p h d", d=64), ukv[:, kc, :, 1, :], dW, writes=[wv.B])
                for i in range(2):
                    S.op("pool", lambda e, i=i: e.memset(vst[i][:], 1.0), writes=[vst[i].B])

                NT = nta or (S_TOK // 512)

                def stage1(i):
                    p = i % 2
                    c0 = 512 * i
                    prep_tile(xb[p], sq[p], r1[p], a1[p], xTv[:, :, c0:c0 + 512], 512, dX[p])
                    S.dma("sp", ct[p][:], cosS[:, c0:c0 + 512], dTab[p], reads=[BtabS], writes=[ct[p].B])
                    S.dma("sp", sn[p][:], sinS[:, c0:c0 + 512], dTabS[p], reads=[BtabS], writes=[sn[p].B])
                    for c in range(2):
                        pe([(bk(1 + c), wkv[:, k, 128 * c:128 * c + 128], xb[p][:, k, :], k == 0, k == 7) for k in range(8)],
                           [wkv.B, xb[p].B], [PB[1 + c]])
                        act(sqr[p][:, c, :], bk(1 + c), AF.Square, [PB[1 + c]], [sqr[p].B])
                    pe([(bk(3)[0:32, :], wkv[:, k, 256:288], xb[p][:, k, :], k == 0, k == 7) for k in range(8)],
                       [wkv.B, xb[p].B], [PB[3]])
                    pe([(bk(4)[0:32, :], wkv[:, k, 288:320], xb[p][:, k, :], k == 0, k == 7) for k in range(8)],
                       [wkv.B, xb[p].B], [PB[4]])
                    tt(t1[:], bk(3)[0:32, :], ct[p][:], ALU.mult, [PB[3], ct[p].B], [t1.B])
                    tt(t2[:], bk(4)[0:32, :], sn[p][:], ALU.mult, [PB[4], sn[p].B], [t2.B])
                    tt(t1[:], t1[:], t2[:], ALU.add, [t1.B, t2.B], [t1.B])
                    tt(krs[p][:], t1[:], r1[p][0:32, :], ALU.mult, [t1.B, r1[p].B], [krs[p].B])
                    S.dma("sp", krS[:, c0:c0 + 512], krs[p][:], dKr[p], reads=[krs[p].B], writes=[BscrR])

                def stage2(i):
                    p = i % 2
                    pe([(bk(0), ones[:], sqr[p][:, c, :], c == 0, c == 1) for c in range(2)],
                       [ones.B, sqr[p].B], [PB[0]])
                    tot_ops(tot[:], tot.B, bk(0), PB[0], a1[p][:], a1[p].B, 256)
                    for c in range(2):
                        stt(ckvn[p][:, c, :], bk(1 + c), gcol(G_KV, c), tot[:], ALU.mult, ALU.mult,
                            [PB[1 + c], gT.B, tot.B], [ckvn[p].B])

                rot = [0]

                def rb():
                    b = 5 + rot[0] % 3
                    rot[0] += 1
                    return b

                def stage3(i):
                    p = i % 2
                    c0 = 512 * i
                    for j in range(4):
                        b = rb()
                        pe([(bk(b), wk[:, kc, 128 * j:128 * j + 128], ckvn[p][:, kc, :], kc == 0, kc == 1) for kc in range(2)],
                           [wk.B, ckvn[p].B], [PB[b]])
                        act(kst[p][:, j, :], bk(b), AF.Copy, [PB[b]], [kst[p].B])
                    for s in range(4):
                        b = rb()
                        pe([(bk(b), ckvn[p][:, kc, 128 * s:128 * s + 128], wv[:, kc, :], kc == 0, kc == 1) for kc in range(2)],
                           [wv.B, ckvn[p].B], [PB[b]])
                        act(vst[p][:, :, s, 0:64], bk(b).rearrange("p (h d) -> p h d", d=64), AF.Copy, [PB[b]], [vst[p].B])
                    S.dma("sp", kSv[:, :, c0:c0 + 512], kst[p][:], dKs[p], reads=[kst[p].B], writes=[BscrK])
                    S.dma("sp", vSv[:, :, 4 * i:4 * i + 4, :], vst[p][:], dVs[p], reads=[vst[p].B], writes=[BscrV])

                def dbg_mark(nm):
                    if cut == nm:
                        S.barrier()
                        S.mark(nm)
                dbg_mark("Aw")
                stage1(0)
                load_q_weights()
                dbg_mark("A1")
                for i in range(NT):
                    stage2(i)
                    if i == 0:
                        dbg_mark("A2")
                    if i + 1 < NT:
                        stage1(i + 1)
                    stage3(i)
                    if i == 0:
                        dbg_mark("A3")
                S.barrier()
                S.mark("A")

            with ExitStack() as st:
                stg = [T(st, "stgq%d" % i, [128, 512], F32) for i in range(2)]
                xb = T(st, "xbq", [128, 8, 512], BF16)
                sq = T(st, "sqq", [128, 8, 512], BF16)
                r1 = T(st, "r1q", [128, 512], F32)
                a1 = T(st, "a1q", [128, 512], F32)
                uT = T(st, "uT", [128, 4, OWN + 16], F32)
                st1 = ExitStack()
                cqraw = T(st1, "cqraw", [128, 3, 512], F32)
                sqcq = T(st1, "sqcq", [128, 3, 512], BF16)
                totq = T(st1, "totq", [128, 512], F32)
                cqn = T(st1, "cqn", [128, 3, 512], BF16)
                ctq = T(st1, "ctq", [128, 512], F32)
                snq = T(st1, "snq", [128, 512], F32)
                tq1 = [T(st1, "tq1%d" % i, [128, 512], F32) for i in range(2)]
                tq2 = [T(st1, "tq2%d" % i, [128, 512], F32) for i in range(2)]
                dW = S.dsem(final=True)
                dX = S.dsem()
                dTab = S.dsem()
                dTabS = S.dsem()
                dIc = S.dsem()

                rot = [0]

                def rb():
                    b = 1 + rot[0] % 7
                    rot[0] += 1
                    return b

                def u_part(n, dst_fns):
                    for g in range(4):
                        b = rb()
                        pe([(bk(b)[:, 0:n], wu[:, k, 128 * g:128 * g + 128], xb[:, k, 0:n], k == 0, k == 7) for k in range(8)],
                           [wu.B, xb.B], [PB[b]])
                        for (lo, hi, dfn) in dst_fns:
                            tt(dfn(g), bk(b)[:, lo:hi], r1[:, lo:hi], ALU.mult, [PB[b], r1.B], [uT.B])

                def dbg_markq(nm):
                    if cut == nm:
                        S.barrier()
                        S.mark(nm)
                dbg_markq("Qw")
                prep_tile(xb, sq, r1, a1, xhv, 16, dX, need_a=False)
                u_part(16, [(0, 8, lambda g: uT[:, g, 0:8]), (8, 16, lambda g: uT[:, g, OWN + 8:OWN + 16])])

                dbg_markq("Qh")
                for i in range(4):
                    if i == 1:
                        dbg_markq("Q0")
                    c0 = 512 * i
                    prep_tile(xb, sq, r1, a1, xTv[:, :, c0:c0 + 512], 512, dX)
                    S.dma("sp", ctq[64:96, :], cosS[:, c0:c0 + 512], dTab, reads=[BtabS], writes=[ctq.B])
                    S.dma("sp", snq[64:96, :], sinS[:, c0:c0 + 512], dTabS, reads=[BtabS], writes=[snq.B])
                    u_part(512, [(0, 512, lambda g, c0=c0: uT[:, g, 8 + c0:8 + c0 + 512])])
                    for c in range(3):
                        b = rb()
                        pe([(bk(b), wcq[:, k, 128 * c:128 * c + 128], xb[:, k, :], k == 0, k == 7) for k in range(8)],
                           [wcq.B, xb.B], [PB[b]])
                        act(sqcq[:, c, :], bk(b), AF.Square, [PB[b]], [sqcq.B])
                        dve(lambda e, b=b, c=c: e.tensor_copy(cqraw[:, c, :], bk(b)), [PB[b], sqcq.B], [cqraw.B])
                    b = rb()
                    pe([(bk(b), ones[:], sqcq[:, c, :], c == 0, c == 2) for c in range(3)], [ones.B, sqcq.B], [PB[b]])
                    tot_ops(totq[:], totq.B, bk(b), PB[b], a1[:], a1.B, 384)
                    for c in range(3):
                        stt(cqn[:, c, :], cqraw[:, c, :], gcol(G_Q, c), totq[:], ALU.mult, ALU.mult,
                            [cqraw.B, gT.B, totq.B], [cqn.B])
                    for h in range(8):
                        b1 = rb()
                        pe([(bk(b1)[0:96, :], wq[:, c, 96 * h:96 * h + 96], cqn[:, c, :], c == 0, c == 2) for c in range(3)],
                           [wq.B, cqn.B], [PB[b1]])
                        b2 = rb()
                        pe([(bk(b2)[0:96, :], wqr[:, c, h, :], cqn[:, c, :], c == 0, c == 2) for c in range(3)],
                           [wqr.B, cqn.B], [PB[b2]])
                        x1_, x2_ = tq1[h % 2], tq2[h % 2]
                        act(x1_[0:96, :], bk(b1)[0:96, :], AF.Copy, [PB[b1]], [x1_.B])
                        act(x2_[0:96, :], bk(b2)[0:96, :], AF.Copy, [PB[b2]], [x2_.B])
                        S.op("pool", lambda e, x1_=x1_, h=h, c0=c0: e.tensor_copy(QT[0:64, h, c0:c0 + 512], x1_[0:64, :]), [x1_.B], [QT.B])
                        tt(x1_[64:96, :], x1_[64:96, :], ctq[64:96, :], ALU.mult, [x1_.B, ctq.B], [x1_.B])
                        tt(x2_[64:96, :], x2_[64:96, :], snq[64:96, :], ALU.mult, [x2_.B, snq.B], [x2_.B])
                        tt(QT[64:96, h, c0:c0 + 512], x1_[64:96, :], x2_[64:96, :], ALU.add, [x1_.B, x2_.B], [QT.B])

                S.barrier()
                S.mark("Q1")
                st1.close()
                TA = T(st, "TA", [128, OWN + 16], F32)
                TB = T(st, "TB", [128, OWN + 16], F32)
                invc = T(st, "invc", [128, OWN], F32)
                pooled = T(st, "pooled", [128, 4, OWN], BF16)
                L = OWN + 16

                def sh(dst, src, lo, hi, d1, d2):
                    return lambda e: e.tensor_tensor(dst[:, lo:hi], src[:, lo + d1:hi + d1], src[:, lo + d2:hi + d2], ALU.add)

                for g in range(4):
                    ug = uT.t[:, g, :]
                    S.dma("sp", invc[:], bass.AP(invcnt.tensor, OWN * g, [[0, 128], [1, OWN]]), dIc, writes=[invc.B])
                    S.op("pool", sh(TA, ug, 1, L, -1, 0), [uT.B], [TA.B])
                    win = TA
                    if g >= 1:
                        S.op("pool", sh(TB, TA, 2, L - 1, -1, 1), [TA.B], [TB.B])
                        win = TB
                    if g >= 2:
                        S.op("pool", sh(TA, TB, 4, L - 3, -2, 2), [TB.B], [TA.B])
                        win = TA
                    if g >= 3:
                        S.op("pool", sh(TB, TA, 8, L - 7, -4, 4), [TA.B], [TB.B])
                        win = TB
                    other = TB if win is TA else TA
                    tt(other[:, 8:8 + OWN], win[:, 8:8 + OWN], invc[:], ALU.mult, [win.B, invc.B], [other.B])
                    tt(pooled[:, g, :], other[:, 8:8 + OWN], ug[:, 8:8 + OWN], ALU.subtract, [other.B, uT.B], [pooled.B])
                    for nt in range(4):
                        b = rb()
                        pe([(bk(b), wpg[:, g, :], pooled[:, g, 512 * nt:512 * nt + 512], True, True)], [wpg.B, pooled.B], [PB[b]])
                        act(PM[:, g, 512 * nt:512 * nt + 512], bk(b), AF.Identity, [PB[b], gT.B], [PM.B], scale=gcol(G_PS, g))
                if debug:
                    dD = S.dsem(final=True)
                    S.dma("sp", qtD[0:96], QT[0:96], dD, reads=[QT.B])
                    S.dma("sp", pmD, PM[:], dD, reads=[PM.B])
                S.barrier()
                S.mark("Q")
            qw.close()

            cw = ExitStack()
            stgc2 = [T(cw, "stgc2%d" % i, [128, 1024], F32, side="right") for i in range(2)]
            wg = T(cw, "wg", [128, 8, 2048], BF16, side="right")
            woa = T(cw, "woa", [64, 8, 1024], BF16, side="right")
            wob = T(cw, "wob", [128, 4, 1024], BF16, side="right")
            wout = T(cw, "wout", [128, 8, 1024], BF16, side="right")
            dWc = S.dsem(final=True)
            dStc = [S.dsem(), S.dsem()]

            def load_c1_weights():
                st = None
                stg = stgc2
                dW = dWc
                load_folded(st, stg, lambda k: wg[:, k, 0:1024], lambda k: w_inv[:, k, 1184:2208], 1024, G_PRE, q="pool", dsems=dStc)
                load_folded(st, stg, lambda k: wg[:, k, 1024:2048], lambda k: w_inv[:, k, 2208:3232], 1024, G_PRE, q="pool", dsems=dStc)
                wg.B.writers = [S.q["dve"][-1]]
                S.dma("pool", woa[:], w_o_attn.rearrange("(h d) n -> d h n", d=64), dW, writes=[woa.B])
                S.dma("pool", wob[:], w_o_pool.rearrange("(c p) n -> p c n", p=128), dW, writes=[wob.B])
                for k in range(8):
                    S.dma("pool", wout[:, k, :], w_out[128 * k:128 * k + 128, :], dW, writes=[wout.B])

            with ExitStack() as st:
                NSL = 4
                NSB = 3
                NPB = 4
                LOOK = 2
                kc = [T(st, "kc%d" % i, [128, 2048], BF16) for i in range(NSL)]
                vc = [T(st, "vc%d" % i, [128, 16, 65], BF16) for i in range(NSL)]
                Pb = [T(st, "P%d" % i, [128, 1024], BF16) for i in range(NPB)]
                rs = T(st, "rs", [128, 1024], F32)
                rc = [T(st, "rc%d" % i, [64, 512], F32) for i in range(2)]
                dSl = [S.dsem() for _ in range(NSL)]
                dSlV = [S.dsem() for _ in range(NSL)]
                scale = 96 ** -0.5
                chunks = [(h, qh, c) for h in range(8) for qh in range(2) for c in range(8)]
                if nchunks:
                    chunks = chunks[:nchunks]

                def load_chunk(n):
                    h, qh, c = chunks[n]
                    s = n % NSL
                    S.dma("sp", kc[s][0:64, :], kS[64 * h:64 * h + 64, 2048 * c:2048 * (c + 1)], dSl[s], reads=[BscrK], writes=[kc[s].B])
                    S.dma("sp", kc[s][64:96, :], krS[:, 2048 * c:2048 * (c + 1)], dSl[s], reads=[BscrR], writes=[kc[s].B])
                    S.dma("sp", vc[s][:], vS[h, :, 16 * c:16 * c + 16, :], dSlV[s], reads=[BscrV], writes=[vc[s].B])

                for n in range(min(3, len(chunks))):
                    load_chunk(n)
                load_c1_weights()
                step = [0]
                pend = []

                def emit_pv(pv):
                    (s, kt, pb, first, last) = pv
                    pe([(ps[0:65, 512 * q:512 * (q + 1)], vc[s][:, kt, :], Pb[pb][:, 512 * q:512 * q + 512], first, last)
                        for q in range(2)],
                       [vc[s].B, Pb[pb].B], [PB[0], PB[1]])

                for n, (h, qh, c) in enumerate(chunks):
                    s = n % NSL
                    q0 = 1024 * qh
                    for kt in range(16):
                        sb_ = step[0] % NSB
                        pb = step[0] % NPB
                        step[0] += 1
                        b0 = 2 + 2 * sb_
                        pe([(bk(b0 + q), kc[s][0:96, 128 * kt:128 * kt + 128], QT[0:96, h, q0 + 512 * q:q0 + 512 * q + 512], True, True)
                            for q in range(2)],
                           [kc[s].B, QT.B], [PB[b0], PB[b0 + 1]])
                        act(Pb[pb][:], ps[:, 512 * b0:512 * b0 + 1024], AF.Exp, [PB[b0], PB[b0 + 1]], [Pb[pb].B], scale=scale)
                        pend.append((s, kt, pb, (c == 0 and kt == 0), (c == 7 and kt == 15)))
                        if len(pend) > LOOK:
                            emit_pv(pend.pop(0))
                        if kt == LOOK and n + 3 < len(chunks):
                            load_chunk(n + 3)
                    if c == 7:
                        while pend:
                            emit_pv(pend.pop(0))
                        for qb in range(2):
                            act(rs[64:65, 512 * qb:512 * qb + 512], bk(qb)[64:65, :], AF.Copy, [PB[qb]], [rs.B])
                            pe([(bk(2 + qb)[0:64, :], onesf[64:65, 0:64], rs[64:65, 512 * qb:512 * qb + 512], True, True)],
                               [onesf.B, rs.B], [PB[2 + qb]])
                            r_ = rc[qb % 2]
                            dve(lambda e, r_=r_, qb=qb: e.reciprocal(r_[:], bk(2 + qb)[0:64, :]), [PB[2 + qb]], [r_.B])
                            tt(AT[0:64, h, q0 + 512 * qb:q0 + 512 * qb + 512], bk(qb)[0:64, :], r_[:], ALU.mult, [PB[qb], r_.B], [AT.B])
                if debug:
                    dD2 = S.dsem(final=True)
                    S.dma("sp", atD[0:64], AT[0:64], dD2, reads=[AT.B])
                S.barrier()
                S.mark("B")

            qsc.close()
            with ExitStack() as st:
                xb = T(st, "xbc", [128, 8, 512], BF16)
                sq = T(st, "sqc", [128, 8, 512], BF16)
                xf = T(st, "xfc", [128, 8, 512], F32)
                r1 = T(st, "r1c", [128, 512], F32)
                r2 = T(st, "r2c", [128, 512], F32)
                zA = [T(st, "zA%d" % i, [128, 512], F32) for i in range(2)]
                zB = [T(st, "zB%d" % i, [128, 512], F32) for i in range(2)]
                m = T(st, "m", [128, 8, 512], BF16)
                y = T(st, "y", [128, 8, 512], F32)
                sqy = [T(st, "sqy%d" % i, [128, 512], BF16) for i in range(2)]
                dW = S.dsem(final=True)
                dX = S.dsem()
                dXf = S.dsem()
                dO = S.dsem()
                Bx1S = Buf("x1S")
                for i in range(4):
                    c0 = 512 * i
                    S.dma("sp", xf[:], xTv[:, :, c0:c0 + 512], dXf, writes=[xf.B])
                    prep_tile(xb, sq, r1, None, xTv[:, :, c0:c0 + 512], 512, dX, need_a=False)
                    for j in range(8):
                        bA, bB, ba, bb = (1, 2, 3, 4) if j % 2 == 0 else (5, 6, 7, 4)
                        z1, z2 = zA[j % 2], zB[j % 2]
                        pe([(bk(bA), wg[:, k, 128 * j:128 * j + 128], xb[:, k, :], k == 0, k == 7) for k in range(8)],
                           [wg.B, xb.B], [PB[bA]])
                        pe([(bk(bB), wg[:, k, 1024 + 128 * j:1024 + 128 * j + 128], xb[:, k, :], k == 0, k == 7) for k in range(8)],
                           [wg.B, xb.B], [PB[bB]])
                        pe([(bk(ba), woa[0:64, h, 128 * j:128 * j + 128], AT[0:64, h, c0:c0 + 512], h == 0, h == 7) for h in range(8)],
                           [woa.B, AT.B], [PB[ba]])
                        tt(z1[:], bk(bA), r1[:], ALU.mult, [PB[bA], r1.B], [z1.B])
                        act(z1[:], z1[:], AF.Sigmoid, [z1.B], [z1.B])
                        tt(z1[:], z1[:], bk(ba), ALU.mult, [z1.B, PB[ba]], [z1.B])
                        pe([(bk(bb), wob[:, c, 128 * j:128 * j + 128], PM[:, c, c0:c0 + 512], c == 0, c == 3) for c in range(4)],
                           [wob.B, PM.B], [PB[bb]])
                        tt(z2[:], bk(bB), r1[:], ALU.mult, [PB[bB], r1.B], [z2.B])
                        act(z2[:], z2[:], AF.Sigmoid, [z2.B], [z2.B])
                        tt(z2[:], z2[:], bk(bb), ALU.mult, [z2.B, PB[bb]], [z2.B])
                        tt(m[:, j, :], z1[:], z2[:], ALU.add, [z1.B, z2.B], [m.B])
                    for j in range(8):
                        b = 1 + j % 6
                        pe([(bk(b), wout[:, k, 128 * j:128 * j + 128], m[:, k, :], k == 0, k == 7) for k in range(8)],
                           [wout.B, m.B], [PB[b]])
                        act(y[:, j, :], bk(b), AF.Copy, [PB[b]], [y.B])
                        sy = sqy[j % 2]
                        act(sy[:], bk(b), AF.Square, [PB[b]], [sy.B])
                        pe([(bk(7), ones[:], sy[:], j == 0, j == 7)], [ones.B, sy.B], [PB[7]])
                    rstd_ops(r2[:], r2.B, bk(7), PB[7], 1.0 / 1024)
                    for j in range(8):
                        stt(y[:, j, :], y[:, j, :], gcol(G_POST, j), r2[:], ALU.mult, ALU.mult, [y.B, gT.B, r2.B], [y.B])
                        tt(y[:, j, :], y[:, j, :], xf[:, j, :], ALU.add, [y.B, xf.B], [y.B])
                    S.dma("sp", x1Sv[:, :, c0:c0 + 512], y[:], dO, reads=[y.B], writes=[Bx1S])
                S.barrier()
                S.mark("C1")
            cw.close()

        with ExitStack() as st:
            x1h = T(st, "x1h", [128, 8, 1024], F32)
            hfb = T(st, "hfb", [128, 8, 1024], BF16)
            actT = T(st, "actT", [128, 22, 1024], BF16)
            y2 = T(st, "y2", [128, 8, 1024], F32)
            wgu = [T(st, "wgu%d" % i, [128, 8, 256], BF16) for i in range(3)]
            wd = [T(st, "wd%d" % i, [128, 22, 128], BF16) for i in range(2)]
            sqc = [T(st, "sqf%d" % i, [128, 512], BF16) for i in range(2)]
            sg = [T(st, "sg%d" % i, [128, 512], F32) for i in range(2)]
            r3 = T(st, "r3", [128, 1024], F32)
            r4 = T(st, "r4", [128, 1024], F32)
            dX1 = S.dsem()
            dGU = [S.dsem() for _ in range(3)]
            dWD = [S.dsem() for _ in range(2)]
            dOut = S.dsem()
            wguv = w_gate_up.rearrange("(k p) n -> p k n", p=128)
            wdv = w_down.rearrange("(k p) n -> p k n", p=128)

            def load_gu(hf, j):
                s = (hf * 22 + j) % 3
                S.dma("pool", wgu[s][:, :, 0:128], wguv[:, :, 128 * j:128 * j + 128], dGU[s], writes=[wgu[s].B])
                S.dma("pool", wgu[s][:, :, 128:256], wguv[:, :, 2816 + 128 * j:2816 + 128 * j + 128], dGU[s], writes=[wgu[s].B])

            def load_wd(hf, i):
                s = (hf * 8 + i) % 2
                S.dma("pool", wd[s][:], wdv[:, :, 128 * i:128 * i + 128], dWD[s], writes=[wd[s].B])

            for hf in range(2):
                h0 = 1024 * hf
                S.dma("sp", x1h[:], x1Sv[:, :, h0:h0 + 1024], dX1, reads=[Bx1S], writes=[x1h.B])
                load_gu(hf, 0)
                load_gu(hf, 1)
                for nt in range(2):
                    n0 = 512 * nt
                    for k in range(8):
                        sc_ = sqc[k % 2]
                        act(sc_[:], x1h[:, k, n0:n0 + 512], AF.Square, [x1h.B], [sc_.B])
                        pe([(bk(6 + nt), ones[:], sc_[:], k == 0, k == 7)], [ones.B, sc_.B], [PB[6 + nt]])
                    rstd_ops(r3[:, n0:n0 + 512], r3.B, bk(6 + nt), PB[6 + nt], 1.0 / 1024)
                    for k in range(8):
                        stt(hfb[:, k, n0:n0 + 512], x1h[:, k, n0:n0 + 512], gcol(G_FPRE, k), r3[:, n0:n0 + 512],
                            ALU.mult, ALU.mult, [x1h.B, gT.B, r3.B], [hfb.B])
                cnt = 0
                for j in range(22):
                    if j + 2 < 22:
                        load_gu(hf, j + 2)
                    if j == 20:
                        load_wd(hf, 0)
                    if j == 21:
                        load_wd(hf, 1)
                    s = (hf * 22 + j) % 3
                    for nt in range(2):
                        n0 = 512 * nt
                        bg, bu = (0, 1) if cnt % 2 == 0 else (2, 3)
                        sg_ = sg[cnt % 2]
                        cnt += 1
                        pe([(bk(bg), wgu[s][:, k, 0:128], hfb[:, k, n0:n0 + 512], k == 0, k == 7) for k in range(8)],
                           [wgu[s].B, hfb.B], [PB[bg]])
                        pe([(bk(bu), wgu[s][:, k, 128:256], hfb[:, k, n0:n0 + 512], k == 0, k == 7) for k in range(8)],
                           [wgu[s].B, hfb.B], [PB[bu]])
                        act(sg_[:], bk(bg), AF.Silu, [PB[bg]], [sg_.B])
                        tt(actT[:, j, n0:n0 + 512], sg_[:], bk(bu), ALU.mult, [sg_.B, PB[bu]], [actT.B])
                cnt = 0
                for i in range(8):
                    s = (hf * 8 + i) % 2
                    for nt in range(2):
                        n0 = 512 * nt
                        b = cnt % 4
                        sc_ = sqc[cnt % 2]
                        cnt += 1
                        pe([(bk(b), wd[s][:, k, :], actT[:, k, n0:n0 + 512], k == 0, k == 21) for k in range(22)],
                           [wd[s].B, actT.B], [PB[b]])
                        act(y2[:, i, n0:n0 + 512], bk(b), AF.Copy, [PB[b]], [y2.B])
                        act(sc_[:], bk(b), AF.Square, [PB[b]], [sc_.B])
                        pe([(bk(4 + nt), ones[:], sc_[:], i == 0, i == 7)], [ones.B, sc_.B], [PB[4 + nt]])
                    if i + 2 < 8:
                        load_wd(hf, i + 2)
                for nt in range(2):
                    n0 = 512 * nt
                    rstd_ops(r4[:, n0:n0 + 512], r4.B, bk(4 + nt), PB[4 + nt], 1.0 / 1024)
                    for i in range(8):
                        stt(y2[:, i, n0:n0 + 512], y2[:, i, n0:n0 + 512], gcol(G_FPOST, i), r4[:, n0:n0 + 512],
                            ALU.mult, ALU.mult, [y2.B, gT.B, r4.B], [y2.B])
                        tt(y2[:, i, n0:n0 + 512], y2[:, i, n0:n0 + 512], x1h[:, i, n0:n0 + 512], ALU.add, [y2.B, x1h.B], [y2.B])
                S.dma("sp", outTv[:, :, h0:h0 + 1024], y2[:], dOut, reads=[y2.B])
            S.barrier()
            S.mark("C2")

        if cut:
            S.cut(cut)
        S.emit(None)
        S.check()
        with nc.Block() as block:
            @block.tensor
            def _(e):
                S.play("pe", e)

            @block.scalar
            def _(e):
                S.play("act", e)

            @block.vector
            def _(e):
                S.play("dve", e)

            @block.gpsimd
            def _(e):
                S.play("pool", e)

            @block.sync
            def _(e):
                S.play("sp", e)
    return nc


def make_inputs(inputs, core):
    x = np.asarray(inputs["x"], np.float32)[0]
    pos = np.asarray(inputs["positions"], np.int32)[0]
    o0 = core * OWN
    xr = np.roll(x, -o0, axis=0)
    xTc = np.ascontiguousarray(xr.T)
    posr = np.ascontiguousarray(np.roll(pos, -o0)[None, :])
    xh = np.zeros((16, 1024), np.float32)
    if o0 - 8 >= 0:
        xh[0:8] = x[o0 - 8:o0]
    if o0 + OWN + 8 <= S_TOK:
        xh[8:16] = x[o0 + OWN:o0 + OWN + 8]
    xhT = np.ascontiguousarray(xh.T)
    return xTc, xhT, posr


_CONSTS = {}


def const_tables(core):
    if core in _CONSTS:
        return _CONSTS[core]
    inv = (1.0 / (10000.0 ** (np.arange(0, 32, 2, dtype=np.float32) / np.float32(32)))).astype(np.float32)
    cst = np.zeros((128, 4), np.float32)
    for p in range(128):
        cst[p, 0] = inv[p % 16]
        cst[p, 1] = -1.0 if (p % 32) < 16 else 1.0
        cst[p, 2] = 0.0
        cst[p, 3] = EPS
    t = np.arange(core * OWN, (core + 1) * OWN)
    invcnt = np.zeros((4, OWN), np.float32)
    for g, w in enumerate(POOL_W):
        left = w // 2
        right = w - left - 1
        cnt = (np.minimum(t + right, S_TOK - 1) - np.maximum(t - left, 0) + 1).astype(np.float32)
        invcnt[g] = (1.0 / cnt).astype(np.float32)
    _CONSTS[core] = (cst, invcnt)
    return _CONSTS[core]


def chunk_cols(v):
    v = np.asarray(v, np.float32)
    return np.ascontiguousarray(v.reshape(-1, 128).T)


_NC = {}


def kernel(x, positions, g_mix_pre, w_in, g_q_lat, w_uq, g_kv_lat, w_ukv, w_o_attn,
           w_pool_group, pool_scale, w_o_pool, w_out, g_mix_post, g_ffn_pre,
           w_gate_up, w_down, g_ffn_post, _debug=False):
    inputs = {"x": x, "positions": positions}
    gains = np.concatenate([chunk_cols(g_mix_pre), chunk_cols(g_q_lat), chunk_cols(g_kv_lat),
                            chunk_cols(pool_scale), chunk_cols(g_mix_post), chunk_cols(g_ffn_pre),
                            chunk_cols(g_ffn_post)], axis=1)
    gains = np.ascontiguousarray(gains, dtype=np.float32)
    assert gains.shape == (128, 41)
    f = lambda a: np.ascontiguousarray(np.asarray(a, np.float32))
    common = dict(gains=gains, w_in=f(w_in), w_uq=f(w_uq), w_ukv=f(w_ukv), w_o_attn=f(w_o_attn),
                  w_pool_group=f(w_pool_group), w_o_pool=f(w_o_pool), w_out=f(w_out),
                  w_gate_up=f(w_gate_up), w_down=f(w_down))
    in_maps = []
    for c in range(NCORES):
        xTc, xhT, posr = make_inputs(inputs, c)
        cst, invcnt = const_tables(c)
        m = dict(common)
        m.update(xT=xTc, xh=xhT, pos=posr, cst=cst, invcnt=invcnt)
        in_maps.append(m)
    key = bool(_debug)
    if key not in _NC:
        _NC[key] = build_program(debug=key)
    nc = _NC[key]
    res = run_bass_kernel_spmd(nc, in_maps, core_ids=list(range(NCORES)))
    if _debug:
        return res
    outT = np.concatenate([np.asarray(r["outT"]) for r in res.results], axis=1)
    return np.ascontiguousarray(outT.T)[None, :, :].astype(np.float32)
```

```python
import math
from contextlib import ExitStack

import numpy as np
import concourse.bass as bass
import concourse.mybir as mybir
from concourse.bass_utils import run_bass_kernel_spmd

ENGS = ("pe", "act", "dve", "pool", "sp")


class Buf:
    __slots__ = ("name", "writers", "readers")

    def __init__(self, name):
        self.name = name
        self.writers = []
        self.readers = []


class DSem:
    __slots__ = ("sem", "count", "final")

    def __init__(self, sem, final=False):
        self.sem = sem
        self.count = 0
        self.final = final


class Op:
    __slots__ = ("eng", "fn", "deps", "signal", "sigval", "dsem", "dval", "idx")

    def __init__(self, eng, fn):
        self.eng = eng
        self.fn = fn
        self.deps = []
        self.signal = False
        self.sigval = None
        self.dsem = None
        self.dval = None


class Sched:
    def __init__(self, nc, stack):
        self.nc = nc
        self.stack = stack
        self.q = {e: [] for e in ENGS}
        self.esem = {e: stack.enter_context(nc.semaphore("es_" + e)) for e in ENGS}
        self.dmas = []
        self.marks = {}
        self.nds = 0

    def mark(self, name):
        self.marks[name] = {e: len(self.q[e]) for e in ENGS}

    def cut(self, name):
        for e in ENGS:
            del self.q[e][self.marks[name][e]:]

    def dsem(self, final=False):
        self.nds += 1
        return DSem(self.stack.enter_context(self.nc.semaphore("ds%d" % self.nds)), final)

    def _track(self, op, reads, writes, group_dma_writes=False):
        deps = set()
        for b in reads:
            for w in b.writers:
                deps.add((w, True))
        for b in writes:
            for w in b.writers:
                if group_dma_writes and w.dsem is not None and not b.readers and w.dsem is op.dsem:
                    continue
                deps.add((w, False))
            for r in b.readers:
                deps.add((r, False))
        for d, raw in deps:
            if d is op:
                continue
            if d.dsem is None and op.dsem is None and d.eng == op.eng:
                if not (raw and op.eng in ("act", "dve", "pool")):
                    continue
            op.deps.append(d)
            if d.dsem is None:
                d.signal = True
        for b in reads:
            b.readers.append(op)
        for b in writes:
            if group_dma_writes and b.writers and all(
                    w.dsem is not None and w.dsem is op.dsem for w in b.writers) and not b.readers:
                b.writers.append(op)
            else:
                b.writers = [op]
                b.readers = []

    def op(self, eng, fn, reads=(), writes=()):
        o = Op(eng, fn)
        self._track(o, reads, writes)
        self.q[eng].append(o)
        return o

    def dma(self, eng, out, in_, dsem, reads=(), writes=()):
        def fn(e, out=out, in_=in_):
            return e.dma_start(out=out, in_=in_)
        o = Op(eng, fn)
        o.dsem = dsem
        dsem.count += 16
        o.dval = dsem.count
        self._track(o, reads, writes, group_dma_writes=True)
        self.q[eng].append(o)
        self.dmas.append(o)
        return o

    def barrier(self):
        lasts = []
        for e in ENGS:
            for o in reversed(self.q[e]):
                if o.dsem is None and o.fn is not None:
                    o.signal = True
                    lasts.append(o)
                    break
        dm = {}
        for o in self.dmas:
            dm[id(o.dsem)] = o
        self.dmas = []
        for e in ENGS:
            def fn(eng):
                return None
            b = Op(e, None)
            for l in lasts:
                if l.eng != e:
                    b.deps.append(l)
            b.deps.extend(dm.values())
            self.q[e].append(b)

    def emit(self, engines):
        for e in ENGS:
            c = 0
            for o in self.q[e]:
                if o.dsem is None and o.signal and o.fn is not None:
                    c += 1
                    o.sigval = c

    def _waits(self, o):
        waits = {}
        for d in o.deps:
            if d.dsem is not None:
                key = id(d.dsem)
                val = d.dsem.count if d.dsem.final else d.dval
                sem = d.dsem.sem
            else:
                key = d.eng
                val = d.sigval
                sem = self.esem[d.eng]
            assert val is not None, (o.eng, d.eng, d.fn)
            if key not in waits or waits[key][1] < val:
                waits[key] = (sem, val)
        return waits

    def check(self):
        cnt = {}
        ptr = {e: 0 for e in ENGS}
        total = sum(len(self.q[e]) for e in ENGS)
        done = 0
        while done < total:
            prog = False
            for e in ENGS:
                while ptr[e] < len(self.q[e]):
                    o = self.q[e][ptr[e]]
                    ok = all(cnt.get(k, 0) >= v for k, (s_, v) in self._waits(o).items())
                    if not ok:
                        break
                    if o.fn is not None:
                        if o.dsem is not None:
                            cnt[id(o.dsem)] = cnt.get(id(o.dsem), 0) + 16
                        elif o.signal:
                            cnt[e] = cnt.get(e, 0) + 1
                    ptr[e] += 1
                    done += 1
                    prog = True
            if not prog:
                st = {e: (ptr[e], len(self.q[e])) for e in ENGS}
                raise RuntimeError("schedule deadlock: %s" % st)
        return True

    def play(self, e, eng):
        seen = {}
        last_real = None
        for o in self.q[e]:
            waits = {}
            for d in o.deps:
                if d.dsem is not None:
                    key = id(d.dsem)
                    val = d.dsem.count if d.dsem.final else d.dval
                    sem = d.dsem.sem
                else:
                    key = d.eng
                    val = d.sigval
                    sem = self.esem[d.eng]
                if key not in waits or waits[key][1] < val:
                    waits[key] = (sem, val)
            for key, (sem, val) in waits.items():
                if seen.get(key, 0) >= val:
                    continue
                seen[key] = val
                eng.wait_ge(sem, val)
            if o.fn is None:
                continue
            ins = o.fn(eng)
            if o.dsem is not None:
                ins.then_inc(o.dsem.sem, 16)
            elif o.signal:
                ins.then_inc(self.esem[e], 1)


F32 = mybir.dt.float32
BF16 = mybir.dt.bfloat16
I32 = mybir.dt.int32
AF = mybir.ActivationFunctionType
ALU = mybir.AluOpType

S_TOK = 16384
OWN = 2048
NCORES = 8
EPS = 1e-6
G_PRE, G_Q, G_KV, G_PS, G_POST, G_FPRE, G_FPOST = 0, 8, 11, 13, 17, 25, 33
POOL_W = (2, 4, 8, 16)


def build_program(debug=False, cut=None, nta=None, nchunks=None):
    nc = bass.Bass("TRN2", target_bir_lowering=False)

    def din(name, shape, dt=F32):
        return nc.dram_tensor(name, shape, dt, kind="ExternalInput").ap()

    xT = din("xT", [1024, S_TOK])
    xh = din("xh", [1024, 16])
    pos = din("pos", [1, S_TOK], I32)
    cst = din("cst", [128, 4])
    gains = din("gains", [128, 41])
    invcnt = din("invcnt", [4, OWN])
    w_in = din("w_in", [1024, 3232])
    w_uq = din("w_uq", [384, 768])
    w_ukv = din("w_ukv", [256, 1024])
    w_o_attn = din("w_o_attn", [512, 1024])
    w_pool_group = din("w_pool_group", [4, 128, 128])
    w_o_pool = din("w_o_pool", [512, 1024])
    w_out = din("w_out", [1024, 1024])
    w_gate_up = din("w_gate_up", [1024, 5632])
    w_down = din("w_down", [2816, 1024])
    outT = nc.dram_tensor("outT", [1024, OWN], F32, kind="ExternalOutput").ap()
    skind = dict(kind="ExternalOutput") if debug else {}
    kS = nc.dram_tensor("kS", [512, S_TOK], BF16, **skind).ap()
    krS = nc.dram_tensor("krS", [32, S_TOK], BF16, **skind).ap()
    vS = nc.dram_tensor("vS", [8, 128, 128, 65], BF16, **skind).ap()
    cosS = nc.dram_tensor("cosS", [32, S_TOK], F32, **skind).ap()
    sinS = nc.dram_tensor("sinS", [32, S_TOK], F32, **skind).ap()
    x1S = nc.dram_tensor("x1S", [1024, OWN], F32, **skind).ap()
    if debug:
        qtD = nc.dram_tensor("qtD", [128, 8, OWN], BF16, kind="ExternalOutput").ap()
        atD = nc.dram_tensor("atD", [128, 8, OWN], BF16, kind="ExternalOutput").ap()
        pmD = nc.dram_tensor("pmD", [128, 4, OWN], BF16, kind="ExternalOutput").ap()

    xTv = xT.rearrange("(k p) n -> p k n", p=128)
    xhv = xh.rearrange("(k p) n -> p k n", p=128)
    w_inv = w_in.rearrange("(k p) n -> p k n", p=128)
    outTv = outT.rearrange("(k p) n -> p k n", p=128)
    x1Sv = x1S.rearrange("(k p) n -> p k n", p=128)
    kSv = kS.rearrange("(j q) t -> q j t", q=128)
    vSv = vS.rearrange("h p t d -> p h t d")

    with ExitStack() as gst:
        S = Sched(nc, gst)
        ps = gst.enter_context(nc.psum_tensor("ps", [128, 4096], F32))
        PB = [Buf("pb%d" % b) for b in range(8)]

        def bk(b):
            return ps[:, 512 * b:512 * (b + 1)]

        class T:
            def __init__(self, st, name, shape, dt, nb=1, side=None):
                if side:
                    self.t = st.enter_context(nc.sbuf_tensor("sb_" + name, shape, dt, side=side))
                else:
                    self.t = st.enter_context(nc.sbuf_tensor("sb_" + name, shape, dt))
                self.b = [Buf(name + str(i)) for i in range(nb)]

            def __getitem__(self, k):
                return self.t[k]

            @property
            def B(self):
                return self.b[0]

        def pe(mms, reads, writes):
            def fn(e, mms=mms):
                ins = None
                for (o, l, r, st_, sp_) in mms:
                    ins = e.matmul(o, l, r, start=st_, stop=sp_)
                return ins
            return S.op("pe", fn, reads, writes)

        def act(out, in_, func, reads, writes, scale=1.0, bias=0.0):
            return S.op("act", lambda e: e.activation(out, in_, func, bias=bias, scale=scale), reads, writes)

        def dve(fn, reads, writes):
            return S.op("dve", fn, reads, writes)

        def tt(out, a, b, op, reads, writes, eng="dve"):
            return S.op(eng, lambda e: e.tensor_tensor(out, a, b, op), reads, writes)

        def stt(out, in0, sc, in1, op0, op1, reads, writes):
            return S.op("dve", lambda e: e.scalar_tensor_tensor(out, in0, sc, in1, op0, op1), reads, writes)

        def ts(out, in0, s1, s2, op0, op1, reads, writes):
            if s2 is None:
                return S.op("dve", lambda e: e.tensor_scalar(out, in0, s1, None, op0), reads, writes)
            return S.op("dve", lambda e: e.tensor_scalar(out, in0, s1, s2, op0, op1), reads, writes)

        cstT = T(gst, "cst", [128, 4], F32)
        gT = T(gst, "gains", [128, 41], F32)
        ones = T(gst, "ones", [128, 128], BF16)
        onesf = T(gst, "onesf", [128, 128], F32)
        dC = S.dsem(final=True)
        S.dma("sp", cstT[:], cst, dC, writes=[cstT.B])
        S.dma("sp", gT[:], gains, dC, writes=[gT.B])
        S.op("pool", lambda e: e.memset(ones[:], 1.0), writes=[ones.B])
        S.op("pool", lambda e: e.memset(onesf[:], 1.0), writes=[onesf.B])
        epsb = cstT[:, 3:4]

        def gcol(off, k, p0=0, p1=128):
            return gT[p0:p1, off + k:off + k + 1]

        def recip(out_ap, out_B, eng):
            if eng == "pool":
                n = out_ap.shape[-1]
                S.op("pool", lambda e: e.tensor_tensor(out_ap, onesf[:, 0:1].to_broadcast([128, n]), out_ap, ALU.divide),
                     [out_B, onesf.B], [out_B])
            else:
                dve(lambda e: e.reciprocal(out_ap, out_ap), [out_B], [out_B])

        def rstd_ops(out_ap, out_B, ss_ap, ss_B, sc, eng="dve"):
            act(out_ap, ss_ap, AF.Sqrt, [ss_B, cstT.B], [out_B], scale=sc, bias=epsb)
            recip(out_ap, out_B, eng)

        def tot_ops(out_ap, out_B, ss_ap, ss_B, a_ap, a_B, nf, eng="dve"):
            stt(out_ap, ss_ap, 1.0 / nf, a_ap, ALU.mult, ALU.add, [ss_B, a_B], [out_B])
            act(out_ap, out_ap, AF.Sqrt, [out_B], [out_B])
            recip(out_ap, out_B, eng)

        dStage = [S.dsem(), S.dsem()]
        stage_ctr = [0]

        def load_folded(st, stg, dst_fn, src_fn, ncols, goff, q="sp", dsems=None):
            dsems = dsems or dStage
            for k in range(8):
                i = stage_ctr[0] % 2
                stage_ctr[0] += 1
                S.dma(q, stg[i][:, 0:ncols], src_fn(k), dsems[i], writes=[stg[i].B])
                ts(dst_fn(k), stg[i][:, 0:ncols], gcol(goff, k), None, ALU.mult, None,
                   [stg[i].B, gT.B], [])

        with ExitStack() as st:
            I = T(st, "ti", [128, 4096], I32)
            X = T(st, "tx", [128, 4096], F32)
            Y = T(st, "ty", [128, 4096], F32)
            Z = T(st, "tz", [128, 4096], F32)
            W = T(st, "tw", [128, 4096], F32)
            d0 = S.dsem(final=True)
            for s in range(4):
                S.dma("sp", I[32 * s:32 * s + 32, :],
                      bass.AP(pos.tensor, 4096 * s, [[0, 32], [1, 4096]]),
                      d0, writes=[I.B])
            C1 = 6.28125
            C2 = 2 * math.pi - C1
            dve(lambda e: e.tensor_copy(X[:], I[:]), [I.B], [X.B])
            ts(X[:], X[:], cstT[:, 0:1], None, ALU.mult, None, [X.B, cstT.B], [X.B])
            ts(I[:], X[:], 1.0 / (2 * math.pi), None, ALU.mult, None, [X.B], [I.B])
            dve(lambda e: e.tensor_copy(Y[:], I[:]), [I.B], [Y.B])
            stt(Z[:], Y[:], -C1, X[:], ALU.mult, ALU.add, [Y.B, X.B], [Z.B])
            stt(Z[:], Y[:], -C2, Z[:], ALU.mult, ALU.add, [Y.B, Z.B], [Z.B])
            ts(Y[:], Z[:], math.pi, -2 * math.pi, ALU.is_gt, ALU.mult, [Z.B], [Y.B])
            tt(Z[:], Z[:], Y[:], ALU.add, [Z.B, Y.B], [Z.B])
            act(X[:], Z[:], AF.Sin, [Z.B, cstT.B], [X.B], scale=cstT[:, 1:2])
            dT = S.dsem(final=True)
            BtabS = Buf("tabS")
            for s in range(4):
                S.dma("sp", sinS[:, 4096 * s:4096 * (s + 1)], X[32 * s:32 * s + 32, :], dT, reads=[X.B], writes=[BtabS])
            ts(Y[:], Z[:], math.pi / 2, None, ALU.add, None, [Z.B], [Y.B])
            ts(W[:], Y[:], math.pi, -2 * math.pi, ALU.is_gt, ALU.mult, [Y.B], [W.B])
            tt(Y[:], Y[:], W[:], ALU.add, [Y.B, W.B], [Y.B])
            act(W[:], Y[:], AF.Sin, [Y.B], [W.B])
            for s in range(4):
                S.dma("sp", cosS[:, 4096 * s:4096 * (s + 1)], W[32 * s:32 * s + 32, :], dT, reads=[W.B], writes=[BtabS])
            S.barrier()
            S.mark("p0")

        def prep_tile(xb, sq, r1, a1, src_ap, n, dX, need_a=True, eng="dve"):
            S.dma("pool", xb[:, :, 0:n], src_ap, dX, writes=[xb.B])
            act(sq[:, :, 0:n], xb[:, :, 0:n], AF.Square, [xb.B], [sq.B])
            pe([(bk(0)[:, 0:n], ones[:], sq[:, k, 0:n], k == 0, k == 7) for k in range(8)],
               [sq.B, ones.B], [PB[0]])
            rstd_ops(r1[:, 0:n], r1.B, bk(0)[:, 0:n], PB[0], 1.0 / 1024, eng=eng)
            if need_a:
                ts(a1[:, 0:n], bk(0)[:, 0:n], EPS / 1024, EPS * EPS, ALU.mult, ALU.add, [PB[0], r1.B], [a1.B])

        with ExitStack() as bq:
            AT = T(bq, "AT", [128, 8, OWN], BF16)
            PM = T(bq, "PM", [128, 4, OWN], BF16)
            qsc = ExitStack()
            QT = T(qsc, "QT", [128, 8, OWN], BF16)

            qw = ExitStack()
            stgq2 = [T(qw, "stgq2%d" % i, [128, 512], F32, side="right") for i in range(2)]
            wcq = T(qw, "wcq", [128, 8, 384], BF16, side="right")
            wu = T(qw, "wu", [128, 8, 512], BF16, side="right")
            wq = T(qw, "wq", [128, 3, 768], BF16, side="right")
            wqr = T(qw, "wqr", [128, 3, 8, 96], BF16, side="right")
            wpg = T(qw, "wpg", [128, 4, 128], BF16, side="right")
            dWq = S.dsem(final=True)
            dStq = [S.dsem(), S.dsem()]

            def load_q_weights():
                st = None
                stg = stgq2
                dW = dWq
                load_folded(st, stg, lambda k: wcq[:, k, :], lambda k: w_inv[:, k, 0:384], 384, G_PRE, dsems=dStq)
                wcq.B.writers = [S.q["dve"][-1]]
                load_folded(st, stg, lambda k: wu[:, k, :], lambda k: w_inv[:, k, 672:1184], 512, G_PRE, dsems=dStq)
                wu.B.writers = [S.q["dve"][-1]]
                S.dma("pool", wq[:], w_uq.rearrange("(k p) n -> p k n", p=128), dW, writes=[wq.B])
                S.op("pool", lambda e: e.memset(wqr[:], 0.0), writes=[wqr.B])
                uqv = w_uq.rearrange("(k p) (h d) -> p k h d", p=128, d=96)
                for c in range(3):
                    S.dma("pool", wqr[:, c, :, 64:80], uqv[:, c, :, 80:96], dW, writes=[wqr.B])
                    S.dma("pool", wqr[:, c, :, 80:96], uqv[:, c, :, 64:80], dW, writes=[wqr.B])
                S.dma("pool", wpg[:], w_pool_group.rearrange("g c d -> c g d"), dW, writes=[wpg.B])


            with ExitStack() as st:
                stg = [T(st, "stg%d" % i, [128, 512], F32) for i in range(2)]
                wkv = T(st, "wkv", [128, 8, 320], BF16)
                wk = T(st, "wk", [128, 2, 512], BF16)
                wv = T(st, "wv", [128, 2, 512], BF16)
                xb = [T(st, "xb%d" % i, [128, 8, 512], BF16) for i in range(2)]
                sq = [T(st, "sq%d" % i, [128, 8, 512], BF16) for i in range(2)]
                r1 = [T(st, "r1%d" % i, [128, 512], F32) for i in range(2)]
                a1 = [T(st, "a1%d" % i, [128, 512], F32) for i in range(2)]
                sqr = [T(st, "sqr%d" % i, [128, 2, 512], BF16) for i in range(2)]
                tot = T(st, "tot", [128, 512], F32)
                ckvn = [T(st, "ckvn%d" % i, [128, 2, 512], BF16) for i in range(2)]
                kst = [T(st, "kst%d" % i, [128, 4, 512], BF16) for i in range(2)]
                vst = [T(st, "vst%d" % i, [128, 8, 4, 65], BF16) for i in range(2)]
                ct = [T(st, "ct%d" % i, [32, 512], F32) for i in range(2)]
                sn = [T(st, "sn%d" % i, [32, 512], F32) for i in range(2)]
                krs = [T(st, "krs%d" % i, [32, 512], BF16) for i in range(2)]
                t1 = T(st, "t1", [32, 512], F32)
                t2 = T(st, "t2", [32, 512], F32)
                dW = S.dsem(final=True)
                dX = [S.dsem(), S.dsem()]
                dTab = [S.dsem(), S.dsem()]
                dTabS = [S.dsem(), S.dsem()]
                dKs = [S.dsem(), S.dsem()]
                dVs = [S.dsem(), S.dsem()]
                dKr = [S.dsem(), S.dsem()]
                BscrK, BscrV, BscrR = Buf("scrK"), Buf("scrV"), Buf("scrR")

                load_folded(st, stg, lambda k: wkv[:, k, 0:288], lambda k: w_inv[:, k, 384:672], 288, G_PRE)
                load_folded(st, stg, lambda k: wkv[:, k, 288:304], lambda k: w_inv[:, k, 656:672], 16, G_PRE)
                load_folded(st, stg, lambda k: wkv[:, k, 304:320], lambda k: w_inv[:, k, 640:656], 16, G_PRE)
                wkv.B.writers = [S.q["dve"][-1]]
                ukv = w_ukv.rearrange("(k p) (h two d) -> p k h two d", p=128, two=2, d=64)
                for kc in range(2):
                    S.dma("pool", wk[:, kc, :].rearrange("p (h d) -> p h d", d=64), ukv[:, kc, :, 0, :], dW, writes=[wk.B])
                    S.dma("pool", wv[:, kc, :].rearrange("p (h d) -> p h d", d=64), ukv[:, kc, :, 1, :], dW, writes=[wv.B])
                for i in range(2):
                    S.op("pool", lambda e, i=i: e.memset(vst[i][:], 1.0), writes=[vst[i].B])

                NT = nta or (S_TOK // 512)

                ckr = T(st, "ckr", [128, 2, 512], F32)

                def stage1a(i):
                    p = i % 2
                    c0 = 512 * i
                    prep_tile(xb[p], sq[p], r1[p], a1[p], xTv[:, :, c0:c0 + 512], 512, dX[p])
                    S.dma("sp", ct[p][:], cosS[:, c0:c0 + 512], dTab[p], reads=[BtabS], writes=[ct[p].B])
                    S.dma("sp", sn[p][:], sinS[:, c0:c0 + 512], dTabS[p], reads=[BtabS], writes=[sn[p].B])

                def stage1b(i):
                    p = i % 2
                    c0 = 512 * i
                    for c in range(2):
                        pe([(bk(1 + c), wkv[:, k, 128 * c:128 * c + 128], xb[p][:, k, :], k == 0, k == 7) for k in range(8)],
                           [wkv.B, xb[p].B], [PB[1 + c]])
                        act(sqr[p][:, c, :], bk(1 + c), AF.Square, [PB[1 + c]], [sqr[p].B])
                        act(ckr[:, c, :], bk(1 + c), AF.Copy, [PB[1 + c]], [ckr.B])
                    pe([(bk(3)[0:32, :], wkv[:, k, 256:288], xb[p][:, k, :], k == 0, k == 7) for k in range(8)],
                       [wkv.B, xb[p].B], [PB[3]])
                    pe([(bk(4)[0:32, :], wkv[:, k, 288:320], xb[p][:, k, :], k == 0, k == 7) for k in range(8)],
                       [wkv.B, xb[p].B], [PB[4]])
                    tt(t1[:], bk(3)[0:32, :], ct[p][:], ALU.mult, [PB[3], ct[p].B], [t1.B])
                    tt(t2[:], bk(4)[0:32, :], sn[p][:], ALU.mult, [PB[4], sn[p].B], [t2.B])
                    tt(t1[:], t1[:], t2[:], ALU.add, [t1.B, t2.B], [t1.B])
                    tt(krs[p][:], t1[:], r1[p][0:32, :], ALU.mult, [t1.B, r1[p].B], [krs[p].B])
                    S.dma("sp", krS[:, c0:c0 + 512], krs[p][:], dKr[p], reads=[krs[p].B], writes=[BscrR])

                def stage1(i):
                    stage1a(i)
                    stage1b(i)

                def stage2(i):
                    p = i % 2
                    pe([(bk(0), ones[:], sqr[p][:, c, :], c == 0, c == 1) for c in range(2)],
                       [ones.B, sqr[p].B], [PB[0]])
                    tot_ops(tot[:], tot.B, bk(0), PB[0], a1[p][:], a1[p].B, 256)
                    for c in range(2):
                        stt(ckvn[p][:, c, :], ckr[:, c, :], gcol(G_KV, c), tot[:], ALU.mult, ALU.mult,
                            [ckr.B, gT.B, tot.B], [ckvn[p].B])

                rot = [0]

                def rb():
                    b = 5 + rot[0] % 3
                    rot[0] += 1
                    return b

                def stage3(i):
                    p = i % 2
                    c0 = 512 * i
                    for j in range(4):
                        b = rb()
                        pe([(bk(b), wk[:, kc, 128 * j:128 * j + 128], ckvn[p][:, kc, :], kc == 0, kc == 1) for kc in range(2)],
                           [wk.B, ckvn[p].B], [PB[b]])
                        act(kst[p][:, j, :], bk(b), AF.Copy, [PB[b]], [kst[p].B])
                    for s in range(4):
                        b = rb()
                        pe([(bk(b), ckvn[p][:, kc, 128 * s:128 * s + 128], wv[:, kc, :], kc == 0, kc == 1) for kc in range(2)],
                           [wv.B, ckvn[p].B], [PB[b]])
                        act(vst[p][:, :, s, 0:64], bk(b).rearrange("p (h d) -> p h d", d=64), AF.Copy, [PB[b]], [vst[p].B])
                    S.dma("sp", kSv[:, :, c0:c0 + 512], kst[p][:], dKs[p], reads=[kst[p].B], writes=[BscrK])
                    S.dma("sp", vSv[:, :, 4 * i:4 * i + 4, :], vst[p][:], dVs[p], reads=[vst[p].B], writes=[BscrV])

                def dbg_mark(nm):
                    if cut == nm:
                        S.barrier()
                        S.mark(nm)
                dbg_mark("Aw")
                stage1(0)
                load_q_weights()
                dbg_mark("A1")
                for i in range(NT):
                    if i + 1 < NT:
                        stage1a(i + 1)
                    stage2(i)
                    if i == 0:
                        dbg_mark("A2")
                    if i + 1 < NT:
                        stage1b(i + 1)
                    stage3(i)
                    if i == 0:
                        dbg_mark("A3")
                S.barrier()
                S.mark("A")

            with ExitStack() as st:
                stg = [T(st, "stgq%d" % i, [128, 512], F32) for i in range(2)]
                xb = T(st, "xbq", [128, 8, 512], BF16)
                sq = T(st, "sqq", [128, 8, 512], BF16)
                r1 = T(st, "r1q", [128, 512], F32)
                a1 = T(st, "a1q", [128, 512], F32)
                uT = T(st, "uT", [128, 4, OWN + 16], F32)
                st1 = ExitStack()
                cqraw = T(st1, "cqraw", [128, 3, 512], F32)
                sqcq = T(st1, "sqcq", [128, 3, 512], BF16)
                totq = T(st1, "totq", [128, 512], F32)
                cqn = T(st1, "cqn", [128, 3, 512], BF16)
                ctq = T(st1, "ctq", [128, 512], F32)
                snq = T(st1, "snq", [128, 512], F32)
                tq1 = [T(st1, "tq1%d" % i, [128, 512], F32) for i in range(2)]
                tq2 = [T(st1, "tq2%d" % i, [128, 512], F32) for i in range(2)]
                dW = S.dsem(final=True)
                dX = S.dsem()
                dTab = S.dsem()
                dTabS = S.dsem()
                dIc = S.dsem()

                rot = [0]

                def rb():
                    b = 1 + rot[0] % 7
                    rot[0] += 1
                    return b

                def u_part(n, dst_fns):
                    for g in range(4):
                        b = rb()
                        pe([(bk(b)[:, 0:n], wu[:, k, 128 * g:128 * g + 128], xb[:, k, 0:n], k == 0, k == 7) for k in range(8)],
                           [wu.B, xb.B], [PB[b]])
                        for (lo, hi, dfn) in dst_fns:
                            tt(dfn(g), bk(b)[:, lo:hi], r1[:, lo:hi], ALU.mult, [PB[b], r1.B], [uT.B])

                def dbg_markq(nm):
                    if cut == nm:
                        S.barrier()
                        S.mark(nm)
                dbg_markq("Qw")
                prep_tile(xb, sq, r1, a1, xhv, 16, dX, need_a=False)
                u_part(16, [(0, 8, lambda g: uT[:, g, 0:8]), (8, 16, lambda g: uT[:, g, OWN + 8:OWN + 16])])

                dbg_markq("Qh")
                for i in range(4):
                    if i == 1:
                        dbg_markq("Q0")
                    c0 = 512 * i
                    prep_tile(xb, sq, r1, a1, xTv[:, :, c0:c0 + 512], 512, dX)
                    S.dma("sp", ctq[64:96, :], cosS[:, c0:c0 + 512], dTab, reads=[BtabS], writes=[ctq.B])
                    S.dma("sp", snq[64:96, :], sinS[:, c0:c0 + 512], dTabS, reads=[BtabS], writes=[snq.B])
                    u_part(512, [(0, 512, lambda g, c0=c0: uT[:, g, 8 + c0:8 + c0 + 512])])
                    for c in range(3):
                        b = rb()
                        pe([(bk(b), wcq[:, k, 128 * c:128 * c + 128], xb[:, k, :], k == 0, k == 7) for k in range(8)],
                           [wcq.B, xb.B], [PB[b]])
                        act(sqcq[:, c, :], bk(b), AF.Square, [PB[b]], [sqcq.B])
                        dve(lambda e, b=b, c=c: e.tensor_copy(cqraw[:, c, :], bk(b)), [PB[b], sqcq.B], [cqraw.B])
                    b = rb()
                    pe([(bk(b), ones[:], sqcq[:, c, :], c == 0, c == 2) for c in range(3)], [ones.B, sqcq.B], [PB[b]])
                    tot_ops(totq[:], totq.B, bk(b), PB[b], a1[:], a1.B, 384)
                    for c in range(3):
                        stt(cqn[:, c, :], cqraw[:, c, :], gcol(G_Q, c), totq[:], ALU.mult, ALU.mult,
                            [cqraw.B, gT.B, totq.B], [cqn.B])
                    for h in range(8):
                        b1 = rb()
                        pe([(bk(b1)[0:96, :], wq[:, c, 96 * h:96 * h + 96], cqn[:, c, :], c == 0, c == 2) for c in range(3)],
                           [wq.B, cqn.B], [PB[b1]])
                        b2 = rb()
                        pe([(bk(b2)[0:96, :], wqr[:, c, h, :], cqn[:, c, :], c == 0, c == 2) for c in range(3)],
                           [wqr.B, cqn.B], [PB[b2]])
                        x1_, x2_ = tq1[h % 2], tq2[h % 2]
                        act(x1_[0:96, :], bk(b1)[0:96, :], AF.Copy, [PB[b1]], [x1_.B])
                        act(x2_[0:96, :], bk(b2)[0:96, :], AF.Copy, [PB[b2]], [x2_.B])
                        S.op("pool", lambda e, x1_=x1_, h=h, c0=c0: e.tensor_copy(QT[0:64, h, c0:c0 + 512], x1_[0:64, :]), [x1_.B], [QT.B])
                        tt(x1_[64:96, :], x1_[64:96, :], ctq[64:96, :], ALU.mult, [x1_.B, ctq.B], [x1_.B])
                        tt(x2_[64:96, :], x2_[64:96, :], snq[64:96, :], ALU.mult, [x2_.B, snq.B], [x2_.B])
                        tt(QT[64:96, h, c0:c0 + 512], x1_[64:96, :], x2_[64:96, :], ALU.add, [x1_.B, x2_.B], [QT.B])

                S.barrier()
                S.mark("Q1")
                st1.close()
                TA = T(st, "TA", [128, OWN + 16], F32)
                TB = T(st, "TB", [128, OWN + 16], F32)
                invc = T(st, "invc", [128, OWN], F32)
                pooled = T(st, "pooled", [128, 4, OWN], BF16)
                L = OWN + 16

                def sh(dst, src, lo, hi, d1, d2):
                    return lambda e: e.tensor_tensor(dst[:, lo:hi], src[:, lo + d1:hi + d1], src[:, lo + d2:hi + d2], ALU.add)

                for g in range(4):
                    ug = uT.t[:, g, :]
                    S.dma("sp", invc[:], bass.AP(invcnt.tensor, OWN * g, [[0, 128], [1, OWN]]), dIc, writes=[invc.B])
                    S.op("pool", sh(TA, ug, 1, L, -1, 0), [uT.B], [TA.B])
                    win = TA
                    if g >= 1:
                        S.op("pool", sh(TB, TA, 2, L - 1, -1, 1), [TA.B], [TB.B])
                        win = TB
                    if g >= 2:
                        S.op("pool", sh(TA, TB, 4, L - 3, -2, 2), [TB.B], [TA.B])
                        win = TA
                    if g >= 3:
                        S.op("pool", sh(TB, TA, 8, L - 7, -4, 4), [TA.B], [TB.B])
                        win = TB
                    other = TB if win is TA else TA
                    tt(other[:, 8:8 + OWN], win[:, 8:8 + OWN], invc[:], ALU.mult, [win.B, invc.B], [other.B])
                    tt(pooled[:, g, :], other[:, 8:8 + OWN], ug[:, 8:8 + OWN], ALU.subtract, [other.B, uT.B], [pooled.B])
                    for nt in range(4):
                        b = rb()
                        pe([(bk(b), wpg[:, g, :], pooled[:, g, 512 * nt:512 * nt + 512], True, True)], [wpg.B, pooled.B], [PB[b]])
                        act(PM[:, g, 512 * nt:512 * nt + 512], bk(b), AF.Identity, [PB[b], gT.B], [PM.B], scale=gcol(G_PS, g))
                if debug:
                    dD = S.dsem(final=True)
                    S.dma("sp", qtD[0:96], QT[0:96], dD, reads=[QT.B])
                    S.dma("sp", pmD, PM[:], dD, reads=[PM.B])
                S.barrier()
                S.mark("Q")
            qw.close()

            cw = ExitStack()
            stgc2 = [T(cw, "stgc2%d" % i, [128, 1024], F32, side="right") for i in range(2)]
            wg = T(cw, "wg", [128, 8, 2048], BF16, side="right")
            woa = T(cw, "woa", [64, 8, 1024], BF16, side="right")
            wob = T(cw, "wob", [128, 4, 1024], BF16, side="right")
            wout = T(cw, "wout", [128, 8, 1024], BF16, side="right")
            dWc = S.dsem(final=True)
            dStc = [S.dsem(), S.dsem()]

            def load_c1_weights():
                st = None
                stg = stgc2
                dW = dWc
                load_folded(st, stg, lambda k: wg[:, k, 0:1024], lambda k: w_inv[:, k, 1184:2208], 1024, G_PRE, q="pool", dsems=dStc)
                load_folded(st, stg, lambda k: wg[:, k, 1024:2048], lambda k: w_inv[:, k, 2208:3232], 1024, G_PRE, q="pool", dsems=dStc)
                wg.B.writers = [S.q["dve"][-1]]
                S.dma("pool", woa[:], w_o_attn.rearrange("(h d) n -> d h n", d=64), dW, writes=[woa.B])
                S.dma("pool", wob[:], w_o_pool.rearrange("(c p) n -> p c n", p=128), dW, writes=[wob.B])
                for k in range(8):
                    S.dma("pool", wout[:, k, :], w_out[128 * k:128 * k + 128, :], dW, writes=[wout.B])

            with ExitStack() as st:
                NSL = 4
                NSB = 3
                NPB = 4
                LOOK = 2
                kc = [T(st, "kc%d" % i, [128, 2048], BF16) for i in range(NSL)]
                vc = [T(st, "vc%d" % i, [128, 16, 65], BF16) for i in range(NSL)]
                Pb = [T(st, "P%d" % i, [128, 1024], BF16) for i in range(NPB)]
                rs = T(st, "rs", [128, 1024], F32)
                rc = [T(st, "rc%d" % i, [64, 512], F32) for i in range(2)]
                dSl = [S.dsem() for _ in range(NSL)]
                dSlV = [S.dsem() for _ in range(NSL)]
                scale = 96 ** -0.5
                chunks = [(h, qh, c) for h in range(8) for qh in range(2) for c in range(8)]
                if nchunks:
                    chunks = chunks[:nchunks]

                def load_chunk(n):
                    h, qh, c = chunks[n]
                    s = n % NSL
                    S.dma("sp", kc[s][0:64, :], kS[64 * h:64 * h + 64, 2048 * c:2048 * (c + 1)], dSl[s], reads=[BscrK], writes=[kc[s].B])
                    S.dma("sp", kc[s][64:96, :], krS[:, 2048 * c:2048 * (c + 1)], dSl[s], reads=[BscrR], writes=[kc[s].B])
                    S.dma("sp", vc[s][:], vS[h, :, 16 * c:16 * c + 16, :], dSlV[s], reads=[BscrV], writes=[vc[s].B])

                for n in range(min(3, len(chunks))):
                    load_chunk(n)
                load_c1_weights()
                step = [0]
                pend = []

                def emit_pv(pv):
                    (s, kt, pb, first, last) = pv
                    pe([(ps[0:65, 512 * q:512 * (q + 1)], vc[s][:, kt, :], Pb[pb][:, 512 * q:512 * q + 512], first, last)
                        for q in range(2)],
                       [vc[s].B, Pb[pb].B], [PB[0], PB[1]])

                for n, (h, qh, c) in enumerate(chunks):
                    s = n % NSL
                    q0 = 1024 * qh
                    for kt in range(16):
                        sb_ = step[0] % NSB
                        pb = step[0] % NPB
                        step[0] += 1
                        b0 = 2 + 2 * sb_
                        pe([(bk(b0 + q), kc[s][0:96, 128 * kt:128 * kt + 128], QT[0:96, h, q0 + 512 * q:q0 + 512 * q + 512], True, True)
                            for q in range(2)],
                           [kc[s].B, QT.B], [PB[b0], PB[b0 + 1]])
                        act(Pb[pb][:], ps[:, 512 * b0:512 * b0 + 1024], AF.Exp, [PB[b0], PB[b0 + 1]], [Pb[pb].B], scale=scale)
                        pend.append((s, kt, pb, (c == 0 and kt == 0), (c == 7 and kt == 15)))
                        if len(pend) > LOOK:
                            emit_pv(pend.pop(0))
                        if kt == LOOK and n + 3 < len(chunks):
                            load_chunk(n + 3)
                    if c == 7:
                        while pend:
                            emit_pv(pend.pop(0))
                        for qb in range(2):
                            act(rs[64:65, 512 * qb:512 * qb + 512], bk(qb)[64:65, :], AF.Copy, [PB[qb]], [rs.B])
                            pe([(bk(2 + qb)[0:64, :], onesf[64:65, 0:64], rs[64:65, 512 * qb:512 * qb + 512], True, True)],
                               [onesf.B, rs.B], [PB[2 + qb]])
                            r_ = rc[qb % 2]
                            dve(lambda e, r_=r_, qb=qb: e.reciprocal(r_[:], bk(2 + qb)[0:64, :]), [PB[2 + qb]], [r_.B])
                            tt(AT[0:64, h, q0 + 512 * qb:q0 + 512 * qb + 512], bk(qb)[0:64, :], r_[:], ALU.mult, [PB[qb], r_.B], [AT.B])
                if debug:
                    dD2 = S.dsem(final=True)
                    S.dma("sp", atD[0:64], AT[0:64], dD2, reads=[AT.B])
                S.barrier()
                S.mark("B")

            qsc.close()
            with ExitStack() as st:
                xb = T(st, "xbc", [128, 8, 512], BF16)
                sq = T(st, "sqc", [128, 8, 512], BF16)
                xf = T(st, "xfc", [128, 8, 512], F32)
                r1 = T(st, "r1c", [128, 512], F32)
                r2 = T(st, "r2c", [128, 512], F32)
                zA = [T(st, "zA%d" % i, [128, 512], F32) for i in range(2)]
                zB = [T(st, "zB%d" % i, [128, 512], F32) for i in range(2)]
                m = T(st, "m", [128, 8, 512], BF16)
                y = T(st, "y", [128, 8, 512], F32)
                sqy = [T(st, "sqy%d" % i, [128, 512], BF16) for i in range(2)]
                dW = S.dsem(final=True)
                dX = S.dsem()
                dXf = S.dsem()
                dO = S.dsem()
                Bx1S = Buf("x1S")
                for i in range(4):
                    c0 = 512 * i
                    S.dma("sp", xf[:], xTv[:, :, c0:c0 + 512], dXf, writes=[xf.B])
                    prep_tile(xb, sq, r1, None, xTv[:, :, c0:c0 + 512], 512, dX, need_a=False)
                    for j in range(8):
                        bA, bB, ba, bb = (1, 2, 3, 4) if j % 2 == 0 else (5, 6, 7, 4)
                        z1, z2 = zA[j % 2], zB[j % 2]
                        pe([(bk(bA), wg[:, k, 128 * j:128 * j + 128], xb[:, k, :], k == 0, k == 7) for k in range(8)],
                           [wg.B, xb.B], [PB[bA]])
                        pe([(bk(bB), wg[:, k, 1024 + 128 * j:1024 + 128 * j + 128], xb[:, k, :], k == 0, k == 7) for k in range(8)],
                           [wg.B, xb.B], [PB[bB]])
                        pe([(bk(ba), woa[0:64, h, 128 * j:128 * j + 128], AT[0:64, h, c0:c0 + 512], h == 0, h == 7) for h in range(8)],
                           [woa.B, AT.B], [PB[ba]])
                        tt(z1[:], bk(bA), r1[:], ALU.mult, [PB[bA], r1.B], [z1.B])
                        act(z1[:], z1[:], AF.Sigmoid, [z1.B], [z1.B])
                        tt(z1[:], z1[:], bk(ba), ALU.mult, [z1.B, PB[ba]], [z1.B])
                        pe([(bk(bb), wob[:, c, 128 * j:128 * j + 128], PM[:, c, c0:c0 + 512], c == 0, c == 3) for c in range(4)],
                           [wob.B, PM.B], [PB[bb]])
                        tt(z2[:], bk(bB), r1[:], ALU.mult, [PB[bB], r1.B], [z2.B])
                        act(z2[:], z2[:], AF.Sigmoid, [z2.B], [z2.B])
                        tt(z2[:], z2[:], bk(bb), ALU.mult, [z2.B, PB[bb]], [z2.B])
                        tt(m[:, j, :], z1[:], z2[:], ALU.add, [z1.B, z2.B], [m.B])
                    for j in range(8):
                        b = 1 + j % 6
                        pe([(bk(b), wout[:, k, 128 * j:128 * j + 128], m[:, k, :], k == 0, k == 7) for k in range(8)],
                           [wout.B, m.B], [PB[b]])
                        act(y[:, j, :], bk(b), AF.Copy, [PB[b]], [y.B])
                        sy = sqy[j % 2]
                        act(sy[:], bk(b), AF.Square, [PB[b]], [sy.B])
                        pe([(bk(7), ones[:], sy[:], j == 0, j == 7)], [ones.B, sy.B], [PB[7]])
                    rstd_ops(r2[:], r2.B, bk(7), PB[7], 1.0 / 1024)
                    for j in range(8):
                        stt(y[:, j, :], y[:, j, :], gcol(G_POST, j), r2[:], ALU.mult, ALU.mult, [y.B, gT.B, r2.B], [y.B])
                        tt(y[:, j, :], y[:, j, :], xf[:, j, :], ALU.add, [y.B, xf.B], [y.B])
                    S.dma("sp", x1Sv[:, :, c0:c0 + 512], y[:], dO, reads=[y.B], writes=[Bx1S])
                S.barrier()
                S.mark("C1")
            cw.close()

        with ExitStack() as st:
            x1h = T(st, "x1h", [128, 8, 1024], F32)
            hfb = T(st, "hfb", [128, 8, 1024], BF16)
            actT = T(st, "actT", [128, 22, 1024], BF16)
            y2 = T(st, "y2", [128, 8, 1024], F32)
            wgu = [T(st, "wgu%d" % i, [128, 8, 256], BF16) for i in range(3)]
            wd = [T(st, "wd%d" % i, [128, 22, 128], BF16) for i in range(2)]
            sqc = [T(st, "sqf%d" % i, [128, 512], BF16) for i in range(2)]
            sg = [T(st, "sg%d" % i, [128, 512], F32) for i in range(2)]
            r3 = T(st, "r3", [128, 1024], F32)
            r4 = T(st, "r4", [128, 1024], F32)
            dX1 = S.dsem()
            dGU = [S.dsem() for _ in range(3)]
            dWD = [S.dsem() for _ in range(2)]
            dOut = S.dsem()
            wguv = w_gate_up.rearrange("(k p) n -> p k n", p=128)
            wdv = w_down.rearrange("(k p) n -> p k n", p=128)

            def load_gu(hf, j):
                s = (hf * 22 + j) % 3
                S.dma("pool", wgu[s][:, :, 0:128], wguv[:, :, 128 * j:128 * j + 128], dGU[s], writes=[wgu[s].B])
                S.dma("pool", wgu[s][:, :, 128:256], wguv[:, :, 2816 + 128 * j:2816 + 128 * j + 128], dGU[s], writes=[wgu[s].B])

            def load_wd(hf, i):
                s = (hf * 8 + i) % 2
                S.dma("pool", wd[s][:], wdv[:, :, 128 * i:128 * i + 128], dWD[s], writes=[wd[s].B])

            for hf in range(2):
                h0 = 1024 * hf
                S.dma("sp", x1h[:], x1Sv[:, :, h0:h0 + 1024], dX1, reads=[Bx1S], writes=[x1h.B])
                load_gu(hf, 0)
                load_gu(hf, 1)
                for nt in range(2):
                    n0 = 512 * nt
                    for k in range(8):
                        sc_ = sqc[k % 2]
                        act(sc_[:], x1h[:, k, n0:n0 + 512], AF.Square, [x1h.B], [sc_.B])
                        pe([(bk(6 + nt), ones[:], sc_[:], k == 0, k == 7)], [ones.B, sc_.B], [PB[6 + nt]])
                    rstd_ops(r3[:, n0:n0 + 512], r3.B, bk(6 + nt), PB[6 + nt], 1.0 / 1024)
                    for k in range(8):
                        stt(hfb[:, k, n0:n0 + 512], x1h[:, k, n0:n0 + 512], gcol(G_FPRE, k), r3[:, n0:n0 + 512],
                            ALU.mult, ALU.mult, [x1h.B, gT.B, r3.B], [hfb.B])
                cnt = 0
                for j in range(22):
                    if j + 2 < 22:
                        load_gu(hf, j + 2)
                    if j == 20:
                        load_wd(hf, 0)
                    if j == 21:
                        load_wd(hf, 1)
                    s = (hf * 22 + j) % 3
                    for nt in range(2):
                        n0 = 512 * nt
                        bg, bu = (0, 1) if cnt % 2 == 0 else (2, 3)
                        sg_ = sg[cnt % 2]
                        cnt += 1
                        pe([(bk(bg), wgu[s][:, k, 0:128], hfb[:, k, n0:n0 + 512], k == 0, k == 7) for k in range(8)],
                           [wgu[s].B, hfb.B], [PB[bg]])
                        pe([(bk(bu), wgu[s][:, k, 128:256], hfb[:, k, n0:n0 + 512], k == 0, k == 7) for k in range(8)],
                           [wgu[s].B, hfb.B], [PB[bu]])
                        act(sg_[:], bk(bg), AF.Silu, [PB[bg]], [sg_.B])
                        tt(actT[:, j, n0:n0 + 512], sg_[:], bk(bu), ALU.mult, [sg_.B, PB[bu]], [actT.B])
                cnt = 0
                for i in range(8):
                    s = (hf * 8 + i) % 2
                    for nt in range(2):
                        n0 = 512 * nt
                        b = cnt % 4
                        sc_ = sqc[cnt % 2]
                        cnt += 1
                        pe([(bk(b), wd[s][:, k, :], actT[:, k, n0:n0 + 512], k == 0, k == 21) for k in range(22)],
                           [wd[s].B, actT.B], [PB[b]])
                        act(y2[:, i, n0:n0 + 512], bk(b), AF.Copy, [PB[b]], [y2.B])
                        act(sc_[:], bk(b), AF.Square, [PB[b]], [sc_.B])
                        pe([(bk(4 + nt), ones[:], sc_[:], i == 0, i == 7)], [ones.B, sc_.B], [PB[4 + nt]])
                    if i + 2 < 8:
                        load_wd(hf, i + 2)
                for nt in range(2):
                    n0 = 512 * nt
                    rstd_ops(r4[:, n0:n0 + 512], r4.B, bk(4 + nt), PB[4 + nt], 1.0 / 1024)
                    for i in range(8):
                        stt(y2[:, i, n0:n0 + 512], y2[:, i, n0:n0 + 512], gcol(G_FPOST, i), r4[:, n0:n0 + 512],
                            ALU.mult, ALU.mult, [y2.B, gT.B, r4.B], [y2.B])
                        tt(y2[:, i, n0:n0 + 512], y2[:, i, n0:n0 + 512], x1h[:, i, n0:n0 + 512], ALU.add, [y2.B, x1h.B], [y2.B])
                S.dma("sp", outTv[:, :, h0:h0 + 1024], y2[:], dOut, reads=[y2.B])
            S.barrier()
            S.mark("C2")

        if cut:
            S.cut(cut)
        S.emit(None)
        S.check()
        with nc.Block() as block:
            @block.tensor
            def _(e):
                S.play("pe", e)

            @block.scalar
            def _(e):
                S.play("act", e)

            @block.vector
            def _(e):
                S.play("dve", e)

            @block.gpsimd
            def _(e):
                S.play("pool", e)

            @block.sync
            def _(e):
                S.play("sp", e)
    return nc


def make_inputs(inputs, core):
    x = np.asarray(inputs["x"], np.float32)[0]
    pos = np.asarray(inputs["positions"], np.int32)[0]
    o0 = core * OWN
    xr = np.roll(x, -o0, axis=0)
    xTc = np.ascontiguousarray(xr.T)
    posr = np.ascontiguousarray(np.roll(pos, -o0)[None, :])
    xh = np.zeros((16, 1024), np.float32)
    if o0 - 8 >= 0:
        xh[0:8] = x[o0 - 8:o0]
    if o0 + OWN + 8 <= S_TOK:
        xh[8:16] = x[o0 + OWN:o0 + OWN + 8]
    xhT = np.ascontiguousarray(xh.T)
    return xTc, xhT, posr


_CONSTS = {}


def const_tables(core):
    if core in _CONSTS:
        return _CONSTS[core]
    inv = (1.0 / (10000.0 ** (np.arange(0, 32, 2, dtype=np.float32) / np.float32(32)))).astype(np.float32)
    cst = np.zeros((128, 4), np.float32)
    for p in range(128):
        cst[p, 0] = inv[p % 16]
        cst[p, 1] = -1.0 if (p % 32) < 16 else 1.0
        cst[p, 2] = 0.0
        cst[p, 3] = EPS
    t = np.arange(core * OWN, (core + 1) * OWN)
    invcnt = np.zeros((4, OWN), np.float32)
    for g, w in enumerate(POOL_W):
        left = w // 2
        right = w - left - 1
        cnt = (np.minimum(t + right, S_TOK - 1) - np.maximum(t - left, 0) + 1).astype(np.float32)
        invcnt[g] = (1.0 / cnt).astype(np.float32)
    _CONSTS[core] = (cst, invcnt)
    return _CONSTS[core]


def chunk_cols(v):
    v = np.asarray(v, np.float32)
    return np.ascontiguousarray(v.reshape(-1, 128).T)


_NC = {}


def kernel(x, positions, g_mix_pre, w_in, g_q_lat, w_uq, g_kv_lat, w_ukv, w_o_attn,
           w_pool_group, pool_scale, w_o_pool, w_out, g_mix_post, g_ffn_pre,
           w_gate_up, w_down, g_ffn_post, _debug=False):
    inputs = {"x": x, "positions": positions}
    gains = np.concatenate([chunk_cols(g_mix_pre), chunk_cols(g_q_lat), chunk_cols(g_kv_lat),
                            chunk_cols(pool_scale), chunk_cols(g_mix_post), chunk_cols(g_ffn_pre),
                            chunk_cols(g_ffn_post)], axis=1)
    gains = np.ascontiguousarray(gains, dtype=np.float32)
    assert gains.shape == (128, 41)
    f = lambda a: np.ascontiguousarray(np.asarray(a, np.float32))
    common = dict(gains=gains, w_in=f(w_in), w_uq=f(w_uq), w_ukv=f(w_ukv), w_o_attn=f(w_o_attn),
                  w_pool_group=f(w_pool_group), w_o_pool=f(w_o_pool), w_out=f(w_out),
                  w_gate_up=f(w_gate_up), w_down=f(w_down))
    in_maps = []
    for c in range(NCORES):
        xTc, xhT, posr = make_inputs(inputs, c)
        cst, invcnt = const_tables(c)
        m = dict(common)
        m.update(xT=xTc, xh=xhT, pos=posr, cst=cst, invcnt=invcnt)
        in_maps.append(m)
    key = bool(_debug)
    if key not in _NC:
        _NC[key] = build_program(debug=key)
    nc = _NC[key]
    res = run_bass_kernel_spmd(nc, in_maps, core_ids=list(range(NCORES)))
    if _debug:
        return res
    outT = np.concatenate([np.asarray(r["outT"]) for r in res.results], axis=1)
    return np.ascontiguousarray(outT.T)[None, :, :].astype(np.float32)
```

```python
import math
from contextlib import ExitStack

import numpy as np
import concourse.bass as bass
import concourse.mybir as mybir
from concourse.bass_utils import run_bass_kernel_spmd

ENGS = ("pe", "act", "dve", "pool", "sp")


class Buf:
    __slots__ = ("name", "writers", "readers")

    def __init__(self, name):
        self.name = name
        self.writers = []
        self.readers = []


class DSem:
    __slots__ = ("sem", "count", "final")

    def __init__(self, sem, final=False):
        self.sem = sem
        self.count = 0
        self.final = final


class Op:
    __slots__ = ("eng", "fn", "deps", "signal", "sigval", "dsem", "dval", "idx")

    def __init__(self, eng, fn):
        self.eng = eng
        self.fn = fn
        self.deps = []
        self.signal = False
        self.sigval = None
        self.dsem = None
        self.dval = None


class Sched:
    def __init__(self, nc, stack):
        self.nc = nc
        self.stack = stack
        self.q = {e: [] for e in ENGS}
        self.esem = {e: stack.enter_context(nc.semaphore("es_" + e)) for e in ENGS}
        self.dmas = []
        self.marks = {}
        self.nds = 0

    def mark(self, name):
        self.marks[name] = {e: len(self.q[e]) for e in ENGS}

    def cut(self, name):
        for e in ENGS:
            del self.q[e][self.marks[name][e]:]

    def dsem(self, final=False):
        self.nds += 1
        return DSem(self.stack.enter_context(self.nc.semaphore("ds%d" % self.nds)), final)

    def _track(self, op, reads, writes, group_dma_writes=False):
        deps = set()
        for b in reads:
            for w in b.writers:
                deps.add((w, True))
        for b in writes:
            for w in b.writers:
                if group_dma_writes and w.dsem is not None and not b.readers and w.dsem is op.dsem:
                    continue
                deps.add((w, False))
            for r in b.readers:
                deps.add((r, False))
        for d, raw in deps:
            if d is op:
                continue
            if d.dsem is None and op.dsem is None and d.eng == op.eng:
                if not (raw and op.eng in ("act", "dve", "pool")):
                    continue
            op.deps.append(d)
            if d.dsem is None:
                d.signal = True
        for b in reads:
            b.readers.append(op)
        for b in writes:
            if group_dma_writes and b.writers and all(
                    w.dsem is not None and w.dsem is op.dsem for w in b.writers) and not b.readers:
                b.writers.append(op)
            else:
                b.writers = [op]
                b.readers = []

    def op(self, eng, fn, reads=(), writes=()):
        o = Op(eng, fn)
        self._track(o, reads, writes)
        self.q[eng].append(o)
        return o

    def dma(self, eng, out, in_, dsem, reads=(), writes=()):
        def fn(e, out=out, in_=in_):
            return e.dma_start(out=out, in_=in_)
        o = Op(eng, fn)
        o.dsem = dsem
        dsem.count += 16
        o.dval = dsem.count
        self._track(o, reads, writes, group_dma_writes=True)
        self.q[eng].append(o)
        self.dmas.append(o)
        return o

    def barrier(self):
        lasts = []
        for e in ENGS:
            for o in reversed(self.q[e]):
                if o.dsem is None and o.fn is not None:
                    o.signal = True
                    lasts.append(o)
                    break
        dm = {}
        for o in self.dmas:
            dm[id(o.dsem)] = o
        self.dmas = []
        for e in ENGS:
            def fn(eng):
                return None
            b = Op(e, None)
            for l in lasts:
                if l.eng != e:
                    b.deps.append(l)
            b.deps.extend(dm.values())
            self.q[e].append(b)

    def emit(self, engines):
        for e in ENGS:
            c = 0
            for o in self.q[e]:
                if o.dsem is None and o.signal and o.fn is not None:
                    c += 1
                    o.sigval = c

    def _waits(self, o):
        waits = {}
        for d in o.deps:
            if d.dsem is not None:
                key = id(d.dsem)
                val = d.dsem.count if d.dsem.final else d.dval
                sem = d.dsem.sem
            else:
                key = d.eng
                val = d.sigval
                sem = self.esem[d.eng]
            assert val is not None, (o.eng, d.eng, d.fn)
            if key not in waits or waits[key][1] < val:
                waits[key] = (sem, val)
        return waits

    def check(self):
        cnt = {}
        ptr = {e: 0 for e in ENGS}
        total = sum(len(self.q[e]) for e in ENGS)
        done = 0
        while done < total:
            prog = False
            for e in ENGS:
                while ptr[e] < len(self.q[e]):
                    o = self.q[e][ptr[e]]
                    ok = all(cnt.get(k, 0) >= v for k, (s_, v) in self._waits(o).items())
                    if not ok:
                        break
                    if o.fn is not None:
                        if o.dsem is not None:
                            cnt[id(o.dsem)] = cnt.get(id(o.dsem), 0) + 16
                        elif o.signal:
                            cnt[e] = cnt.get(e, 0) + 1
                    ptr[e] += 1
                    done += 1
                    prog = True
            if not prog:
                st = {e: (ptr[e], len(self.q[e])) for e in ENGS}
                raise RuntimeError("schedule deadlock: %s" % st)
        return True

    def play(self, e, eng):
        seen = {}
        last_real = None
        for o in self.q[e]:
            waits = {}
            for d in o.deps:
                if d.dsem is not None:
                    key = id(d.dsem)
                    val = d.dsem.count if d.dsem.final else d.dval
                    sem = d.dsem.sem
                else:
                    key = d.eng
                    val = d.sigval
                    sem = self.esem[d.eng]
                if key not in waits or waits[key][1] < val:
                    waits[key] = (sem, val)
            for key, (sem, val) in waits.items():
                if seen.get(key, 0) >= val:
                    continue
                seen[key] = val
                eng.wait_ge(sem, val)
            if o.fn is None:
                continue
            ins = o.fn(eng)
            if o.dsem is not None:
                ins.then_inc(o.dsem.sem, 16)
            elif o.signal:
                ins.then_inc(self.esem[e], 1)


F32 = mybir.dt.float32
BF16 = mybir.dt.bfloat16
I32 = mybir.dt.int32
AF = mybir.ActivationFunctionType
ALU = mybir.AluOpType

S_TOK = 16384
OWN = 2048
NCORES = 8
EPS = 1e-6
G_PRE, G_Q, G_KV, G_PS, G_POST, G_FPRE, G_FPOST = 0, 8, 11, 13, 17, 25, 33
POOL_W = (2, 4, 8, 16)


def build_program(debug=False, cut=None, nta=None, nchunks=None):
    nc = bass.Bass("TRN2", target_bir_lowering=False)

    def din(name, shape, dt=F32):
        return nc.dram_tensor(name, shape, dt, kind="ExternalInput").ap()

    xT = din("xT", [1024, S_TOK])
    xh = din("xh", [1024, 16])
    pos = din("pos", [1, S_TOK], I32)
    cst = din("cst", [128, 4])
    gains = din("gains", [128, 41])
    invcnt = din("invcnt", [4, OWN])
    w_in = din("w_in", [1024, 3232])
    w_uq = din("w_uq", [384, 768])
    w_ukv = din("w_ukv", [256, 1024])
    w_o_attn = din("w_o_attn", [512, 1024])
    w_pool_group = din("w_pool_group", [4, 128, 128])
    w_o_pool = din("w_o_pool", [512, 1024])
    w_out = din("w_out", [1024, 1024])
    w_gate_up = din("w_gate_up", [1024, 5632])
    w_down = din("w_down", [2816, 1024])
    outT = nc.dram_tensor("outT", [1024, OWN], F32, kind="ExternalOutput").ap()
    skind = dict(kind="ExternalOutput") if debug else {}
    kS = nc.dram_tensor("kS", [512, S_TOK], BF16, **skind).ap()
    krS = nc.dram_tensor("krS", [32, S_TOK], BF16, **skind).ap()
    vS = nc.dram_tensor("vS", [8, 128, 128, 65], BF16, **skind).ap()
    cosS = nc.dram_tensor("cosS", [32, S_TOK], F32, **skind).ap()
    sinS = nc.dram_tensor("sinS", [32, S_TOK], F32, **skind).ap()
    x1S = nc.dram_tensor("x1S", [1024, OWN], F32, **skind).ap()
    if debug:
        qtD = nc.dram_tensor("qtD", [128, 8, OWN], BF16, kind="ExternalOutput").ap()
        atD = nc.dram_tensor("atD", [128, 8, OWN], BF16, kind="ExternalOutput").ap()
        pmD = nc.dram_tensor("pmD", [128, 4, OWN], BF16, kind="ExternalOutput").ap()

    xTv = xT.rearrange("(k p) n -> p k n", p=128)
    xhv = xh.rearrange("(k p) n -> p k n", p=128)
    w_inv = w_in.rearrange("(k p) n -> p k n", p=128)
    outTv = outT.rearrange("(k p) n -> p k n", p=128)
    x1Sv = x1S.rearrange("(k p) n -> p k n", p=128)
    kSv = kS.rearrange("(j q) t -> q j t", q=128)
    vSv = vS.rearrange("h p t d -> p h t d")

    with ExitStack() as gst:
        S = Sched(nc, gst)
        ps = gst.enter_context(nc.psum_tensor("ps", [128, 4096], F32))
        PB = [Buf("pb%d" % b) for b in range(8)]

        def bk(b):
            return ps[:, 512 * b:512 * (b + 1)]

        class T:
            def __init__(self, st, name, shape, dt, nb=1, side=None):
                if side:
                    self.t = st.enter_context(nc.sbuf_tensor("sb_" + name, shape, dt, side=side))
                else:
                    self.t = st.enter_context(nc.sbuf_tensor("sb_" + name, shape, dt))
                self.b = [Buf(name + str(i)) for i in range(nb)]

            def __getitem__(self, k):
                return self.t[k]

            @property
            def B(self):
                return self.b[0]

        def pe(mms, reads, writes):
            def fn(e, mms=mms):
                ins = None
                for (o, l, r, st_, sp_) in mms:
                    ins = e.matmul(o, l, r, start=st_, stop=sp_)
                return ins
            return S.op("pe", fn, reads, writes)

        def act(out, in_, func, reads, writes, scale=1.0, bias=0.0):
            return S.op("act", lambda e: e.activation(out, in_, func, bias=bias, scale=scale), reads, writes)

        def dve(fn, reads, writes):
            return S.op("dve", fn, reads, writes)

        def tt(out, a, b, op, reads, writes, eng="dve"):
            return S.op(eng, lambda e: e.tensor_tensor(out, a, b, op), reads, writes)

        def stt(out, in0, sc, in1, op0, op1, reads, writes):
            return S.op("dve", lambda e: e.scalar_tensor_tensor(out, in0, sc, in1, op0, op1), reads, writes)

        def ts(out, in0, s1, s2, op0, op1, reads, writes):
            if s2 is None:
                return S.op("dve", lambda e: e.tensor_scalar(out, in0, s1, None, op0), reads, writes)
            return S.op("dve", lambda e: e.tensor_scalar(out, in0, s1, s2, op0, op1), reads, writes)

        cstT = T(gst, "cst", [128, 4], F32)
        gT = T(gst, "gains", [128, 41], F32)
        ones = T(gst, "ones", [128, 128], BF16)
        onesf = T(gst, "onesf", [128, 128], F32)
        dC = S.dsem(final=True)
        S.dma("sp", cstT[:], cst, dC, writes=[cstT.B])
        S.dma("sp", gT[:], gains, dC, writes=[gT.B])
        S.op("pool", lambda e: e.memset(ones[:], 1.0), writes=[ones.B])
        S.op("pool", lambda e: e.memset(onesf[:], 1.0), writes=[onesf.B])
        epsb = cstT[:, 3:4]

        def gcol(off, k, p0=0, p1=128):
            return gT[p0:p1, off + k:off + k + 1]

        def rstd_ops(out_ap, out_B, ss_ap, ss_B, sc):
            act(out_ap, ss_ap, AF.Sqrt, [ss_B, cstT.B], [out_B], scale=sc, bias=epsb)
            dve(lambda e: e.reciprocal(out_ap, out_ap), [out_B], [out_B])

        def tot_ops(out_ap, out_B, ss_ap, ss_B, a_ap, a_B, nf):
            stt(out_ap, ss_ap, 1.0 / nf, a_ap, ALU.mult, ALU.add, [ss_B, a_B], [out_B])
            act(out_ap, out_ap, AF.Sqrt, [out_B], [out_B])
            dve(lambda e: e.reciprocal(out_ap, out_ap), [out_B], [out_B])

        dStage = [S.dsem(), S.dsem()]
        stage_ctr = [0]

        def load_folded(st, stg, dst_fn, src_fn, ncols, goff, q="sp", dsems=None):
            dsems = dsems or dStage
            for k in range(8):
                i = stage_ctr[0] % 2
                stage_ctr[0] += 1
                S.dma(q, stg[i][:, 0:ncols], src_fn(k), dsems[i], writes=[stg[i].B])
                ts(dst_fn(k), stg[i][:, 0:ncols], gcol(goff, k), None, ALU.mult, None,
                   [stg[i].B, gT.B], [])

        with ExitStack() as st:
            I = T(st, "ti", [128, 4096], I32)
            X = T(st, "tx", [128, 4096], F32)
            Y = T(st, "ty", [128, 4096], F32)
            Z = T(st, "tz", [128, 4096], F32)
            W = T(st, "tw", [128, 4096], F32)
            d0 = S.dsem(final=True)
            for s in range(4):
                S.dma("sp", I[32 * s:32 * s + 32, :],
                      bass.AP(pos.tensor, 4096 * s, [[0, 32], [1, 4096]]),
                      d0, writes=[I.B])
            C1 = 6.28125
            C2 = 2 * math.pi - C1
            dve(lambda e: e.tensor_copy(X[:], I[:]), [I.B], [X.B])
            ts(X[:], X[:], cstT[:, 0:1], None, ALU.mult, None, [X.B, cstT.B], [X.B])
            ts(I[:], X[:], 1.0 / (2 * math.pi), None, ALU.mult, None, [X.B], [I.B])
            dve(lambda e: e.tensor_copy(Y[:], I[:]), [I.B], [Y.B])
            stt(Z[:], Y[:], -C1, X[:], ALU.mult, ALU.add, [Y.B, X.B], [Z.B])
            stt(Z[:], Y[:], -C2, Z[:], ALU.mult, ALU.add, [Y.B, Z.B], [Z.B])
            ts(Y[:], Z[:], math.pi, -2 * math.pi, ALU.is_gt, ALU.mult, [Z.B], [Y.B])
            tt(Z[:], Z[:], Y[:], ALU.add, [Z.B, Y.B], [Z.B])
            act(X[:], Z[:], AF.Sin, [Z.B, cstT.B], [X.B], scale=cstT[:, 1:2])
            dT = S.dsem(final=True)
            BtabS = Buf("tabS")
            for s in range(4):
                S.dma("sp", sinS[:, 4096 * s:4096 * (s + 1)], X[32 * s:32 * s + 32, :], dT, reads=[X.B], writes=[BtabS])
            ts(Y[:], Z[:], math.pi / 2, None, ALU.add, None, [Z.B], [Y.B])
            ts(W[:], Y[:], math.pi, -2 * math.pi, ALU.is_gt, ALU.mult, [Y.B], [W.B])
            tt(Y[:], Y[:], W[:], ALU.add, [Y.B, W.B], [Y.B])
            act(W[:], Y[:], AF.Sin, [Y.B], [W.B])
            for s in range(4):
                S.dma("sp", cosS[:, 4096 * s:4096 * (s + 1)], W[32 * s:32 * s + 32, :], dT, reads=[W.B], writes=[BtabS])
            S.barrier()
            S.mark("p0")

        def prep_tile(xb, sq, r1, a1, src_ap, n, dX, need_a=True):
            S.dma("pool", xb[:, :, 0:n], src_ap, dX, writes=[xb.B])
            act(sq[:, :, 0:n], xb[:, :, 0:n], AF.Square, [xb.B], [sq.B])
            pe([(bk(0)[:, 0:n], ones[:], sq[:, k, 0:n], k == 0, k == 7) for k in range(8)],
               [sq.B, ones.B], [PB[0]])
            rstd_ops(r1[:, 0:n], r1.B, bk(0)[:, 0:n], PB[0], 1.0 / 1024)
            if need_a:
                ts(a1[:, 0:n], bk(0)[:, 0:n], EPS / 1024, EPS * EPS, ALU.mult, ALU.add, [PB[0], r1.B], [a1.B])

        with ExitStack() as bq:
            AT = T(bq, "AT", [128, 8, OWN], BF16)
            PM = T(bq, "PM", [128, 4, OWN], BF16)
            qsc = ExitStack()
            QT = T(qsc, "QT", [128, 8, OWN], BF16)

            qw = ExitStack()
            stgq2 = [T(qw, "stgq2%d" % i, [128, 512], F32, side="right") for i in range(2)]
            wcq = T(qw, "wcq", [128, 8, 384], BF16, side="right")
            wu = T(qw, "wu", [128, 8, 512], BF16, side="right")
            wq = T(qw, "wq", [128, 3, 768], BF16, side="right")
            wqr = T(qw, "wqr", [128, 3, 8, 96], BF16, side="right")
            wpg = T(qw, "wpg", [128, 4, 128], BF16, side="right")
            dWq = S.dsem(final=True)
            dStq = [S.dsem(), S.dsem()]

            def load_q_weights():
                st = None
                stg = stgq2
                dW = dWq
                load_folded(st, stg, lambda k: wcq[:, k, :], lambda k: w_inv[:, k, 0:384], 384, G_PRE, dsems=dStq)
                wcq.B.writers = [S.q["dve"][-1]]
                load_folded(st, stg, lambda k: wu[:, k, :], lambda k: w_inv[:, k, 672:1184], 512, G_PRE, dsems=dStq)
                wu.B.writers = [S.q["dve"][-1]]
                S.dma("pool", wq[:], w_uq.rearrange("(k p) n -> p k n", p=128), dW, writes=[wq.B])
                S.op("pool", lambda e: e.memset(wqr[:], 0.0), writes=[wqr.B])
                uqv = w_uq.rearrange("(k p) (h d) -> p k h d", p=128, d=96)
                for c in range(3):
                    S.dma("pool", wqr[:, c, :, 64:80], uqv[:, c, :, 80:96], dW, writes=[wqr.B])
                    S.dma("pool", wqr[:, c, :, 80:96], uqv[:, c, :, 64:80], dW, writes=[wqr.B])
                S.dma("pool", wpg[:], w_pool_group.rearrange("g c d -> c g d"), dW, writes=[wpg.B])


            with ExitStack() as st:
                stg = [T(st, "stg%d" % i, [128, 512], F32) for i in range(2)]
                wkv = T(st, "wkv", [128, 8, 320], BF16)
                wk = T(st, "wk", [128, 2, 512], BF16)
                wv = T(st, "wv", [128, 2, 512], BF16)
                xb = [T(st, "xb%d" % i, [128, 8, 512], BF16) for i in range(2)]
                sq = [T(st, "sq%d" % i, [128, 8, 512], BF16) for i in range(2)]
                r1 = [T(st, "r1%d" % i, [128, 512], F32) for i in range(2)]
                a1 = [T(st, "a1%d" % i, [128, 512], F32) for i in range(2)]
                sqr = [T(st, "sqr%d" % i, [128, 2, 512], BF16) for i in range(2)]
                tot = T(st, "tot", [128, 512], F32)
                ckvn = [T(st, "ckvn%d" % i, [128, 2, 512], BF16) for i in range(2)]
                kst = [T(st, "kst%d" % i, [128, 4, 512], BF16) for i in range(2)]
                vst = [T(st, "vst%d" % i, [128, 8, 4, 65], BF16) for i in range(2)]
                ct = [T(st, "ct%d" % i, [32, 512], F32) for i in range(2)]
                sn = [T(st, "sn%d" % i, [32, 512], F32) for i in range(2)]
                krs = [T(st, "krs%d" % i, [32, 512], BF16) for i in range(2)]
                t1 = T(st, "t1", [32, 512], F32)
                t2 = T(st, "t2", [32, 512], F32)
                dW = S.dsem(final=True)
                dX = [S.dsem(), S.dsem()]
                dTab = [S.dsem(), S.dsem()]
                dTabS = [S.dsem(), S.dsem()]
                dKs = [S.dsem(), S.dsem()]
                dVs = [S.dsem(), S.dsem()]
                dKr = [S.dsem(), S.dsem()]
                BscrK, BscrV, BscrR = Buf("scrK"), Buf("scrV"), Buf("scrR")

                load_folded(st, stg, lambda k: wkv[:, k, 0:288], lambda k: w_inv[:, k, 384:672], 288, G_PRE)
                load_folded(st, stg, lambda k: wkv[:, k, 288:304], lambda k: w_inv[:, k, 656:672], 16, G_PRE)
                load_folded(st, stg, lambda k: wkv[:, k, 304:320], lambda k: w_inv[:, k, 640:656], 16, G_PRE)
                wkv.B.writers = [S.q["dve"][-1]]
                ukv = w_ukv.rearrange("(k p) (h two d) -> p k h two d", p=128, two=2, d=64)
                for kc in range(2):
                    S.dma("pool", wk[:, kc, :].rearrange("p (h d) -> p h d", d=64), ukv[:, kc, :, 0, :], dW, writes=[wk.B])
                    S.dma("pool", wv[:, kc, :].rearrange("p (h d) -> p h d", d=64), ukv[:, kc, :, 1, :], dW, writes=[wv.B])
                for i in range(2):
                    S.op("pool", lambda e, i=i: e.memset(vst[i][:], 1.0), writes=[vst[i].B])

                NT = nta or (S_TOK // 512)

                def pre(i):
                    p = i % 2
                    c0 = 512 * i
                    S.dma("pool", xb[p][:], xTv[:, :, c0:c0 + 512], dX[p], writes=[xb[p].B])
                    act(sq[p][:], xb[p][:], AF.Square, [xb[p].B], [sq[p].B])

                def stage1(i):
                    p = i % 2
                    c0 = 512 * i
                    pe([(bk(0), ones[:], sq[p][:, k, :], k == 0, k == 7) for k in range(8)],
                       [sq[p].B, ones.B], [PB[0]])
                    rstd_ops(r1[p][:], r1[p].B, bk(0), PB[0], 1.0 / 1024)
                    ts(a1[p][:], bk(0), EPS / 1024, EPS * EPS, ALU.mult, ALU.add, [PB[0], r1[p].B], [a1[p].B])
                    S.dma("sp", ct[p][:], cosS[:, c0:c0 + 512], dTab[p], reads=[BtabS], writes=[ct[p].B])
                    S.dma("sp", sn[p][:], sinS[:, c0:c0 + 512], dTabS[p], reads=[BtabS], writes=[sn[p].B])
                    for c in range(2):
                        pe([(bk(1 + c), wkv[:, k, 128 * c:128 * c + 128], xb[p][:, k, :], k == 0, k == 7) for k in range(8)],
                           [wkv.B, xb[p].B], [PB[1 + c]])
                        act(sqr[p][:, c, :], bk(1 + c), AF.Square, [PB[1 + c]], [sqr[p].B])
                    pe([(bk(3)[0:32, :], wkv[:, k, 256:288], xb[p][:, k, :], k == 0, k == 7) for k in range(8)],
                       [wkv.B, xb[p].B], [PB[3]])
                    pe([(bk(4)[0:32, :], wkv[:, k, 288:320], xb[p][:, k, :], k == 0, k == 7) for k in range(8)],
                       [wkv.B, xb[p].B], [PB[4]])
                    tt(t1[:], bk(3)[0:32, :], ct[p][:], ALU.mult, [PB[3], ct[p].B], [t1.B])
                    tt(t2[:], bk(4)[0:32, :], sn[p][:], ALU.mult, [PB[4], sn[p].B], [t2.B])
                    tt(t1[:], t1[:], t2[:], ALU.add, [t1.B, t2.B], [t1.B])
                    tt(krs[p][:], t1[:], r1[p][0:32, :], ALU.mult, [t1.B, r1[p].B], [krs[p].B])
                    S.dma("sp", krS[:, c0:c0 + 512], krs[p][:], dKr[p], reads=[krs[p].B], writes=[BscrR])

                def stage2(i):
                    p = i % 2
                    pe([(bk(0), ones[:], sqr[p][:, c, :], c == 0, c == 1) for c in range(2)],
                       [ones.B, sqr[p].B], [PB[0]])
                    tot_ops(tot[:], tot.B, bk(0), PB[0], a1[p][:], a1[p].B, 256)
                    for c in range(2):
                        stt(ckvn[p][:, c, :], bk(1 + c), gcol(G_KV, c), tot[:], ALU.mult, ALU.mult,
                            [PB[1 + c], gT.B, tot.B], [ckvn[p].B])

                rot = [0]

                def rb():
                    b = 5 + rot[0] % 3
                    rot[0] += 1
                    return b

                def stage3(i):
                    p = i % 2
                    c0 = 512 * i
                    for j in range(4):
                        b = rb()
                        pe([(bk(b), wk[:, kc, 128 * j:128 * j + 128], ckvn[p][:, kc, :], kc == 0, kc == 1) for kc in range(2)],
                           [wk.B, ckvn[p].B], [PB[b]])
                        act(kst[p][:, j, :], bk(b), AF.Copy, [PB[b]], [kst[p].B])
                    for s in range(4):
                        b = rb()
                        pe([(bk(b), ckvn[p][:, kc, 128 * s:128 * s + 128], wv[:, kc, :], kc == 0, kc == 1) for kc in range(2)],
                           [wv.B, ckvn[p].B], [PB[b]])
                        act(vst[p][:, :, s, 0:64], bk(b).rearrange("p (h d) -> p h d", d=64), AF.Copy, [PB[b]], [vst[p].B])
                    S.dma("sp", kSv[:, :, c0:c0 + 512], kst[p][:], dKs[p], reads=[kst[p].B], writes=[BscrK])
                    S.dma("sp", vSv[:, :, 4 * i:4 * i + 4, :], vst[p][:], dVs[p], reads=[vst[p].B], writes=[BscrV])

                def dbg_mark(nm):
                    if cut == nm:
                        S.barrier()
                        S.mark(nm)
                dbg_mark("Aw")
                pre(0)
                if NT > 1:
                    pre(1)
                stage1(0)
                load_q_weights()
                dbg_mark("A1")
                for i in range(NT):
                    stage2(i)
                    if i == 0:
                        dbg_mark("A2")
                    if i + 1 < NT:
                        stage1(i + 1)
                    if i + 2 < NT:
                        pre(i + 2)
                    stage3(i)
                    if i == 0:
                        dbg_mark("A3")
                S.barrier()
                S.mark("A")

            with ExitStack() as st:
                stg = [T(st, "stgq%d" % i, [128, 512], F32) for i in range(2)]
                xb = T(st, "xbq", [128, 8, 512], BF16)
                sq = T(st, "sqq", [128, 8, 512], BF16)
                r1 = T(st, "r1q", [128, 512], F32)
                a1 = T(st, "a1q", [128, 512], F32)
                uT = T(st, "uT", [128, 4, OWN + 16], F32)
                st1 = ExitStack()
                cqraw = T(st1, "cqraw", [128, 3, 512], F32)
                sqcq = T(st1, "sqcq", [128, 3, 512], BF16)
                totq = T(st1, "totq", [128, 512], F32)
                cqn = T(st1, "cqn", [128, 3, 512], BF16)
                ctq = T(st1, "ctq", [128, 512], F32)
                snq = T(st1, "snq", [128, 512], F32)
                tq1 = [T(st1, "tq1%d" % i, [128, 512], F32) for i in range(2)]
                tq2 = [T(st1, "tq2%d" % i, [128, 512], F32) for i in range(2)]
                dW = S.dsem(final=True)
                dX = S.dsem()
                dTab = S.dsem()
                dTabS = S.dsem()
                dIc = S.dsem()

                rot = [0]

                def rb():
                    b = 1 + rot[0] % 7
                    rot[0] += 1
                    return b

                def u_part(n, dst_fns):
                    for g in range(4):
                        b = rb()
                        pe([(bk(b)[:, 0:n], wu[:, k, 128 * g:128 * g + 128], xb[:, k, 0:n], k == 0, k == 7) for k in range(8)],
                           [wu.B, xb.B], [PB[b]])
                        for (lo, hi, dfn) in dst_fns:
                            tt(dfn(g), bk(b)[:, lo:hi], r1[:, lo:hi], ALU.mult, [PB[b], r1.B], [uT.B])

                def dbg_markq(nm):
                    if cut == nm:
                        S.barrier()
                        S.mark(nm)
                dbg_markq("Qw")
                prep_tile(xb, sq, r1, a1, xhv, 16, dX, need_a=False)
                u_part(16, [(0, 8, lambda g: uT[:, g, 0:8]), (8, 16, lambda g: uT[:, g, OWN + 8:OWN + 16])])

                dbg_markq("Qh")
                for i in range(4):
                    if i == 1:
                        dbg_markq("Q0")
                    c0 = 512 * i
                    prep_tile(xb, sq, r1, a1, xTv[:, :, c0:c0 + 512], 512, dX)
                    S.dma("sp", ctq[64:96, :], cosS[:, c0:c0 + 512], dTab, reads=[BtabS], writes=[ctq.B])
                    S.dma("sp", snq[64:96, :], sinS[:, c0:c0 + 512], dTabS, reads=[BtabS], writes=[snq.B])
                    u_part(512, [(0, 512, lambda g, c0=c0: uT[:, g, 8 + c0:8 + c0 + 512])])
                    for c in range(3):
                        b = rb()
                        pe([(bk(b), wcq[:, k, 128 * c:128 * c + 128], xb[:, k, :], k == 0, k == 7) for k in range(8)],
                           [wcq.B, xb.B], [PB[b]])
                        act(sqcq[:, c, :], bk(b), AF.Square, [PB[b]], [sqcq.B])
                        dve(lambda e, b=b, c=c: e.tensor_copy(cqraw[:, c, :], bk(b)), [PB[b], sqcq.B], [cqraw.B])
                    b = rb()
                    pe([(bk(b), ones[:], sqcq[:, c, :], c == 0, c == 2) for c in range(3)], [ones.B, sqcq.B], [PB[b]])
                    tot_ops(totq[:], totq.B, bk(b), PB[b], a1[:], a1.B, 384)
                    for c in range(3):
                        stt(cqn[:, c, :], cqraw[:, c, :], gcol(G_Q, c), totq[:], ALU.mult, ALU.mult,
                            [cqraw.B, gT.B, totq.B], [cqn.B])
                    for h in range(8):
                        b1 = rb()
                        pe([(bk(b1)[0:96, :], wq[:, c, 96 * h:96 * h + 96], cqn[:, c, :], c == 0, c == 2) for c in range(3)],
                           [wq.B, cqn.B], [PB[b1]])
                        b2 = rb()
                        pe([(bk(b2)[0:96, :], wqr[:, c, h, :], cqn[:, c, :], c == 0, c == 2) for c in range(3)],
                           [wqr.B, cqn.B], [PB[b2]])
                        x1_, x2_ = tq1[h % 2], tq2[h % 2]
                        act(x1_[0:96, :], bk(b1)[0:96, :], AF.Copy, [PB[b1]], [x1_.B])
                        act(x2_[0:96, :], bk(b2)[0:96, :], AF.Copy, [PB[b2]], [x2_.B])
                        S.op("pool", lambda e, x1_=x1_, h=h, c0=c0: e.tensor_copy(QT[0:64, h, c0:c0 + 512], x1_[0:64, :]), [x1_.B], [QT.B])
                        tt(x1_[64:96, :], x1_[64:96, :], ctq[64:96, :], ALU.mult, [x1_.B, ctq.B], [x1_.B])
                        tt(x2_[64:96, :], x2_[64:96, :], snq[64:96, :], ALU.mult, [x2_.B, snq.B], [x2_.B])
                        tt(QT[64:96, h, c0:c0 + 512], x1_[64:96, :], x2_[64:96, :], ALU.add, [x1_.B, x2_.B], [QT.B])

                S.barrier()
                S.mark("Q1")
                st1.close()
                TA = T(st, "TA", [128, OWN + 16], F32)
                TB = T(st, "TB", [128, OWN + 16], F32)
                invc = T(st, "invc", [128, OWN], F32)
                pooled = T(st, "pooled", [128, 4, OWN], BF16)
                L = OWN + 16

                def sh(dst, src, lo, hi, d1, d2):
                    return lambda e: e.tensor_tensor(dst[:, lo:hi], src[:, lo + d1:hi + d1], src[:, lo + d2:hi + d2], ALU.add)

                for g in range(4):
                    ug = uT.t[:, g, :]
                    S.dma("sp", invc[:], bass.AP(invcnt.tensor, OWN * g, [[0, 128], [1, OWN]]), dIc, writes=[invc.B])
                    S.op("pool", sh(TA, ug, 1, L, -1, 0), [uT.B], [TA.B])
                    win = TA
                    if g >= 1:
                        S.op("pool", sh(TB, TA, 2, L - 1, -1, 1), [TA.B], [TB.B])
                        win = TB
                    if g >= 2:
                        S.op("pool", sh(TA, TB, 4, L - 3, -2, 2), [TB.B], [TA.B])
                        win = TA
                    if g >= 3:
                        S.op("pool", sh(TB, TA, 8, L - 7, -4, 4), [TA.B], [TB.B])
                        win = TB
                    other = TB if win is TA else TA
                    tt(other[:, 8:8 + OWN], win[:, 8:8 + OWN], invc[:], ALU.mult, [win.B, invc.B], [other.B])
                    tt(pooled[:, g, :], other[:, 8:8 + OWN], ug[:, 8:8 + OWN], ALU.subtract, [other.B, uT.B], [pooled.B])
                    for nt in range(4):
                        b = rb()
                        pe([(bk(b), wpg[:, g, :], pooled[:, g, 512 * nt:512 * nt + 512], True, True)], [wpg.B, pooled.B], [PB[b]])
                        act(PM[:, g, 512 * nt:512 * nt + 512], bk(b), AF.Identity, [PB[b], gT.B], [PM.B], scale=gcol(G_PS, g))
                if debug:
                    dD = S.dsem(final=True)
                    S.dma("sp", qtD[0:96], QT[0:96], dD, reads=[QT.B])
                    S.dma("sp", pmD, PM[:], dD, reads=[PM.B])
                S.barrier()
                S.mark("Q")
            qw.close()

            cw = ExitStack()
            stgc2 = [T(cw, "stgc2%d" % i, [128, 1024], F32, side="right") for i in range(2)]
            wg = T(cw, "wg", [128, 8, 2048], BF16, side="right")
            woa = T(cw, "woa", [64, 8, 1024], BF16, side="right")
            wob = T(cw, "wob", [128, 4, 1024], BF16, side="right")
            wout = T(cw, "wout", [128, 8, 1024], BF16, side="right")
            dWc = S.dsem(final=True)
            dStc = [S.dsem(), S.dsem()]

            def load_c1_weights():
                st = None
                stg = stgc2
                dW = dWc
                load_folded(st, stg, lambda k: wg[:, k, 0:1024], lambda k: w_inv[:, k, 1184:2208], 1024, G_PRE, q="pool", dsems=dStc)
                load_folded(st, stg, lambda k: wg[:, k, 1024:2048], lambda k: w_inv[:, k, 2208:3232], 1024, G_PRE, q="pool", dsems=dStc)
                wg.B.writers = [S.q["dve"][-1]]
                S.dma("pool", woa[:], w_o_attn.rearrange("(h d) n -> d h n", d=64), dW, writes=[woa.B])
                S.dma("pool", wob[:], w_o_pool.rearrange("(c p) n -> p c n", p=128), dW, writes=[wob.B])
                for k in range(8):
                    S.dma("pool", wout[:, k, :], w_out[128 * k:128 * k + 128, :], dW, writes=[wout.B])

            with ExitStack() as st:
                NSL = 4
                NSB = 3
                NPB = 4
                LOOK = 2
                kc = [T(st, "kc%d" % i, [128, 2048], BF16) for i in range(NSL)]
                vc = [T(st, "vc%d" % i, [128, 16, 65], BF16) for i in range(NSL)]
                Pb = [T(st, "P%d" % i, [128, 1024], BF16) for i in range(NPB)]
                rs = T(st, "rs", [128, 1024], F32)
                rc = [T(st, "rc%d" % i, [64, 512], F32) for i in range(2)]
                dSl = [S.dsem() for _ in range(NSL)]
                dSlV = [S.dsem() for _ in range(NSL)]
                scale = 96 ** -0.5
                chunks = [(h, qh, c) for h in range(8) for qh in range(2) for c in range(8)]
                if nchunks:
                    chunks = chunks[:nchunks]

                def load_chunk(n):
                    h, qh, c = chunks[n]
                    s = n % NSL
                    S.dma("sp", kc[s][0:64, :], kS[64 * h:64 * h + 64, 2048 * c:2048 * (c + 1)], dSl[s], reads=[BscrK], writes=[kc[s].B])
                    S.dma("sp", kc[s][64:96, :], krS[:, 2048 * c:2048 * (c + 1)], dSl[s], reads=[BscrR], writes=[kc[s].B])
                    S.dma("sp", vc[s][:], vS[h, :, 16 * c:16 * c + 16, :], dSlV[s], reads=[BscrV], writes=[vc[s].B])

                for n in range(min(3, len(chunks))):
                    load_chunk(n)
                load_c1_weights()
                step = [0]
                pend = []

                def emit_pv(pv):
                    (s, kt, pb, first, last) = pv
                    pe([(ps[0:65, 512 * q:512 * (q + 1)], vc[s][:, kt, :], Pb[pb][:, 512 * q:512 * q + 512], first, last)
                        for q in range(2)],
                       [vc[s].B, Pb[pb].B], [PB[0], PB[1]])

                for n, (h, qh, c) in enumerate(chunks):
                    s = n % NSL
                    q0 = 1024 * qh
                    for kt in range(16):
                        sb_ = step[0] % NSB
                        pb = step[0] % NPB
                        step[0] += 1
                        b0 = 2 + 2 * sb_
                        pe([(bk(b0 + q), kc[s][0:96, 128 * kt:128 * kt + 128], QT[0:96, h, q0 + 512 * q:q0 + 512 * q + 512], True, True)
                            for q in range(2)],
                           [kc[s].B, QT.B], [PB[b0], PB[b0 + 1]])
                        act(Pb[pb][:], ps[:, 512 * b0:512 * b0 + 1024], AF.Exp, [PB[b0], PB[b0 + 1]], [Pb[pb].B], scale=scale)
                        pend.append((s, kt, pb, (c == 0 and kt == 0), (c == 7 and kt == 15)))
                        if len(pend) > LOOK:
                            emit_pv(pend.pop(0))
                        if kt == LOOK and n + 3 < len(chunks):
                            load_chunk(n + 3)
                    if c == 7:
                        while pend:
                            emit_pv(pend.pop(0))
                        for qb in range(2):
                            act(rs[64:65, 512 * qb:512 * qb + 512], bk(qb)[64:65, :], AF.Copy, [PB[qb]], [rs.B])
                            pe([(bk(2 + qb)[0:64, :], onesf[64:65, 0:64], rs[64:65, 512 * qb:512 * qb + 512], True, True)],
                               [onesf.B, rs.B], [PB[2 + qb]])
                            r_ = rc[qb % 2]
                            dve(lambda e, r_=r_, qb=qb: e.reciprocal(r_[:], bk(2 + qb)[0:64, :]), [PB[2 + qb]], [r_.B])
                            tt(AT[0:64, h, q0 + 512 * qb:q0 + 512 * qb + 512], bk(qb)[0:64, :], r_[:], ALU.mult, [PB[qb], r_.B], [AT.B])
                if debug:
                    dD2 = S.dsem(final=True)
                    S.dma("sp", atD[0:64], AT[0:64], dD2, reads=[AT.B])
                S.barrier()
                S.mark("B")

            qsc.close()
            with ExitStack() as st:
                xb = T(st, "xbc", [128, 8, 512], BF16)
                sq = T(st, "sqc", [128, 8, 512], BF16)
                xf = T(st, "xfc", [128, 8, 512], F32)
                r1 = T(st, "r1c", [128, 512], F32)
                r2 = T(st, "r2c", [128, 512], F32)
                zA = [T(st, "zA%d" % i, [128, 512], F32) for i in range(2)]
                zB = [T(st, "zB%d" % i, [128, 512], F32) for i in range(2)]
                m = T(st, "m", [128, 8, 512], BF16)
                y = T(st, "y", [128, 8, 512], F32)
                sqy = [T(st, "sqy%d" % i, [128, 512], BF16) for i in range(2)]
                dW = S.dsem(final=True)
                dX = S.dsem()
                dXf = S.dsem()
                dO = S.dsem()
                Bx1S = Buf("x1S")
                for i in range(4):
                    c0 = 512 * i
                    S.dma("sp", xf[:], xTv[:, :, c0:c0 + 512], dXf, writes=[xf.B])
                    prep_tile(xb, sq, r1, None, xTv[:, :, c0:c0 + 512], 512, dX, need_a=False)
                    for j in range(8):
                        bA, bB, ba, bb = (1, 2, 3, 4) if j % 2 == 0 else (5, 6, 7, 4)
                        z1, z2 = zA[j % 2], zB[j % 2]
                        pe([(bk(bA), wg[:, k, 128 * j:128 * j + 128], xb[:, k, :], k == 0, k == 7) for k in range(8)],
                           [wg.B, xb.B], [PB[bA]])
                        pe([(bk(bB), wg[:, k, 1024 + 128 * j:1024 + 128 * j + 128], xb[:, k, :], k == 0, k == 7) for k in range(8)],
                           [wg.B, xb.B], [PB[bB]])
                        pe([(bk(ba), woa[0:64, h, 128 * j:128 * j + 128], AT[0:64, h, c0:c0 + 512], h == 0, h == 7) for h in range(8)],
                           [woa.B, AT.B], [PB[ba]])
                        tt(z1[:], bk(bA), r1[:], ALU.mult, [PB[bA], r1.B], [z1.B])
                        act(z1[:], z1[:], AF.Sigmoid, [z1.B], [z1.B])
                        tt(z1[:], z1[:], bk(ba), ALU.mult, [z1.B, PB[ba]], [z1.B])
                        pe([(bk(bb), wob[:, c, 128 * j:128 * j + 128], PM[:, c, c0:c0 + 512], c == 0, c == 3) for c in range(4)],
                           [wob.B, PM.B], [PB[bb]])
                        tt(z2[:], bk(bB), r1[:], ALU.mult, [PB[bB], r1.B], [z2.B])
                        act(z2[:], z2[:], AF.Sigmoid, [z2.B], [z2.B])
                        tt(z2[:], z2[:], bk(bb), ALU.mult, [z2.B, PB[bb]], [z2.B])
                        tt(m[:, j, :], z1[:], z2[:], ALU.add, [z1.B, z2.B], [m.B])
                    for j in range(8):
                        b = 1 + j % 6
                        pe([(bk(b), wout[:, k, 128 * j:128 * j + 128], m[:, k, :], k == 0, k == 7) for k in range(8)],
                           [wout.B, m.B], [PB[b]])
                        act(y[:, j, :], bk(b), AF.Copy, [PB[b]], [y.B])
                        sy = sqy[j % 2]
                        act(sy[:], bk(b), AF.Square, [PB[b]], [sy.B])
                        pe([(bk(7), ones[:], sy[:], j == 0, j == 7)], [ones.B, sy.B], [PB[7]])
                    rstd_ops(r2[:], r2.B, bk(7), PB[7], 1.0 / 1024)
                    for j in range(8):
                        stt(y[:, j, :], y[:, j, :], gcol(G_POST, j), r2[:], ALU.mult, ALU.mult, [y.B, gT.B, r2.B], [y.B])
                        tt(y[:, j, :], y[:, j, :], xf[:, j, :], ALU.add, [y.B, xf.B], [y.B])
                    S.dma("sp", x1Sv[:, :, c0:c0 + 512], y[:], dO, reads=[y.B], writes=[Bx1S])
                S.barrier()
                S.mark("C1")
            cw.close()

        with ExitStack() as st:
            x1h = T(st, "x1h", [128, 8, 1024], F32)
            hfb = T(st, "hfb", [128, 8, 1024], BF16)
            actT = T(st, "actT", [128, 22, 1024], BF16)
            y2 = T(st, "y2", [128, 8, 1024], F32)
            wgu = [T(st, "wgu%d" % i, [128, 8, 256], BF16) for i in range(3)]
            wd = [T(st, "wd%d" % i, [128, 22, 128], BF16) for i in range(2)]
            sqc = [T(st, "sqf%d" % i, [128, 512], BF16) for i in range(2)]
            sg = [T(st, "sg%d" % i, [128, 512], F32) for i in range(2)]
            r3 = T(st, "r3", [128, 1024], F32)
            r4 = T(st, "r4", [128, 1024], F32)
            dX1 = S.dsem()
            dGU = [S.dsem() for _ in range(3)]
            dWD = [S.dsem() for _ in range(2)]
            dOut = S.dsem()
            wguv = w_gate_up.rearrange("(k p) n -> p k n", p=128)
            wdv = w_down.rearrange("(k p) n -> p k n", p=128)

            def load_gu(hf, j):
                s = (hf * 22 + j) % 3
                S.dma("pool", wgu[s][:, :, 0:128], wguv[:, :, 128 * j:128 * j + 128], dGU[s], writes=[wgu[s].B])
                S.dma("pool", wgu[s][:, :, 128:256], wguv[:, :, 2816 + 128 * j:2816 + 128 * j + 128], dGU[s], writes=[wgu[s].B])

            def load_wd(hf, i):
                s = (hf * 8 + i) % 2
                S.dma("pool", wd[s][:], wdv[:, :, 128 * i:128 * i + 128], dWD[s], writes=[wd[s].B])

            for hf in range(2):
                h0 = 1024 * hf
                S.dma("sp", x1h[:], x1Sv[:, :, h0:h0 + 1024], dX1, reads=[Bx1S], writes=[x1h.B])
                load_gu(hf, 0)
                load_gu(hf, 1)
                for nt in range(2):
                    n0 = 512 * nt
                    for k in range(8):
                        sc_ = sqc[k % 2]
                        act(sc_[:], x1h[:, k, n0:n0 + 512], AF.Square, [x1h.B], [sc_.B])
                        pe([(bk(6 + nt), ones[:], sc_[:], k == 0, k == 7)], [ones.B, sc_.B], [PB[6 + nt]])
                    rstd_ops(r3[:, n0:n0 + 512], r3.B, bk(6 + nt), PB[6 + nt], 1.0 / 1024)
                    for k in range(8):
                        stt(hfb[:, k, n0:n0 + 512], x1h[:, k, n0:n0 + 512], gcol(G_FPRE, k), r3[:, n0:n0 + 512],
                            ALU.mult, ALU.mult, [x1h.B, gT.B, r3.B], [hfb.B])
                cnt = 0
                for j in range(22):
                    if j + 2 < 22:
                        load_gu(hf, j + 2)
                    if j == 20:
                        load_wd(hf, 0)
                    if j == 21:
                        load_wd(hf, 1)
                    s = (hf * 22 + j) % 3
                    for nt in range(2):
                        n0 = 512 * nt
                        bg, bu = (0, 1) if cnt % 2 == 0 else (2, 3)
                        sg_ = sg[cnt % 2]
                        cnt += 1
                        pe([(bk(bg), wgu[s][:, k, 0:128], hfb[:, k, n0:n0 + 512], k == 0, k == 7) for k in range(8)],
                           [wgu[s].B, hfb.B], [PB[bg]])
                        pe([(bk(bu), wgu[s][:, k, 128:256], hfb[:, k, n0:n0 + 512], k == 0, k == 7) for k in range(8)],
                           [wgu[s].B, hfb.B], [PB[bu]])
                        act(sg_[:], bk(bg), AF.Silu, [PB[bg]], [sg_.B])
                        tt(actT[:, j, n0:n0 + 512], sg_[:], bk(bu), ALU.mult, [sg_.B, PB[bu]], [actT.B])
                cnt = 0
                for i in range(8):
                    s = (hf * 8 + i) % 2
                    for nt in range(2):
                        n0 = 512 * nt
                        b = cnt % 4
                        sc_ = sqc[cnt % 2]
                        cnt += 1
                        pe([(bk(b), wd[s][:, k, :], actT[:, k, n0:n0 + 512], k == 0, k == 21) for k in range(22)],
                           [wd[s].B, actT.B], [PB[b]])
                        act(y2[:, i, n0:n0 + 512], bk(b), AF.Copy, [PB[b]], [y2.B])
                        act(sc_[:], bk(b), AF.Square, [PB[b]], [sc_.B])
                        pe([(bk(4 + nt), ones[:], sc_[:], i == 0, i == 7)], [ones.B, sc_.B], [PB[4 + nt]])
                    if i + 2 < 8:
                        load_wd(hf, i + 2)
                for nt in range(2):
                    n0 = 512 * nt
                    rstd_ops(r4[:, n0:n0 + 512], r4.B, bk(4 + nt), PB[4 + nt], 1.0 / 1024)
                    for i in range(8):
                        stt(y2[:, i, n0:n0 + 512], y2[:, i, n0:n0 + 512], gcol(G_FPOST, i), r4[:, n0:n0 + 512],
                            ALU.mult, ALU.mult, [y2.B, gT.B, r4.B], [y2.B])
                        tt(y2[:, i, n0:n0 + 512], y2[:, i, n0:n0 + 512], x1h[:, i, n0:n0 + 512], ALU.add, [y2.B, x1h.B], [y2.B])
                S.dma("sp", outTv[:, :, h0:h0 + 1024], y2[:], dOut, reads=[y2.B])
            S.barrier()
            S.mark("C2")

        if cut:
            S.cut(cut)
        S.emit(None)
        S.check()
        with nc.Block() as block:
            @block.tensor
            def _(e):
                S.play("pe", e)

            @block.scalar
            def _(e):
                S.play("act", e)

            @block.vector
            def _(e):
                S.play("dve", e)

            @block.gpsimd
            def _(e):
                S.play("pool", e)

            @block.sync
            def _(e):
                S.play("sp", e)
    return nc


def make_inputs(inputs, core):
    x = np.asarray(inputs["x"], np.float32)[0]
    pos = np.asarray(inputs["positions"], np.int32)[0]
    o0 = core * OWN
    xr = np.roll(x, -o0, axis=0)
    xTc = np.ascontiguousarray(xr.T)
    posr = np.ascontiguousarray(np.roll(pos, -o0)[None, :])
    xh = np.zeros((16, 1024), np.float32)
    if o0 - 8 >= 0:
        xh[0:8] = x[o0 - 8:o0]
    if o0 + OWN + 8 <= S_TOK:
        xh[8:16] = x[o0 + OWN:o0 + OWN + 8]
    xhT = np.ascontiguousarray(xh.T)
    return xTc, xhT, posr


_CONSTS = {}


def const_tables(core):
    if core in _CONSTS:
        return _CONSTS[core]
    inv = (1.0 / (10000.0 ** (np.arange(0, 32, 2, dtype=np.float32) / np.float32(32)))).astype(np.float32)
    cst = np.zeros((128, 4), np.float32)
    for p in range(128):
        cst[p, 0] = inv[p % 16]
        cst[p, 1] = -1.0 if (p % 32) < 16 else 1.0
        cst[p, 2] = 0.0
        cst[p, 3] = EPS
    t = np.arange(core * OWN, (core + 1) * OWN)
    invcnt = np.zeros((4, OWN), np.float32)
    for g, w in enumerate(POOL_W):
        left = w // 2
        right = w - left - 1
        cnt = (np.minimum(t + right, S_TOK - 1) - np.maximum(t - left, 0) + 1).astype(np.float32)
        invcnt[g] = (1.0 / cnt).astype(np.float32)
    _CONSTS[core] = (cst, invcnt)
    return _CONSTS[core]


def chunk_cols(v):
    v = np.asarray(v, np.float32)
    return np.ascontiguousarray(v.reshape(-1, 128).T)


_NC = {}


def kernel(x, positions, g_mix_pre, w_in, g_q_lat, w_uq, g_kv_lat, w_ukv, w_o_attn,
           w_pool_group, pool_scale, w_o_pool, w_out, g_mix_post, g_ffn_pre,
           w_gate_up, w_down, g_ffn_post, _debug=False):
    inputs = {"x": x, "positions": positions}
    gains = np.concatenate([chunk_cols(g_mix_pre), chunk_cols(g_q_lat), chunk_cols(g_kv_lat),
                            chunk_cols(pool_scale), chunk_cols(g_mix_post), chunk_cols(g_ffn_pre),
                            chunk_cols(g_ffn_post)], axis=1)
    gains = np.ascontiguousarray(gains, dtype=np.float32)
    assert gains.shape == (128, 41)
    f = lambda a: np.ascontiguousarray(np.asarray(a, np.float32))
    common = dict(gains=gains, w_in=f(w_in), w_uq=f(w_uq), w_ukv=f(w_ukv), w_o_attn=f(w_o_attn),
                  w_pool_group=f(w_pool_group), w_o_pool=f(w_o_pool), w_out=f(w_out),
                  w_gate_up=f(w_gate_up), w_down=f(w_down))
    in_maps = []
    for c in range(NCORES):
        xTc, xhT, posr = make_inputs(inputs, c)
        cst, invcnt = const_tables(c)
        m = dict(common)
        m.update(xT=xTc, xh=xhT, pos=posr, cst=cst, invcnt=invcnt)
        in_maps.append(m)
    key = bool(_debug)
    if key not in _NC:
        _NC[key] = build_program(debug=key)
    nc = _NC[key]
    res = run_bass_kernel_spmd(nc, in_maps, core_ids=list(range(NCORES)))
    if _debug:
        return res
    outT = np.concatenate([np.asarray(r["outT"]) for r in res.results], axis=1)
    return np.ascontiguousarray(outT.T)[None, :, :].astype(np.float32)
```

```python
import math
from contextlib import ExitStack

import numpy as np
import concourse.bass as bass
import concourse.mybir as mybir
from concourse.bass_utils import run_bass_kernel_spmd

ENGS = ("pe", "act", "dve", "pool", "sp")


class Buf:
    __slots__ = ("name", "writers", "readers")

    def __init__(self, name):
        self.name = name
        self.writers = []
        self.readers = []


class DSem:
    __slots__ = ("sem", "count", "final")

    def __init__(self, sem, final=False):
        self.sem = sem
        self.count = 0
        self.final = final


class Op:
    __slots__ = ("eng", "fn", "deps", "signal", "sigval", "dsem", "dval", "idx")

    def __init__(self, eng, fn):
        self.eng = eng
        self.fn = fn
        self.deps = []
        self.signal = False
        self.sigval = None
        self.dsem = None
        self.dval = None


class Sched:
    def __init__(self, nc, stack):
        self.nc = nc
        self.stack = stack
        self.q = {e: [] for e in ENGS}
        self.esem = {e: stack.enter_context(nc.semaphore("es_" + e)) for e in ENGS}
        self.dmas = []
        self.marks = {}
        self.nds = 0

    def mark(self, name):
        self.marks[name] = {e: len(self.q[e]) for e in ENGS}

    def cut(self, name):
        for e in ENGS:
            del self.q[e][self.marks[name][e]:]

    def dsem(self, final=False):
        self.nds += 1
        return DSem(self.stack.enter_context(self.nc.semaphore("ds%d" % self.nds)), final)

    def _track(self, op, reads, writes, group_dma_writes=False):
        deps = set()
        for b in reads:
            for w in b.writers:
                deps.add((w, True))
        for b in writes:
            for w in b.writers:
                if group_dma_writes and w.dsem is not None and not b.readers and w.dsem is op.dsem:
                    continue
                deps.add((w, False))
            for r in b.readers:
                deps.add((r, False))
        for d, raw in deps:
            if d is op:
                continue
            if d.dsem is None and op.dsem is None and d.eng == op.eng:
                if not (raw and op.eng in ("act", "dve", "pool")):
                    continue
            op.deps.append(d)
            if d.dsem is None:
                d.signal = True
        for b in reads:
            b.readers.append(op)
        for b in writes:
            if group_dma_writes and b.writers and all(
                    w.dsem is not None and w.dsem is op.dsem for w in b.writers) and not b.readers:
                b.writers.append(op)
            else:
                b.writers = [op]
                b.readers = []

    def op(self, eng, fn, reads=(), writes=()):
        o = Op(eng, fn)
        self._track(o, reads, writes)
        self.q[eng].append(o)
        return o

    def dma(self, eng, out, in_, dsem, reads=(), writes=()):
        def fn(e, out=out, in_=in_):
            return e.dma_start(out=out, in_=in_)
        o = Op(eng, fn)
        o.dsem = dsem
        dsem.count += 16
        o.dval = dsem.count
        self._track(o, reads, writes, group_dma_writes=True)
        self.q[eng].append(o)
        self.dmas.append(o)
        return o

    def barrier(self):
        lasts = []
        for e in ENGS:
            for o in reversed(self.q[e]):
                if o.dsem is None and o.fn is not None:
                    o.signal = True
                    lasts.append(o)
                    break
        dm = {}
        for o in self.dmas:
            dm[id(o.dsem)] = o
        self.dmas = []
        for e in ENGS:
            def fn(eng):
                return None
            b = Op(e, None)
            for l in lasts:
                if l.eng != e:
                    b.deps.append(l)
            b.deps.extend(dm.values())
            self.q[e].append(b)

    def emit(self, engines):
        for e in ENGS:
            c = 0
            for o in self.q[e]:
                if o.dsem is None and o.signal and o.fn is not None:
                    c += 1
                    o.sigval = c

    def _waits(self, o):
        waits = {}
        for d in o.deps:
            if d.dsem is not None:
                key = id(d.dsem)
                val = d.dsem.count if d.dsem.final else d.dval
                sem = d.dsem.sem
            else:
                key = d.eng
                val = d.sigval
                sem = self.esem[d.eng]
            assert val is not None, (o.eng, d.eng, d.fn)
            if key not in waits or waits[key][1] < val:
                waits[key] = (sem, val)
        return waits

    def check(self):
        cnt = {}
        ptr = {e: 0 for e in ENGS}
        total = sum(len(self.q[e]) for e in ENGS)
        done = 0
        while done < total:
            prog = False
            for e in ENGS:
                while ptr[e] < len(self.q[e]):
                    o = self.q[e][ptr[e]]
                    ok = all(cnt.get(k, 0) >= v for k, (s_, v) in self._waits(o).items())
                    if not ok:
                        break
                    if o.fn is not None:
                        if o.dsem is not None:
                            cnt[id(o.dsem)] = cnt.get(id(o.dsem), 0) + 16
                        elif o.signal:
                            cnt[e] = cnt.get(e, 0) + 1
                    ptr[e] += 1
                    done += 1
                    prog = True
            if not prog:
                st = {e: (ptr[e], len(self.q[e])) for e in ENGS}
                raise RuntimeError("schedule deadlock: %s" % st)
        return True

    def play(self, e, eng):
        seen = {}
        last_real = None
        for o in self.q[e]:
            waits = {}
            for d in o.deps:
                if d.dsem is not None:
                    key = id(d.dsem)
                    val = d.dsem.count if d.dsem.final else d.dval
                    sem = d.dsem.sem
                else:
                    key = d.eng
                    val = d.sigval
                    sem = self.esem[d.eng]
                if key not in waits or waits[key][1] < val:
                    waits[key] = (sem, val)
            for key, (sem, val) in waits.items():
                if seen.get(key, 0) >= val:
                    continue
                seen[key] = val
                eng.wait_ge(sem, val)
            if o.fn is None:
                continue
            ins = o.fn(eng)
            if o.dsem is not None:
                ins.then_inc(o.dsem.sem, 16)
            elif o.signal:
                ins.then_inc(self.esem[e], 1)


F32 = mybir.dt.float32
BF16 = mybir.dt.bfloat16
I32 = mybir.dt.int32
AF = mybir.ActivationFunctionType
ALU = mybir.AluOpType

S_TOK = 16384
OWN = 2048
NCORES = 8
EPS = 1e-6
G_PRE, G_Q, G_KV, G_PS, G_POST, G_FPRE, G_FPOST = 0, 8, 11, 13, 17, 25, 33
POOL_W = (2, 4, 8, 16)


def build_program(debug=False, cut=None, nta=None, nchunks=None):
    nc = bass.Bass("TRN2", target_bir_lowering=False)

    def din(name, shape, dt=F32):
        return nc.dram_tensor(name, shape, dt, kind="ExternalInput").ap()

    xT = din("xT", [1024, S_TOK])
    xh = din("xh", [1024, 16])
    pos = din("pos", [1, S_TOK], I32)
    cst = din("cst", [128, 4])
    gains = din("gains", [128, 41])
    invcnt = din("invcnt", [4, OWN])
    w_in = din("w_in", [1024, 3232])
    w_uq = din("w_uq", [384, 768])
    w_ukv = din("w_ukv", [256, 1024])
    w_o_attn = din("w_o_attn", [512, 1024])
    w_pool_group = din("w_pool_group", [4, 128, 128])
    w_o_pool = din("w_o_pool", [512, 1024])
    w_out = din("w_out", [1024, 1024])
    w_gate_up = din("w_gate_up", [1024, 5632])
    w_down = din("w_down", [2816, 1024])
    outT = nc.dram_tensor("outT", [1024, OWN], F32, kind="ExternalOutput").ap()
    skind = dict(kind="ExternalOutput") if debug else {}
    kS = nc.dram_tensor("kS", [512, S_TOK], BF16, **skind).ap()
    krS = nc.dram_tensor("krS", [32, S_TOK], BF16, **skind).ap()
    vS = nc.dram_tensor("vS", [8, 128, 128, 65], BF16, **skind).ap()
    cosS = nc.dram_tensor("cosS", [32, S_TOK], F32, **skind).ap()
    sinS = nc.dram_tensor("sinS", [32, S_TOK], F32, **skind).ap()
    x1S = nc.dram_tensor("x1S", [1024, OWN], F32, **skind).ap()
    if debug:
        qtD = nc.dram_tensor("qtD", [128, 8, OWN], BF16, kind="ExternalOutput").ap()
        atD = nc.dram_tensor("atD", [128, 8, OWN], BF16, kind="ExternalOutput").ap()
        pmD = nc.dram_tensor("pmD", [128, 4, OWN], BF16, kind="ExternalOutput").ap()

    xTv = xT.rearrange("(k p) n -> p k n", p=128)
    xhv = xh.rearrange("(k p) n -> p k n", p=128)
    w_inv = w_in.rearrange("(k p) n -> p k n", p=128)
    outTv = outT.rearrange("(k p) n -> p k n", p=128)
    x1Sv = x1S.rearrange("(k p) n -> p k n", p=128)
    kSv = kS.rearrange("(j q) t -> q j t", q=128)
    vSv = vS.rearrange("h p t d -> p h t d")

    with ExitStack() as gst:
        S = Sched(nc, gst)
        ps = gst.enter_context(nc.psum_tensor("ps", [128, 4096], F32))
        PB = [Buf("pb%d" % b) for b in range(8)]

        def bk(b):
            return ps[:, 512 * b:512 * (b + 1)]

        class T:
            def __init__(self, st, name, shape, dt, nb=1, side=None):
                if side:
                    self.t = st.enter_context(nc.sbuf_tensor("sb_" + name, shape, dt, side=side))
                else:
                    self.t = st.enter_context(nc.sbuf_tensor("sb_" + name, shape, dt))
                self.b = [Buf(name + str(i)) for i in range(nb)]

            def __getitem__(self, k):
                return self.t[k]

            @property
            def B(self):
                return self.b[0]

        def pe(mms, reads, writes):
            def fn(e, mms=mms):
                ins = None
                for (o, l, r, st_, sp_) in mms:
                    ins = e.matmul(o, l, r, start=st_, stop=sp_)
                return ins
            return S.op("pe", fn, reads, writes)

        def act(out, in_, func, reads, writes, scale=1.0, bias=0.0):
            return S.op("act", lambda e: e.activation(out, in_, func, bias=bias, scale=scale), reads, writes)

        def dve(fn, reads, writes):
            return S.op("dve", fn, reads, writes)

        def tt(out, a, b, op, reads, writes, eng="dve"):
            return S.op(eng, lambda e: e.tensor_tensor(out, a, b, op), reads, writes)

        def stt(out, in0, sc, in1, op0, op1, reads, writes):
            return S.op("dve", lambda e: e.scalar_tensor_tensor(out, in0, sc, in1, op0, op1), reads, writes)

        def ts(out, in0, s1, s2, op0, op1, reads, writes):
            if s2 is None:
                return S.op("dve", lambda e: e.tensor_scalar(out, in0, s1, None, op0), reads, writes)
            return S.op("dve", lambda e: e.tensor_scalar(out, in0, s1, s2, op0, op1), reads, writes)

        cstT = T(gst, "cst", [128, 4], F32)
        gT = T(gst, "gains", [128, 41], F32)
        ones = T(gst, "ones", [128, 128], BF16)
        onesf = T(gst, "onesf", [128, 128], F32)
        dC = S.dsem(final=True)
        S.dma("sp", cstT[:], cst, dC, writes=[cstT.B])
        S.dma("sp", gT[:], gains, dC, writes=[gT.B])
        S.op("pool", lambda e: e.memset(ones[:], 1.0), writes=[ones.B])
        S.op("pool", lambda e: e.memset(onesf[:], 1.0), writes=[onesf.B])
        epsb = cstT[:, 3:4]

        def gcol(off, k, p0=0, p1=128):
            return gT[p0:p1, off + k:off + k + 1]

        def rstd_ops(out_ap, out_B, ss_ap, ss_B, sc):
            act(out_ap, ss_ap, AF.Sqrt, [ss_B, cstT.B], [out_B], scale=sc, bias=epsb)
            dve(lambda e: e.reciprocal(out_ap, out_ap), [out_B], [out_B])

        def tot_ops(out_ap, out_B, ss_ap, ss_B, a_ap, a_B, nf):
            stt(out_ap, ss_ap, 1.0 / nf, a_ap, ALU.mult, ALU.add, [ss_B, a_B], [out_B])
            act(out_ap, out_ap, AF.Sqrt, [out_B], [out_B])
            dve(lambda e: e.reciprocal(out_ap, out_ap), [out_B], [out_B])

        dStage = [S.dsem(), S.dsem()]
        stage_ctr = [0]

        def load_folded(st, stg, dst_fn, src_fn, ncols, goff, q="sp", dsems=None):
            dsems = dsems or dStage
            for k in range(8):
                i = stage_ctr[0] % 2
                stage_ctr[0] += 1
                S.dma(q, stg[i][:, 0:ncols], src_fn(k), dsems[i], writes=[stg[i].B])
                ts(dst_fn(k), stg[i][:, 0:ncols], gcol(goff, k), None, ALU.mult, None,
                   [stg[i].B, gT.B], [])

        with ExitStack() as st:
            I = T(st, "ti", [128, 4096], I32)
            X = T(st, "tx", [128, 4096], F32)
            Y = T(st, "ty", [128, 4096], F32)
            Z = T(st, "tz", [128, 4096], F32)
            W = T(st, "tw", [128, 4096], F32)
            d0 = S.dsem(final=True)
            for s in range(4):
                S.dma("sp", I[32 * s:32 * s + 32, :],
                      bass.AP(pos.tensor, 4096 * s, [[0, 32], [1, 4096]]),
                      d0, writes=[I.B])
            C1 = 6.28125
            C2 = 2 * math.pi - C1
            dve(lambda e: e.tensor_copy(X[:], I[:]), [I.B], [X.B])
            ts(X[:], X[:], cstT[:, 0:1], None, ALU.mult, None, [X.B, cstT.B], [X.B])
            ts(I[:], X[:], 1.0 / (2 * math.pi), None, ALU.mult, None, [X.B], [I.B])
            dve(lambda e: e.tensor_copy(Y[:], I[:]), [I.B], [Y.B])
            stt(Z[:], Y[:], -C1, X[:], ALU.mult, ALU.add, [Y.B, X.B], [Z.B])
            stt(Z[:], Y[:], -C2, Z[:], ALU.mult, ALU.add, [Y.B, Z.B], [Z.B])
            ts(Y[:], Z[:], math.pi, -2 * math.pi, ALU.is_gt, ALU.mult, [Z.B], [Y.B])
            tt(Z[:], Z[:], Y[:], ALU.add, [Z.B, Y.B], [Z.B])
            act(X[:], Z[:], AF.Sin, [Z.B, cstT.B], [X.B], scale=cstT[:, 1:2])
            dT = S.dsem(final=True)
            BtabS = Buf("tabS")
            for s in range(4):
                S.dma("sp", sinS[:, 4096 * s:4096 * (s + 1)], X[32 * s:32 * s + 32, :], dT, reads=[X.B], writes=[BtabS])
            ts(Y[:], Z[:], math.pi / 2, None, ALU.add, None, [Z.B], [Y.B])
            ts(W[:], Y[:], math.pi, -2 * math.pi, ALU.is_gt, ALU.mult, [Y.B], [W.B])
            tt(Y[:], Y[:], W[:], ALU.add, [Y.B, W.B], [Y.B])
            act(W[:], Y[:], AF.Sin, [Y.B], [W.B])
            for s in range(4):
                S.dma("sp", cosS[:, 4096 * s:4096 * (s + 1)], W[32 * s:32 * s + 32, :], dT, reads=[W.B], writes=[BtabS])
            S.barrier()
            S.mark("p0")

        def prep_tile(xb, sq, r1, a1, src_ap, n, dX, need_a=True):
            S.dma("pool", xb[:, :, 0:n], src_ap, dX, writes=[xb.B])
            act(sq[:, :, 0:n], xb[:, :, 0:n], AF.Square, [xb.B], [sq.B])
            pe([(bk(0)[:, 0:n], ones[:], sq[:, k, 0:n], k == 0, k == 7) for k in range(8)],
               [sq.B, ones.B], [PB[0]])
            rstd_ops(r1[:, 0:n], r1.B, bk(0)[:, 0:n], PB[0], 1.0 / 1024)
            if need_a:
                ts(a1[:, 0:n], bk(0)[:, 0:n], EPS / 1024, EPS * EPS, ALU.mult, ALU.add, [PB[0], r1.B], [a1.B])

        with ExitStack() as bq:
            AT = T(bq, "AT", [128, 8, OWN], BF16)
            PM = T(bq, "PM", [128, 4, OWN], BF16)
            qsc = ExitStack()
            QT = T(qsc, "QT", [128, 8, OWN], BF16)

            qw = ExitStack()
            stgq2 = [T(qw, "stgq2%d" % i, [128, 512], F32, side="right") for i in range(2)]
            wcq = T(qw, "wcq", [128, 8, 384], BF16, side="right")
            wu = T(qw, "wu", [128, 8, 512], BF16, side="right")
            wq = T(qw, "wq", [128, 3, 768], BF16, side="right")
            wqr = T(qw, "wqr", [128, 3, 8, 96], BF16, side="right")
            wpg = T(qw, "wpg", [128, 4, 128], BF16, side="right")
            dWq = S.dsem(final=True)
            dStq = [S.dsem(), S.dsem()]

            def load_q_weights():
                st = None
                stg = stgq2
                dW = dWq
                load_folded(st, stg, lambda k: wcq[:, k, :], lambda k: w_inv[:, k, 0:384], 384, G_PRE, dsems=dStq)
                wcq.B.writers = [S.q["dve"][-1]]
                load_folded(st, stg, lambda k: wu[:, k, :], lambda k: w_inv[:, k, 672:1184], 512, G_PRE, dsems=dStq)
                wu.B.writers = [S.q["dve"][-1]]
                S.dma("pool", wq[:], w_uq.rearrange("(k p) n -> p k n", p=128), dW, writes=[wq.B])
                S.op("pool", lambda e: e.memset(wqr[:], 0.0), writes=[wqr.B])
                uqv = w_uq.rearrange("(k p) (h d) -> p k h d", p=128, d=96)
                for c in range(3):
                    S.dma("pool", wqr[:, c, :, 64:80], uqv[:, c, :, 80:96], dW, writes=[wqr.B])
                    S.dma("pool", wqr[:, c, :, 80:96], uqv[:, c, :, 64:80], dW, writes=[wqr.B])
                S.dma("pool", wpg[:], w_pool_group.rearrange("g c d -> c g d"), dW, writes=[wpg.B])


            with ExitStack() as st:
                stg = [T(st, "stg%d" % i, [128, 512], F32) for i in range(2)]
                wkv = T(st, "wkv", [128, 8, 320], BF16)
                wk = T(st, "wk", [128, 2, 512], BF16)
                wv = T(st, "wv", [128, 2, 512], BF16)
                xb = [T(st, "xb%d" % i, [128, 8, 512], BF16) for i in range(2)]
                sq = [T(st, "sq%d" % i, [128, 8, 512], BF16) for i in range(2)]
                r1 = [T(st, "r1%d" % i, [128, 512], F32) for i in range(2)]
                a1 = [T(st, "a1%d" % i, [128, 512], F32) for i in range(2)]
                sqr = [T(st, "sqr%d" % i, [128, 2, 512], BF16) for i in range(2)]
                tot = T(st, "tot", [128, 512], F32)
                ckvn = [T(st, "ckvn%d" % i, [128, 2, 512], BF16) for i in range(2)]
                kst = [T(st, "kst%d" % i, [128, 4, 512], BF16) for i in range(2)]
                vst = [T(st, "vst%d" % i, [128, 8, 4, 65], BF16) for i in range(2)]
                ct = [T(st, "ct%d" % i, [32, 512], F32) for i in range(2)]
                sn = [T(st, "sn%d" % i, [32, 512], F32) for i in range(2)]
                krs = [T(st, "krs%d" % i, [32, 512], BF16) for i in range(2)]
                t1 = T(st, "t1", [32, 512], F32)
                t2 = T(st, "t2", [32, 512], F32)
                dW = S.dsem(final=True)
                dX = [S.dsem(), S.dsem()]
                dTab = [S.dsem(), S.dsem()]
                dTabS = [S.dsem(), S.dsem()]
                dKs = [S.dsem(), S.dsem()]
                dVs = [S.dsem(), S.dsem()]
                dKr = [S.dsem(), S.dsem()]
                BscrK, BscrV, BscrR = Buf("scrK"), Buf("scrV"), Buf("scrR")

                load_folded(st, stg, lambda k: wkv[:, k, 0:288], lambda k: w_inv[:, k, 384:672], 288, G_PRE)
                load_folded(st, stg, lambda k: wkv[:, k, 288:304], lambda k: w_inv[:, k, 656:672], 16, G_PRE)
                load_folded(st, stg, lambda k: wkv[:, k, 304:320], lambda k: w_inv[:, k, 640:656], 16, G_PRE)
                wkv.B.writers = [S.q["dve"][-1]]
                ukv = w_ukv.rearrange("(k p) (h two d) -> p k h two d", p=128, two=2, d=64)
                for kc in range(2):
                    S.dma("pool", wk[:, kc, :].rearrange("p (h d) -> p h d", d=64), ukv[:, kc, :, 0, :], dW, writes=[wk.B])
                    S.dma("pool", wv[:, kc, :].rearrange("p (h d) -> p h d", d=64), ukv[:, kc, :, 1, :], dW, writes=[wv.B])
                for i in range(2):
                    S.op("pool", lambda e, i=i: e.memset(vst[i][:], 1.0), writes=[vst[i].B])

                NT = nta or (S_TOK // 512)

                def pre(i):
                    p = i % 2
                    c0 = 512 * i
                    S.dma("pool", xb[p][:], xTv[:, :, c0:c0 + 512], dX[p], writes=[xb[p].B])
                    act(sq[p][:], xb[p][:], AF.Square, [xb[p].B], [sq[p].B])

                def stage1(i):
                    p = i % 2
                    c0 = 512 * i
                    pe([(bk(0), ones[:], sq[p][:, k, :], k == 0, k == 7) for k in range(8)],
                       [sq[p].B, ones.B], [PB[0]])
                    rstd_ops(r1[p][:], r1[p].B, bk(0), PB[0], 1.0 / 1024)
                    ts(a1[p][:], bk(0), EPS / 1024, EPS * EPS, ALU.mult, ALU.add, [PB[0], r1[p].B], [a1[p].B])
                    S.dma("sp", ct[p][:], cosS[:, c0:c0 + 512], dTab[p], reads=[BtabS], writes=[ct[p].B])
                    S.dma("sp", sn[p][:], sinS[:, c0:c0 + 512], dTabS[p], reads=[BtabS], writes=[sn[p].B])
                    for c in range(2):
                        pe([(bk(1 + c), wkv[:, k, 128 * c:128 * c + 128], xb[p][:, k, :], k == 0, k == 7) for k in range(8)],
                           [wkv.B, xb[p].B], [PB[1 + c]])
                        act(sqr[p][:, c, :], bk(1 + c), AF.Square, [PB[1 + c]], [sqr[p].B])
                    pe([(bk(3)[0:32, :], wkv[:, k, 256:288], xb[p][:, k, :], k == 0, k == 7) for k in range(8)],
                       [wkv.B, xb[p].B], [PB[3]])
                    pe([(bk(4)[0:32, :], wkv[:, k, 288:320], xb[p][:, k, :], k == 0, k == 7) for k in range(8)],
                       [wkv.B, xb[p].B], [PB[4]])
                    tt(t1[:], bk(3)[0:32, :], ct[p][:], ALU.mult, [PB[3], ct[p].B], [t1.B])
                    tt(t2[:], bk(4)[0:32, :], sn[p][:], ALU.mult, [PB[4], sn[p].B], [t2.B])
                    tt(t1[:], t1[:], t2[:], ALU.add, [t1.B, t2.B], [t1.B])
                    tt(krs[p][:], t1[:], r1[p][0:32, :], ALU.mult, [t1.B, r1[p].B], [krs[p].B])
                    S.dma("sp", krS[:, c0:c0 + 512], krs[p][:], dKr[p], reads=[krs[p].B], writes=[BscrR])

                def stage2(i):
                    p = i % 2
                    pe([(bk(0), ones[:], sqr[p][:, c, :], c == 0, c == 1) for c in range(2)],
                       [ones.B, sqr[p].B], [PB[0]])
                    tot_ops(tot[:], tot.B, bk(0), PB[0], a1[p][:], a1[p].B, 256)
                    for c in range(2):
                        stt(ckvn[p][:, c, :], bk(1 + c), gcol(G_KV, c), tot[:], ALU.mult, ALU.mult,
                            [PB[1 + c], gT.B, tot.B], [ckvn[p].B])

                rot = [0]

                def rb():
                    b = 5 + rot[0] % 3
                    rot[0] += 1
                    return b

                def stage3(i):
                    p = i % 2
                    c0 = 512 * i
                    for j in range(4):
                        b = rb()
                        pe([(bk(b), wk[:, kc, 128 * j:128 * j + 128], ckvn[p][:, kc, :], kc == 0, kc == 1) for kc in range(2)],
                           [wk.B, ckvn[p].B], [PB[b]])
                        act(kst[p][:, j, :], bk(b), AF.Copy, [PB[b]], [kst[p].B])
                    for s in range(4):
                        b = rb()
                        pe([(bk(b), ckvn[p][:, kc, 128 * s:128 * s + 128], wv[:, kc, :], kc == 0, kc == 1) for kc in range(2)],
                           [wv.B, ckvn[p].B], [PB[b]])
                        act(vst[p][:, :, s, 0:64], bk(b).rearrange("p (h d) -> p h d", d=64), AF.Copy, [PB[b]], [vst[p].B])
                    S.dma("sp", kSv[:, :, c0:c0 + 512], kst[p][:], dKs[p], reads=[kst[p].B], writes=[BscrK])
                    S.dma("sp", vSv[:, :, 4 * i:4 * i + 4, :], vst[p][:], dVs[p], reads=[vst[p].B], writes=[BscrV])

                def dbg_mark(nm):
                    if cut == nm:
                        S.barrier()
                        S.mark(nm)
                dbg_mark("Aw")
                pre(0)
                if NT > 1:
                    pre(1)
                stage1(0)
                load_q_weights()
                dbg_mark("A1")
                for i in range(NT):
                    stage2(i)
                    if i == 0:
                        dbg_mark("A2")
                    if i + 1 < NT:
                        stage1(i + 1)
                    if i + 2 < NT:
                        pre(i + 2)
                    stage3(i)
                    if i == 0:
                        dbg_mark("A3")
                S.barrier()
                S.mark("A")

            with ExitStack() as st:
                stg = [T(st, "stgq%d" % i, [128, 512], F32) for i in range(2)]
                xb = T(st, "xbq", [128, 8, 512], BF16)
                sq = T(st, "sqq", [128, 8, 512], BF16)
                r1 = T(st, "r1q", [128, 512], F32)
                a1 = T(st, "a1q", [128, 512], F32)
                uT = T(st, "uT", [128, 4, OWN + 16], F32)
                st1 = ExitStack()
                cqraw = T(st1, "cqraw", [128, 3, 512], F32)
                sqcq = T(st1, "sqcq", [128, 3, 512], BF16)
                totq = T(st1, "totq", [128, 512], F32)
                cqn = T(st1, "cqn", [128, 3, 512], BF16)
                ctq = T(st1, "ctq", [128, 512], F32)
                snq = T(st1, "snq", [128, 512], F32)
                tq1 = [T(st1, "tq1%d" % i, [128, 512], F32) for i in range(2)]
                tq2 = [T(st1, "tq2%d" % i, [128, 512], F32) for i in range(2)]
                dW = S.dsem(final=True)
                dX = S.dsem()
                dTab = S.dsem()
                dTabS = S.dsem()
                dIc = S.dsem()

                rot = [0]

                def rb():
                    b = 1 + rot[0] % 7
                    rot[0] += 1
                    return b

                def u_part(n, dst_fns):
                    for g in range(4):
                        b = rb()
                        pe([(bk(b)[:, 0:n], wu[:, k, 128 * g:128 * g + 128], xb[:, k, 0:n], k == 0, k == 7) for k in range(8)],
                           [wu.B, xb.B], [PB[b]])
                        for (lo, hi, dfn) in dst_fns:
                            tt(dfn(g), bk(b)[:, lo:hi], r1[:, lo:hi], ALU.mult, [PB[b], r1.B], [uT.B])

                def dbg_markq(nm):
                    if cut == nm:
                        S.barrier()
                        S.mark(nm)
                dbg_markq("Qw")
                prep_tile(xb, sq, r1, a1, xhv, 16, dX, need_a=False)
                u_part(16, [(0, 8, lambda g: uT[:, g, 0:8]), (8, 16, lambda g: uT[:, g, OWN + 8:OWN + 16])])

                dbg_markq("Qh")
                for i in range(4):
                    if i == 1:
                        dbg_markq("Q0")
                    c0 = 512 * i
                    prep_tile(xb, sq, r1, a1, xTv[:, :, c0:c0 + 512], 512, dX)
                    S.dma("sp", ctq[64:96, :], cosS[:, c0:c0 + 512], dTab, reads=[BtabS], writes=[ctq.B])
                    S.dma("sp", snq[64:96, :], sinS[:, c0:c0 + 512], dTabS, reads=[BtabS], writes=[snq.B])
                    u_part(512, [(0, 512, lambda g, c0=c0: uT[:, g, 8 + c0:8 + c0 + 512])])
                    for c in range(3):
                        b = rb()
                        pe([(bk(b), wcq[:, k, 128 * c:128 * c + 128], xb[:, k, :], k == 0, k == 7) for k in range(8)],
                           [wcq.B, xb.B], [PB[b]])
                        act(sqcq[:, c, :], bk(b), AF.Square, [PB[b]], [sqcq.B])
                        dve(lambda e, b=b, c=c: e.tensor_copy(cqraw[:, c, :], bk(b)), [PB[b], sqcq.B], [cqraw.B])
                    b = rb()
                    pe([(bk(b), ones[:], sqcq[:, c, :], c == 0, c == 2) for c in range(3)], [ones.B, sqcq.B], [PB[b]])
                    tot_ops(totq[:], totq.B, bk(b), PB[b], a1[:], a1.B, 384)
                    for c in range(3):
                        stt(cqn[:, c, :], cqraw[:, c, :], gcol(G_Q, c), totq[:], ALU.mult, ALU.mult,
                            [cqraw.B, gT.B, totq.B], [cqn.B])
                    for h in range(8):
                        b1 = rb()
                        pe([(bk(b1)[0:96, :], wq[:, c, 96 * h:96 * h + 96], cqn[:, c, :], c == 0, c == 2) for c in range(3)],
                           [wq.B, cqn.B], [PB[b1]])
                        b2 = rb()
                        pe([(bk(b2)[0:96, :], wqr[:, c, h, :], cqn[:, c, :], c == 0, c == 2) for c in range(3)],
                           [wqr.B, cqn.B], [PB[b2]])
                        x1_, x2_ = tq1[h % 2], tq2[h % 2]
                        act(x1_[0:96, :], bk(b1)[0:96, :], AF.Copy, [PB[b1]], [x1_.B])
                        act(x2_[0:96, :], bk(b2)[0:96, :], AF.Copy, [PB[b2]], [x2_.B])
                        act(QT[0:64, h, c0:c0 + 512], bk(b1)[0:64, :], AF.Copy, [PB[b1]], [QT.B])
                        tt(x1_[64:96, :], x1_[64:96, :], ctq[64:96, :], ALU.mult, [x1_.B, ctq.B], [x1_.B])
                        tt(x2_[64:96, :], x2_[64:96, :], snq[64:96, :], ALU.mult, [x2_.B, snq.B], [x2_.B])
                        tt(QT[64:96, h, c0:c0 + 512], x1_[64:96, :], x2_[64:96, :], ALU.add, [x1_.B, x2_.B], [QT.B])

                S.barrier()
                S.mark("Q1")
                st1.close()
                TA = T(st, "TA", [128, OWN + 16], F32)
                TB = T(st, "TB", [128, OWN + 16], F32)
                invc = T(st, "invc", [128, OWN], F32)
                pooled = T(st, "pooled", [128, 4, OWN], BF16)
                L = OWN + 16

                def sh(dst, src, lo, hi, d1, d2):
                    return lambda e: e.tensor_tensor(dst[:, lo:hi], src[:, lo + d1:hi + d1], src[:, lo + d2:hi + d2], ALU.add)

                for g in range(4):
                    ug = uT.t[:, g, :]
                    S.dma("sp", invc[:], bass.AP(invcnt.tensor, OWN * g, [[0, 128], [1, OWN]]), dIc, writes=[invc.B])
                    S.op("pool", sh(TA, ug, 1, L, -1, 0), [uT.B], [TA.B])
                    win = TA
                    if g >= 1:
                        S.op("pool", sh(TB, TA, 2, L - 1, -1, 1), [TA.B], [TB.B])
                        win = TB
                    if g >= 2:
                        S.op("pool", sh(TA, TB, 4, L - 3, -2, 2), [TB.B], [TA.B])
                        win = TA
                    if g >= 3:
                        S.op("pool", sh(TB, TA, 8, L - 7, -4, 4), [TA.B], [TB.B])
                        win = TB
                    other = TB if win is TA else TA
                    tt(other[:, 8:8 + OWN], win[:, 8:8 + OWN], invc[:], ALU.mult, [win.B, invc.B], [other.B])
                    tt(pooled[:, g, :], other[:, 8:8 + OWN], ug[:, 8:8 + OWN], ALU.subtract, [other.B, uT.B], [pooled.B])
                    for nt in range(4):
                        b = rb()
                        pe([(bk(b), wpg[:, g, :], pooled[:, g, 512 * nt:512 * nt + 512], True, True)], [wpg.B, pooled.B], [PB[b]])
                        act(PM[:, g, 512 * nt:512 * nt + 512], bk(b), AF.Identity, [PB[b], gT.B], [PM.B], scale=gcol(G_PS, g))
                if debug:
                    dD = S.dsem(final=True)
                    S.dma("sp", qtD[0:96], QT[0:96], dD, reads=[QT.B])
                    S.dma("sp", pmD, PM[:], dD, reads=[PM.B])
                S.barrier()
                S.mark("Q")
            qw.close()

            cw = ExitStack()
            stgc2 = [T(cw, "stgc2%d" % i, [128, 1024], F32, side="right") for i in range(2)]
            wg = T(cw, "wg", [128, 8, 2048], BF16, side="right")
            woa = T(cw, "woa", [64, 8, 1024], BF16, side="right")
            wob = T(cw, "wob", [128, 4, 1024], BF16, side="right")
            wout = T(cw, "wout", [128, 8, 1024], BF16, side="right")
            dWc = S.dsem(final=True)
            dStc = [S.dsem(), S.dsem()]

            def load_c1_weights():
                st = None
                stg = stgc2
                dW = dWc
                load_folded(st, stg, lambda k: wg[:, k, 0:1024], lambda k: w_inv[:, k, 1184:2208], 1024, G_PRE, q="pool", dsems=dStc)
                load_folded(st, stg, lambda k: wg[:, k, 1024:2048], lambda k: w_inv[:, k, 2208:3232], 1024, G_PRE, q="pool", dsems=dStc)
                wg.B.writers = [S.q["dve"][-1]]
                S.dma("pool", woa[:], w_o_attn.rearrange("(h d) n -> d h n", d=64), dW, writes=[woa.B])
                S.dma("pool", wob[:], w_o_pool.rearrange("(c p) n -> p c n", p=128), dW, writes=[wob.B])
                for k in range(8):
                    S.dma("pool", wout[:, k, :], w_out[128 * k:128 * k + 128, :], dW, writes=[wout.B])

            with ExitStack() as st:
                NSL = 4
                NSB = 3
                NPB = 4
                LOOK = 2
                kc = [T(st, "kc%d" % i, [128, 2048], BF16) for i in range(NSL)]
                vc = [T(st, "vc%d" % i, [128, 16, 65], BF16) for i in range(NSL)]
                Pb = [T(st, "P%d" % i, [128, 1024], BF16) for i in range(NPB)]
                rs = T(st, "rs", [128, 1024], F32)
                rc = [T(st, "rc%d" % i, [64, 512], F32) for i in range(2)]
                dSl = [S.dsem() for _ in range(NSL)]
                dSlV = [S.dsem() for _ in range(NSL)]
                scale = 96 ** -0.5
                chunks = [(h, qh, c) for h in range(8) for qh in range(2) for c in range(8)]
                if nchunks:
                    chunks = chunks[:nchunks]

                def load_chunk(n):
                    h, qh, c = chunks[n]
                    s = n % NSL
                    S.dma("sp", kc[s][0:64, :], kS[64 * h:64 * h + 64, 2048 * c:2048 * (c + 1)], dSl[s], reads=[BscrK], writes=[kc[s].B])
                    S.dma("sp", kc[s][64:96, :], krS[:, 2048 * c:2048 * (c + 1)], dSl[s], reads=[BscrR], writes=[kc[s].B])
                    S.dma("sp", vc[s][:], vS[h, :, 16 * c:16 * c + 16, :], dSlV[s], reads=[BscrV], writes=[vc[s].B])

                for n in range(min(3, len(chunks))):
                    load_chunk(n)
                load_c1_weights()
                step = [0]
                pend = []

                def emit_pv(pv):
                    (s, kt, pb, first, last) = pv
                    pe([(ps[0:65, 512 * q:512 * (q + 1)], vc[s][:, kt, :], Pb[pb][:, 512 * q:512 * q + 512], first, last)
                        for q in range(2)],
                       [vc[s].B, Pb[pb].B], [PB[0], PB[1]])

                for n, (h, qh, c) in enumerate(chunks):
                    s = n % NSL
                    q0 = 1024 * qh
                    for kt in range(16):
                        sb_ = step[0] % NSB
                        pb = step[0] % NPB
                        step[0] += 1
                        b0 = 2 + 2 * sb_
                        pe([(bk(b0 + q), kc[s][0:96, 128 * kt:128 * kt + 128], QT[0:96, h, q0 + 512 * q:q0 + 512 * q + 512], True, True)
                            for q in range(2)],
                           [kc[s].B, QT.B], [PB[b0], PB[b0 + 1]])
                        act(Pb[pb][:], ps[:, 512 * b0:512 * b0 + 1024], AF.Exp, [PB[b0], PB[b0 + 1]], [Pb[pb].B], scale=scale)
                        pend.append((s, kt, pb, (c == 0 and kt == 0), (c == 7 and kt == 15)))
                        if len(pend) > LOOK:
                            emit_pv(pend.pop(0))
                        if kt == LOOK and n + 3 < len(chunks):
                            load_chunk(n + 3)
                    if c == 7:
                        while pend:
                            emit_pv(pend.pop(0))
                        for qb in range(2):
                            act(rs[64:65, 512 * qb:512 * qb + 512], bk(qb)[64:65, :], AF.Copy, [PB[qb]], [rs.B])
                            pe([(bk(2 + qb)[0:64, :], onesf[64:65, 0:64], rs[64:65, 512 * qb:512 * qb + 512], True, True)],
                               [onesf.B, rs.B], [PB[2 + qb]])
                            r_ = rc[qb % 2]
                            dve(lambda e, r_=r_, qb=qb: e.reciprocal(r_[:], bk(2 + qb)[0:64, :]), [PB[2 + qb]], [r_.B])
                            tt(AT[0:64, h, q0 + 512 * qb:q0 + 512 * qb + 512], bk(qb)[0:64, :], r_[:], ALU.mult, [PB[qb], r_.B], [AT.B])
                if debug:
                    dD2 = S.dsem(final=True)
                    S.dma("sp", atD[0:64], AT[0:64], dD2, reads=[AT.B])
                S.barrier()
                S.mark("B")

            qsc.close()
            with ExitStack() as st:
                xb = T(st, "xbc", [128, 8, 512], BF16)
                sq = T(st, "sqc", [128, 8, 512], BF16)
                xf = T(st, "xfc", [128, 8, 512], F32)
                r1 = T(st, "r1c", [128, 512], F32)
                r2 = T(st, "r2c", [128, 512], F32)
                zA = [T(st, "zA%d" % i, [128, 512], F32) for i in range(2)]
                zB = [T(st, "zB%d" % i, [128, 512], F32) for i in range(2)]
                m = T(st, "m", [128, 8, 512], BF16)
                y = T(st, "y", [128, 8, 512], F32)
                sqy = [T(st, "sqy%d" % i, [128, 512], BF16) for i in range(2)]
                dW = S.dsem(final=True)
                dX = S.dsem()
                dXf = S.dsem()
                dO = S.dsem()
                Bx1S = Buf("x1S")
                for i in range(4):
                    c0 = 512 * i
                    S.dma("sp", xf[:], xTv[:, :, c0:c0 + 512], dXf, writes=[xf.B])
                    prep_tile(xb, sq, r1, None, xTv[:, :, c0:c0 + 512], 512, dX, need_a=False)
                    for j in range(8):
                        bA, bB, ba, bb = (1, 2, 3, 4) if j % 2 == 0 else (5, 6, 7, 4)
                        z1, z2 = zA[j % 2], zB[j % 2]
                        pe([(bk(bA), wg[:, k, 128 * j:128 * j + 128], xb[:, k, :], k == 0, k == 7) for k in range(8)],
                           [wg.B, xb.B], [PB[bA]])
                        pe([(bk(bB), wg[:, k, 1024 + 128 * j:1024 + 128 * j + 128], xb[:, k, :], k == 0, k == 7) for k in range(8)],
                           [wg.B, xb.B], [PB[bB]])
                        pe([(bk(ba), woa[0:64, h, 128 * j:128 * j + 128], AT[0:64, h, c0:c0 + 512], h == 0, h == 7) for h in range(8)],
                           [woa.B, AT.B], [PB[ba]])
                        tt(z1[:], bk(bA), r1[:], ALU.mult, [PB[bA], r1.B], [z1.B])
                        act(z1[:], z1[:], AF.Sigmoid, [z1.B], [z1.B])
                        tt(z1[:], z1[:], bk(ba), ALU.mult, [z1.B, PB[ba]], [z1.B])
                        pe([(bk(bb), wob[:, c, 128 * j:128 * j + 128], PM[:, c, c0:c0 + 512], c == 0, c == 3) for c in range(4)],
                           [wob.B, PM.B], [PB[bb]])
                        tt(z2[:], bk(bB), r1[:], ALU.mult, [PB[bB], r1.B], [z2.B])
                        act(z2[:], z2[:], AF.Sigmoid, [z2.B], [z2.B])
                        tt(z2[:], z2[:], bk(bb), ALU.mult, [z2.B, PB[bb]], [z2.B])
                        tt(m[:, j, :], z1[:], z2[:], ALU.add, [z1.B, z2.B], [m.B])
                    for j in range(8):
                        b = 1 + j % 6
                        pe([(bk(b), wout[:, k, 128 * j:128 * j + 128], m[:, k, :], k == 0, k == 7) for k in range(8)],
                           [wout.B, m.B], [PB[b]])
                        act(y[:, j, :], bk(b), AF.Copy, [PB[b]], [y.B])
                        sy = sqy[j % 2]
                        act(sy[:], bk(b), AF.Square, [PB[b]], [sy.B])
                        pe([(bk(7), ones[:], sy[:], j == 0, j == 7)], [ones.B, sy.B], [PB[7]])
                    rstd_ops(r2[:], r2.B, bk(7), PB[7], 1.0 / 1024)
                    for j in range(8):
                        stt(y[:, j, :], y[:, j, :], gcol(G_POST, j), r2[:], ALU.mult, ALU.mult, [y.B, gT.B, r2.B], [y.B])
                        tt(y[:, j, :], y[:, j, :], xf[:, j, :], ALU.add, [y.B, xf.B], [y.B])
                    S.dma("sp", x1Sv[:, :, c0:c0 + 512], y[:], dO, reads=[y.B], writes=[Bx1S])
                S.barrier()
                S.mark("C1")
            cw.close()

        with ExitStack() as st:
            x1h = [T(st, "x1h%d" % i, [128, 8, 1024], F32) for i in range(2)]
            hfb = T(st, "hfb", [128, 8, 1024], BF16)
            actT = T(st, "actT", [128, 22, 1024], BF16)
            y2 = T(st, "y2", [128, 8, 1024], F32)
            wgu = [T(st, "wgu%d" % i, [128, 8, 256], BF16) for i in range(3)]
            wd = [T(st, "wd%d" % i, [128, 22, 128], BF16) for i in range(2)]
            sqc = [T(st, "sqf%d" % i, [128, 512], BF16) for i in range(2)]
            sg = [T(st, "sg%d" % i, [128, 512], F32) for i in range(2)]
            r3 = T(st, "r3", [128, 1024], F32)
            r4 = T(st, "r4", [128, 1024], F32)
            dX1 = [S.dsem(), S.dsem()]
            dGU = [S.dsem() for _ in range(3)]
            dWD = [S.dsem() for _ in range(2)]
            dOut = S.dsem()
            wguv = w_gate_up.rearrange("(k p) n -> p k n", p=128)
            wdv = w_down.rearrange("(k p) n -> p k n", p=128)

            def load_gu(hf, j):
                s = (hf * 22 + j) % 3
                S.dma("pool", wgu[s][:, :, 0:128], wguv[:, :, 128 * j:128 * j + 128], dGU[s], writes=[wgu[s].B])
                S.dma("pool", wgu[s][:, :, 128:256], wguv[:, :, 2816 + 128 * j:2816 + 128 * j + 128], dGU[s], writes=[wgu[s].B])

            def load_wd(hf, i):
                s = (hf * 8 + i) % 2
                S.dma("pool", wd[s][:], wdv[:, :, 128 * i:128 * i + 128], dWD[s], writes=[wd[s].B])

            def prologue(hf):
                xh_ = x1h[hf]
                for nt in range(2):
                    n0 = 512 * nt
                    for k in range(8):
                        sc_ = sqc[k % 2]
                        act(sc_[:], xh_[:, k, n0:n0 + 512], AF.Square, [xh_.B], [sc_.B])
                        pe([(bk(6 + nt), ones[:], sc_[:], k == 0, k == 7)], [ones.B, sc_.B], [PB[6 + nt]])
                    rstd_ops(r3[:, n0:n0 + 512], r3.B, bk(6 + nt), PB[6 + nt], 1.0 / 1024)
                    for k in range(8):
                        stt(hfb[:, k, n0:n0 + 512], xh_[:, k, n0:n0 + 512], gcol(G_FPRE, k), r3[:, n0:n0 + 512],
                            ALU.mult, ALU.mult, [xh_.B, gT.B, r3.B], [hfb.B])

            def epi_ops(hf):
                xh_ = x1h[hf]
                ops = []
                for nt in range(2):
                    n0 = 512 * nt
                    for i in range(8):
                        def f(i=i, n0=n0):
                            stt(y2[:, i, n0:n0 + 512], y2[:, i, n0:n0 + 512], gcol(G_FPOST, i), r4[:, n0:n0 + 512],
                                ALU.mult, ALU.mult, [y2.B, gT.B, r4.B], [y2.B])
                            tt(y2[:, i, n0:n0 + 512], y2[:, i, n0:n0 + 512], xh_[:, i, n0:n0 + 512], ALU.add, [y2.B, xh_.B], [y2.B])
                        ops.append(f)
                return ops

            def gu_phase(hf, inter):
                cnt = 0
                for j in range(22):
                    if j + 2 < 22:
                        load_gu(hf, j + 2)
                    if j == 20:
                        load_wd(hf, 0)
                    if j == 21:
                        load_wd(hf, 1)
                    s = (hf * 22 + j) % 3
                    for nt in range(2):
                        n0 = 512 * nt
                        bg, bu = (0, 1) if cnt % 2 == 0 else (2, 3)
                        sg_ = sg[cnt % 2]
                        cnt += 1
                        pe([(bk(bg), wgu[s][:, k, 0:128], hfb[:, k, n0:n0 + 512], k == 0, k == 7) for k in range(8)],
                           [wgu[s].B, hfb.B], [PB[bg]])
                        pe([(bk(bu), wgu[s][:, k, 128:256], hfb[:, k, n0:n0 + 512], k == 0, k == 7) for k in range(8)],
                           [wgu[s].B, hfb.B], [PB[bu]])
                        act(sg_[:], bk(bg), AF.Silu, [PB[bg]], [sg_.B])
                        tt(actT[:, j, n0:n0 + 512], sg_[:], bk(bu), ALU.mult, [sg_.B, PB[bu]], [actT.B])
                    if inter:
                        inter.pop(0)()
                while inter:
                    inter.pop(0)()

            def down_phase(hf):
                cnt = 0
                for i in range(8):
                    s = (hf * 8 + i) % 2
                    for nt in range(2):
                        n0 = 512 * nt
                        b = cnt % 4
                        sc_ = sqc[cnt % 2]
                        cnt += 1
                        pe([(bk(b), wd[s][:, k, :], actT[:, k, n0:n0 + 512], k == 0, k == 21) for k in range(22)],
                           [wd[s].B, actT.B], [PB[b]])
                        act(y2[:, i, n0:n0 + 512], bk(b), AF.Copy, [PB[b]], [y2.B])
                        act(sc_[:], bk(b), AF.Square, [PB[b]], [sc_.B])
                        pe([(bk(4 + nt), ones[:], sc_[:], i == 0, i == 7)], [ones.B, sc_.B], [PB[4 + nt]])
                    if i + 2 < 8:
                        load_wd(hf, i + 2)
                for nt in range(2):
                    n0 = 512 * nt
                    rstd_ops(r4[:, n0:n0 + 512], r4.B, bk(4 + nt), PB[4 + nt], 1.0 / 1024)

            for hf in range(2):
                S.dma("sp", x1h[hf][:], x1Sv[:, :, 1024 * hf:1024 * hf + 1024], dX1[hf], reads=[Bx1S], writes=[x1h[hf].B])
            load_gu(0, 0)
            load_gu(0, 1)
            prologue(0)
            gu_phase(0, [])
            load_gu(1, 0)
            load_gu(1, 1)
            prologue(1)
            down_phase(0)
            gu_phase(1, epi_ops(0))
            S.dma("sp", outTv[:, :, 0:1024], y2[:], dOut, reads=[y2.B])
            down_phase(1)
            for f in epi_ops(1):
                f()
            S.dma("sp", outTv[:, :, 1024:2048], y2[:], dOut, reads=[y2.B])
            S.barrier()
            S.mark("C2")

        if cut:
            S.cut(cut)
        S.emit(None)
        S.check()
        with nc.Block() as block:
            @block.tensor
            def _(e):
                S.play("pe", e)

            @block.scalar
            def _(e):
                S.play("act", e)

            @block.vector
            def _(e):
                S.play("dve", e)

            @block.gpsimd
            def _(e):
                S.play("pool", e)

            @block.sync
            def _(e):
                S.play("sp", e)
    return nc


def make_inputs(inputs, core):
    x = np.asarray(inputs["x"], np.float32)[0]
    pos = np.asarray(inputs["positions"], np.int32)[0]
    o0 = core * OWN
    xr = np.roll(x, -o0, axis=0)
    xTc = np.ascontiguousarray(xr.T)
    posr = np.ascontiguousarray(np.roll(pos, -o0)[None, :])
    xh = np.zeros((16, 1024), np.float32)
    if o0 - 8 >= 0:
        xh[0:8] = x[o0 - 8:o0]
    if o0 + OWN + 8 <= S_TOK:
        xh[8:16] = x[o0 + OWN:o0 + OWN + 8]
    xhT = np.ascontiguousarray(xh.T)
    return xTc, xhT, posr


_CONSTS = {}


def const_tables(core):
    if core in _CONSTS:
        return _CONSTS[core]
    inv = (1.0 / (10000.0 ** (np.arange(0, 32, 2, dtype=np.float32) / np.float32(32)))).astype(np.float32)
    cst = np.zeros((128, 4), np.float32)
    for p in range(128):
        cst[p, 0] = inv[p % 16]
        cst[p, 1] = -1.0 if (p % 32) < 16 else 1.0
        cst[p, 2] = 0.0
        cst[p, 3] = EPS
    t = np.arange(core * OWN, (core + 1) * OWN)
    invcnt = np.zeros((4, OWN), np.float32)
    for g, w in enumerate(POOL_W):
        left = w // 2
        right = w - left - 1
        cnt = (np.minimum(t + right, S_TOK - 1) - np.maximum(t - left, 0) + 1).astype(np.float32)
        invcnt[g] = (1.0 / cnt).astype(np.float32)
    _CONSTS[core] = (cst, invcnt)
    return _CONSTS[core]


def chunk_cols(v):
    v = np.asarray(v, np.float32)
    return np.ascontiguousarray(v.reshape(-1, 128).T)


_NC = {}


def kernel(x, positions, g_mix_pre, w_in, g_q_lat, w_uq, g_kv_lat, w_ukv, w_o_attn,
           w_pool_group, pool_scale, w_o_pool, w_out, g_mix_post, g_ffn_pre,
           w_gate_up, w_down, g_ffn_post, _debug=False):
    inputs = {"x": x, "positions": positions}
    gains = np.concatenate([chunk_cols(g_mix_pre), chunk_cols(g_q_lat), chunk_cols(g_kv_lat),
                            chunk_cols(pool_scale), chunk_cols(g_mix_post), chunk_cols(g_ffn_pre),
                            chunk_cols(g_ffn_post)], axis=1)
    gains = np.ascontiguousarray(gains, dtype=np.float32)
    assert gains.shape == (128, 41)
    f = lambda a: np.ascontiguousarray(np.asarray(a, np.float32))
    common = dict(gains=gains, w_in=f(w_in), w_uq=f(w_uq), w_ukv=f(w_ukv), w_o_attn=f(w_o_attn),
                  w_pool_group=f(w_pool_group), w_o_pool=f(w_o_pool), w_out=f(w_out),
                  w_gate_up=f(w_gate_up), w_down=f(w_down))
    in_maps = []
    for c in range(NCORES):
        xTc, xhT, posr = make_inputs(inputs, c)
        cst, invcnt = const_tables(c)
        m = dict(common)
        m.update(xT=xTc, xh=xhT, pos=posr, cst=cst, invcnt=invcnt)
        in_maps.append(m)
    key = bool(_debug)
    if key not in _NC:
        _NC[key] = build_program(debug=key)
    nc = _NC[key]
    res = run_bass_kernel_spmd(nc, in_maps, core_ids=list(range(NCORES)))
    if _debug:
        return res
    outT = np.concatenate([np.asarray(r["outT"]) for r in res.results], axis=1)
    return np.ascontiguousarray(outT.T)[None, :, :].astype(np.float32)
```

```python
import math
from contextlib import ExitStack

import numpy as np
import concourse.bass as bass
import concourse.mybir as mybir
from concourse.bass_utils import run_bass_kernel_spmd

ENGS = ("pe", "act", "dve", "pool", "sp")


class Buf:
    __slots__ = ("name", "writers", "readers")

    def __init__(self, name):
        self.name = name
        self.writers = []
        self.readers = []


class DSem:
    __slots__ = ("sem", "count", "final")

    def __init__(self, sem, final=False):
        self.sem = sem
        self.count = 0
        self.final = final


class Op:
    __slots__ = ("eng", "fn", "deps", "signal", "sigval", "dsem", "dval", "idx")

    def __init__(self, eng, fn):
        self.eng = eng
        self.fn = fn
        self.deps = []
        self.signal = False
        self.sigval = None
        self.dsem = None
        self.dval = None


class Sched:
    def __init__(self, nc, stack):
        self.nc = nc
        self.stack = stack
        self.q = {e: [] for e in ENGS}
        self.esem = {e: stack.enter_context(nc.semaphore("es_" + e)) for e in ENGS}
        self.dmas = []
        self.marks = {}
        self.nds = 0

    def mark(self, name):
        self.marks[name] = {e: len(self.q[e]) for e in ENGS}

    def cut(self, name):
        for e in ENGS:
            del self.q[e][self.marks[name][e]:]

    def dsem(self, final=False):
        self.nds += 1
        return DSem(self.stack.enter_context(self.nc.semaphore("ds%d" % self.nds)), final)

    def _track(self, op, reads, writes, group_dma_writes=False):
        deps = set()
        for b in reads:
            for w in b.writers:
                deps.add((w, True))
        for b in writes:
            for w in b.writers:
                if group_dma_writes and w.dsem is not None and not b.readers and w.dsem is op.dsem:
                    continue
                deps.add((w, False))
            for r in b.readers:
                deps.add((r, False))
        for d, raw in deps:
            if d is op:
                continue
            if d.dsem is None and op.dsem is None and d.eng == op.eng:
                if not (raw and op.eng in ("act", "dve", "pool")):
                    continue
            op.deps.append(d)
            if d.dsem is None:
                d.signal = True
        for b in reads:
            b.readers.append(op)
        for b in writes:
            if group_dma_writes and b.writers and all(
                    w.dsem is not None and w.dsem is op.dsem for w in b.writers) and not b.readers:
                b.writers.append(op)
            else:
                b.writers = [op]
                b.readers = []

    def op(self, eng, fn, reads=(), writes=()):
        o = Op(eng, fn)
        self._track(o, reads, writes)
        self.q[eng].append(o)
        return o

    def dma(self, eng, out, in_, dsem, reads=(), writes=()):
        def fn(e, out=out, in_=in_):
            return e.dma_start(out=out, in_=in_)
        o = Op(eng, fn)
        o.dsem = dsem
        dsem.count += 16
        o.dval = dsem.count
        self._track(o, reads, writes, group_dma_writes=True)
        self.q[eng].append(o)
        self.dmas.append(o)
        return o

    def barrier(self):
        lasts = []
        for e in ENGS:
            for o in reversed(self.q[e]):
                if o.dsem is None and o.fn is not None:
                    o.signal = True
                    lasts.append(o)
                    break
        dm = {}
        for o in self.dmas:
            dm[id(o.dsem)] = o
        self.dmas = []
        for e in ENGS:
            def fn(eng):
                return None
            b = Op(e, None)
            for l in lasts:
                if l.eng != e:
                    b.deps.append(l)
            b.deps.extend(dm.values())
            self.q[e].append(b)

    def emit(self, engines):
        for e in ENGS:
            c = 0
            for o in self.q[e]:
                if o.dsem is None and o.signal and o.fn is not None:
                    c += 1
                    o.sigval = c

    def _waits(self, o):
        waits = {}
        for d in o.deps:
            if d.dsem is not None:
                key = id(d.dsem)
                val = d.dsem.count if d.dsem.final else d.dval
                sem = d.dsem.sem
            else:
                key = d.eng
                val = d.sigval
                sem = self.esem[d.eng]
            assert val is not None, (o.eng, d.eng, d.fn)
            if key not in waits or waits[key][1] < val:
                waits[key] = (sem, val)
        return waits

    def check(self):
        cnt = {}
        ptr = {e: 0 for e in ENGS}
        total = sum(len(self.q[e]) for e in ENGS)
        done = 0
        while done < total:
            prog = False
            for e in ENGS:
                while ptr[e] < len(self.q[e]):
                    o = self.q[e][ptr[e]]
                    ok = all(cnt.get(k, 0) >= v for k, (s_, v) in self._waits(o).items())
                    if not ok:
                        break
                    if o.fn is not None:
                        if o.dsem is not None:
                            cnt[id(o.dsem)] = cnt.get(id(o.dsem), 0) + 16
                        elif o.signal:
                            cnt[e] = cnt.get(e, 0) + 1
                    ptr[e] += 1
                    done += 1
                    prog = True
            if not prog:
                st = {e: (ptr[e], len(self.q[e])) for e in ENGS}
                raise RuntimeError("schedule deadlock: %s" % st)
        return True

    def play(self, e, eng):
        seen = {}
        last_real = None
        for o in self.q[e]:
            waits = {}
            for d in o.deps:
                if d.dsem is not None:
                    key = id(d.dsem)
                    val = d.dsem.count if d.dsem.final else d.dval
                    sem = d.dsem.sem
                else:
                    key = d.eng
                    val = d.sigval
                    sem = self.esem[d.eng]
                if key not in waits or waits[key][1] < val:
                    waits[key] = (sem, val)
            for key, (sem, val) in waits.items():
                if seen.get(key, 0) >= val:
                    continue
                seen[key] = val
                eng.wait_ge(sem, val)
            if o.fn is None:
                continue
            ins = o.fn(eng)
            if o.dsem is not None:
                ins.then_inc(o.dsem.sem, 16)
            elif o.signal:
                ins.then_inc(self.esem[e], 1)


F32 = mybir.dt.float32
BF16 = mybir.dt.bfloat16
I32 = mybir.dt.int32
AF = mybir.ActivationFunctionType
ALU = mybir.AluOpType

S_TOK = 16384
OWN = 2048
NCORES = 8
EPS = 1e-6
G_PRE, G_Q, G_KV, G_PS, G_POST, G_FPRE, G_FPOST = 0, 8, 11, 13, 17, 25, 33
POOL_W = (2, 4, 8, 16)


def build_program(debug=False, cut=None, nta=None, nchunks=None):
    nc = bass.Bass("TRN2", target_bir_lowering=False)

    def din(name, shape, dt=F32):
        return nc.dram_tensor(name, shape, dt, kind="ExternalInput").ap()

    xT = din("xT", [1024, S_TOK])
    xh = din("xh", [1024, 16])
    pos = din("pos", [1, S_TOK], I32)
    cst = din("cst", [128, 4])
    gains = din("gains", [128, 41])
    invcnt = din("invcnt", [4, OWN])
    w_in = din("w_in", [1024, 3232])
    w_uq = din("w_uq", [384, 768])
    w_ukv = din("w_ukv", [256, 1024])
    w_o_attn = din("w_o_attn", [512, 1024])
    w_pool_group = din("w_pool_group", [4, 128, 128])
    w_o_pool = din("w_o_pool", [512, 1024])
    w_out = din("w_out", [1024, 1024])
    w_gate_up = din("w_gate_up", [1024, 5632])
    w_down = din("w_down", [2816, 1024])
    outT = nc.dram_tensor("outT", [1024, OWN], F32, kind="ExternalOutput").ap()
    skind = dict(kind="ExternalOutput") if debug else {}
    kS = nc.dram_tensor("kS", [512, S_TOK], BF16, **skind).ap()
    krS = nc.dram_tensor("krS", [32, S_TOK], BF16, **skind).ap()
    vS = nc.dram_tensor("vS", [8, 128, 128, 65], BF16, **skind).ap()
    cosS = nc.dram_tensor("cosS", [32, S_TOK], F32, **skind).ap()
    sinS = nc.dram_tensor("sinS", [32, S_TOK], F32, **skind).ap()
    x1S = nc.dram_tensor("x1S", [1024, OWN], F32, **skind).ap()
    rsD = nc.dram_tensor("rsD", [64, 512], F32).ap()
    if debug:
        qtD = nc.dram_tensor("qtD", [128, 8, OWN], BF16, kind="ExternalOutput").ap()
        atD = nc.dram_tensor("atD", [128, 8, OWN], BF16, kind="ExternalOutput").ap()
        pmD = nc.dram_tensor("pmD", [128, 4, OWN], BF16, kind="ExternalOutput").ap()

    xTv = xT.rearrange("(k p) n -> p k n", p=128)
    xhv = xh.rearrange("(k p) n -> p k n", p=128)
    w_inv = w_in.rearrange("(k p) n -> p k n", p=128)
    outTv = outT.rearrange("(k p) n -> p k n", p=128)
    x1Sv = x1S.rearrange("(k p) n -> p k n", p=128)
    kSv = kS.rearrange("(j q) t -> q j t", q=128)
    vSv = vS.rearrange("h p t d -> p h t d")

    with ExitStack() as gst:
        S = Sched(nc, gst)
        ps = gst.enter_context(nc.psum_tensor("ps", [128, 4096], F32))
        PB = [Buf("pb%d" % b) for b in range(8)]

        def bk(b):
            return ps[:, 512 * b:512 * (b + 1)]

        class T:
            def __init__(self, st, name, shape, dt, nb=1, side=None):
                if side:
                    self.t = st.enter_context(nc.sbuf_tensor("sb_" + name, shape, dt, side=side))
                else:
                    self.t = st.enter_context(nc.sbuf_tensor("sb_" + name, shape, dt))
                self.b = [Buf(name + str(i)) for i in range(nb)]

            def __getitem__(self, k):
                return self.t[k]

            @property
            def B(self):
                return self.b[0]

        def pe(mms, reads, writes):
            def fn(e, mms=mms):
                ins = None
                for (o, l, r, st_, sp_) in mms:
                    ins = e.matmul(o, l, r, start=st_, stop=sp_)
                return ins
            return S.op("pe", fn, reads, writes)

        def act(out, in_, func, reads, writes, scale=1.0, bias=0.0):
            return S.op("act", lambda e: e.activation(out, in_, func, bias=bias, scale=scale), reads, writes)

        def dve(fn, reads, writes):
            return S.op("dve", fn, reads, writes)

        def tt(out, a, b, op, reads, writes, eng="dve"):
            return S.op(eng, lambda e: e.tensor_tensor(out, a, b, op), reads, writes)

        def stt(out, in0, sc, in1, op0, op1, reads, writes):
            return S.op("dve", lambda e: e.scalar_tensor_tensor(out, in0, sc, in1, op0, op1), reads, writes)

        def ts(out, in0, s1, s2, op0, op1, reads, writes):
            if s2 is None:
                return S.op("dve", lambda e: e.tensor_scalar(out, in0, s1, None, op0), reads, writes)
            return S.op("dve", lambda e: e.tensor_scalar(out, in0, s1, s2, op0, op1), reads, writes)

        cstT = T(gst, "cst", [128, 4], F32)
        gT = T(gst, "gains", [128, 41], F32)
        ones = T(gst, "ones", [128, 128], BF16)
        onesf = T(gst, "onesf", [128, 128], F32)
        dC = S.dsem(final=True)
        S.dma("sp", cstT[:], cst, dC, writes=[cstT.B])
        S.dma("sp", gT[:], gains, dC, writes=[gT.B])
        S.op("pool", lambda e: e.memset(ones[:], 1.0), writes=[ones.B])
        S.op("pool", lambda e: e.memset(onesf[:], 1.0), writes=[onesf.B])
        epsb = cstT[:, 3:4]

        def gcol(off, k, p0=0, p1=128):
            return gT[p0:p1, off + k:off + k + 1]

        def rstd_ops(out_ap, out_B, ss_ap, ss_B, sc):
            act(out_ap, ss_ap, AF.Ln, [ss_B, cstT.B], [out_B], scale=sc, bias=epsb)
            act(out_ap, out_ap, AF.Exp, [out_B], [out_B], scale=-0.5)

        def tot_ops(out_ap, out_B, ss_ap, ss_B, a_ap, a_B, nf):
            stt(out_ap, ss_ap, 1.0 / nf, a_ap, ALU.mult, ALU.add, [ss_B, a_B], [out_B])
            act(out_ap, out_ap, AF.Ln, [out_B], [out_B])
            act(out_ap, out_ap, AF.Exp, [out_B], [out_B], scale=-0.5)

        dStage = [S.dsem(), S.dsem()]
        stage_ctr = [0]

        def load_folded(st, stg, dst_fn, src_fn, ncols, goff, q="sp", dsems=None):
            dsems = dsems or dStage
            for k in range(8):
                i = stage_ctr[0] % 2
                stage_ctr[0] += 1
                S.dma(q, stg[i][:, 0:ncols], src_fn(k), dsems[i], writes=[stg[i].B])
                ts(dst_fn(k), stg[i][:, 0:ncols], gcol(goff, k), None, ALU.mult, None,
                   [stg[i].B, gT.B], [])

        with ExitStack() as st:
            I = T(st, "ti", [128, 4096], I32)
            X = T(st, "tx", [128, 4096], F32)
            Y = T(st, "ty", [128, 4096], F32)
            Z = T(st, "tz", [128, 4096], F32)
            W = T(st, "tw", [128, 4096], F32)
            d0 = S.dsem(final=True)
            for s in range(4):
                S.dma("sp", I[32 * s:32 * s + 32, :],
                      bass.AP(pos.tensor, 4096 * s, [[0, 32], [1, 4096]]),
                      d0, writes=[I.B])
            C1 = 6.28125
            C2 = 2 * math.pi - C1
            dve(lambda e: e.tensor_copy(X[:], I[:]), [I.B], [X.B])
            ts(X[:], X[:], cstT[:, 0:1], None, ALU.mult, None, [X.B, cstT.B], [X.B])
            ts(I[:], X[:], 1.0 / (2 * math.pi), None, ALU.mult, None, [X.B], [I.B])
            dve(lambda e: e.tensor_copy(Y[:], I[:]), [I.B], [Y.B])
            stt(Z[:], Y[:], -C1, X[:], ALU.mult, ALU.add, [Y.B, X.B], [Z.B])
            stt(Z[:], Y[:], -C2, Z[:], ALU.mult, ALU.add, [Y.B, Z.B], [Z.B])
            ts(Y[:], Z[:], math.pi, -2 * math.pi, ALU.is_gt, ALU.mult, [Z.B], [Y.B])
            tt(Z[:], Z[:], Y[:], ALU.add, [Z.B, Y.B], [Z.B])
            act(X[:], Z[:], AF.Sin, [Z.B, cstT.B], [X.B], scale=cstT[:, 1:2])
            dT = S.dsem(final=True)
            BtabS = Buf("tabS")
            for s in range(4):
                S.dma("sp", sinS[:, 4096 * s:4096 * (s + 1)], X[32 * s:32 * s + 32, :], dT, reads=[X.B], writes=[BtabS])
            ts(Y[:], Z[:], math.pi / 2, None, ALU.add, None, [Z.B], [Y.B])
            ts(W[:], Y[:], math.pi, -2 * math.pi, ALU.is_gt, ALU.mult, [Y.B], [W.B])
            tt(Y[:], Y[:], W[:], ALU.add, [Y.B, W.B], [Y.B])
            act(W[:], Y[:], AF.Sin, [Y.B], [W.B])
            for s in range(4):
                S.dma("sp", cosS[:, 4096 * s:4096 * (s + 1)], W[32 * s:32 * s + 32, :], dT, reads=[W.B], writes=[BtabS])
            S.barrier()
            S.mark("p0")

        def prep_tile(xb, sq, r1, a1, src_ap, n, dX, need_a=True):
            S.dma("pool", xb[:, :, 0:n], src_ap, dX, writes=[xb.B])
            act(sq[:, :, 0:n], xb[:, :, 0:n], AF.Square, [xb.B], [sq.B])
            pe([(bk(0)[:, 0:n], ones[:], sq[:, k, 0:n], k == 0, k == 7) for k in range(8)],
               [sq.B, ones.B], [PB[0]])
            rstd_ops(r1[:, 0:n], r1.B, bk(0)[:, 0:n], PB[0], 1.0 / 1024)
            if need_a:
                ts(a1[:, 0:n], bk(0)[:, 0:n], EPS / 1024, EPS * EPS, ALU.mult, ALU.add, [PB[0], r1.B], [a1.B])

        with ExitStack() as bq:
            AT = T(bq, "AT", [128, 8, OWN], BF16)
            PM = T(bq, "PM", [128, 4, OWN], BF16)
            qsc = ExitStack()
            QT = T(qsc, "QT", [128, 8, OWN], BF16)

            qw = ExitStack()
            stgq2 = [T(qw, "stgq2%d" % i, [128, 512], F32, side="right") for i in range(2)]
            wcq = T(qw, "wcq", [128, 8, 384], BF16, side="right")
            wu = T(qw, "wu", [128, 8, 512], BF16, side="right")
            wq = T(qw, "wq", [128, 3, 768], BF16, side="right")
            wqr = T(qw, "wqr", [128, 3, 8, 96], BF16, side="right")
            wpg = T(qw, "wpg", [128, 4, 128], BF16, side="right")
            dWq = S.dsem(final=True)
            dStq = [S.dsem(), S.dsem()]

            def load_q_weights():
                st = None
                stg = stgq2
                dW = dWq
                load_folded(st, stg, lambda k: wcq[:, k, :], lambda k: w_inv[:, k, 0:384], 384, G_PRE, dsems=dStq)
                wcq.B.writers = [S.q["dve"][-1]]
                load_folded(st, stg, lambda k: wu[:, k, :], lambda k: w_inv[:, k, 672:1184], 512, G_PRE, dsems=dStq)
                wu.B.writers = [S.q["dve"][-1]]
                S.dma("pool", wq[:], w_uq.rearrange("(k p) n -> p k n", p=128), dW, writes=[wq.B])
                S.op("pool", lambda e: e.memset(wqr[:], 0.0), writes=[wqr.B])
                uqv = w_uq.rearrange("(k p) (h d) -> p k h d", p=128, d=96)
                for c in range(3):
                    S.dma("pool", wqr[:, c, :, 64:80], uqv[:, c, :, 80:96], dW, writes=[wqr.B])
                    S.dma("pool", wqr[:, c, :, 80:96], uqv[:, c, :, 64:80], dW, writes=[wqr.B])
                S.dma("pool", wpg[:], w_pool_group.rearrange("g c d -> c g d"), dW, writes=[wpg.B])


            with ExitStack() as st:
                stg = [T(st, "stg%d" % i, [128, 512], F32) for i in range(2)]
                wkv = T(st, "wkv", [128, 8, 320], BF16)
                wk = T(st, "wk", [128, 2, 512], BF16)
                wv = T(st, "wv", [128, 2, 512], BF16)
                xb = [T(st, "xb%d" % i, [128, 8, 512], BF16) for i in range(2)]
                sq = [T(st, "sq%d" % i, [128, 8, 512], BF16) for i in range(2)]
                r1 = [T(st, "r1%d" % i, [128, 512], F32) for i in range(2)]
                a1 = [T(st, "a1%d" % i, [128, 512], F32) for i in range(2)]
                sqr = [T(st, "sqr%d" % i, [128, 2, 512], BF16) for i in range(2)]
                tot = T(st, "tot", [128, 512], F32)
                ckvn = [T(st, "ckvn%d" % i, [128, 2, 512], BF16) for i in range(2)]
                kst = [T(st, "kst%d" % i, [128, 4, 512], BF16) for i in range(2)]
                vst = [T(st, "vst%d" % i, [128, 8, 4, 65], BF16) for i in range(2)]
                ct = [T(st, "ct%d" % i, [32, 512], F32) for i in range(2)]
                sn = [T(st, "sn%d" % i, [32, 512], F32) for i in range(2)]
                krs = [T(st, "krs%d" % i, [32, 512], BF16) for i in range(2)]
                t1 = T(st, "t1", [32, 512], F32)
                t2 = T(st, "t2", [32, 512], F32)
                dW = S.dsem(final=True)
                dX = [S.dsem(), S.dsem()]
                dTab = [S.dsem(), S.dsem()]
                dTabS = [S.dsem(), S.dsem()]
                dKs = [S.dsem(), S.dsem()]
                dVs = [S.dsem(), S.dsem()]
                dKr = [S.dsem(), S.dsem()]
                BscrK, BscrV, BscrR = Buf("scrK"), Buf("scrV"), Buf("scrR")

                load_folded(st, stg, lambda k: wkv[:, k, 0:288], lambda k: w_inv[:, k, 384:672], 288, G_PRE)
                load_folded(st, stg, lambda k: wkv[:, k, 288:304], lambda k: w_inv[:, k, 656:672], 16, G_PRE)
                load_folded(st, stg, lambda k: wkv[:, k, 304:320], lambda k: w_inv[:, k, 640:656], 16, G_PRE)
                wkv.B.writers = [S.q["dve"][-1]]
                ukv = w_ukv.rearrange("(k p) (h two d) -> p k h two d", p=128, two=2, d=64)
                for kc in range(2):
                    S.dma("pool", wk[:, kc, :].rearrange("p (h d) -> p h d", d=64), ukv[:, kc, :, 0, :], dW, writes=[wk.B])
                    S.dma("pool", wv[:, kc, :].rearrange("p (h d) -> p h d", d=64), ukv[:, kc, :, 1, :], dW, writes=[wv.B])
                for i in range(2):
                    S.op("pool", lambda e, i=i: e.memset(vst[i][:], 1.0), writes=[vst[i].B])

                NT = nta or (S_TOK // 512)

                def pre(i):
                    p = i % 2
                    c0 = 512 * i
                    S.dma("pool", xb[p][:], xTv[:, :, c0:c0 + 512], dX[p], writes=[xb[p].B])
                    act(sq[p][:], xb[p][:], AF.Square, [xb[p].B], [sq[p].B])

                def stage1(i):
                    p = i % 2
                    c0 = 512 * i
                    pe([(bk(0), ones[:], sq[p][:, k, :], k == 0, k == 7) for k in range(8)],
                       [sq[p].B, ones.B], [PB[0]])
                    rstd_ops(r1[p][:], r1[p].B, bk(0), PB[0], 1.0 / 1024)
                    ts(a1[p][:], bk(0), EPS / 1024, EPS * EPS, ALU.mult, ALU.add, [PB[0], r1[p].B], [a1[p].B])
                    S.dma("sp", ct[p][:], cosS[:, c0:c0 + 512], dTab[p], reads=[BtabS], writes=[ct[p].B])
                    S.dma("sp", sn[p][:], sinS[:, c0:c0 + 512], dTabS[p], reads=[BtabS], writes=[sn[p].B])
                    for c in range(2):
                        pe([(bk(1 + c), wkv[:, k, 128 * c:128 * c + 128], xb[p][:, k, :], k == 0, k == 7) for k in range(8)],
                           [wkv.B, xb[p].B], [PB[1 + c]])
                        act(sqr[p][:, c, :], bk(1 + c), AF.Square, [PB[1 + c]], [sqr[p].B])
                    pe([(bk(3)[0:32, :], wkv[:, k, 256:288], xb[p][:, k, :], k == 0, k == 7) for k in range(8)],
                       [wkv.B, xb[p].B], [PB[3]])
                    pe([(bk(4)[0:32, :], wkv[:, k, 288:320], xb[p][:, k, :], k == 0, k == 7) for k in range(8)],
                       [wkv.B, xb[p].B], [PB[4]])
                    tt(t1[:], bk(3)[0:32, :], ct[p][:], ALU.mult, [PB[3], ct[p].B], [t1.B])
                    tt(t2[:], bk(4)[0:32, :], sn[p][:], ALU.mult, [PB[4], sn[p].B], [t2.B])
                    tt(t1[:], t1[:], t2[:], ALU.add, [t1.B, t2.B], [t1.B])
                    tt(krs[p][:], t1[:], r1[p][0:32, :], ALU.mult, [t1.B, r1[p].B], [krs[p].B])
                    S.dma("sp", krS[:, c0:c0 + 512], krs[p][:], dKr[p], reads=[krs[p].B], writes=[BscrR])

                def stage2(i):
                    p = i % 2
                    pe([(bk(0), ones[:], sqr[p][:, c, :], c == 0, c == 1) for c in range(2)],
                       [ones.B, sqr[p].B], [PB[0]])
                    tot_ops(tot[:], tot.B, bk(0), PB[0], a1[p][:], a1[p].B, 256)
                    for c in range(2):
                        stt(ckvn[p][:, c, :], bk(1 + c), gcol(G_KV, c), tot[:], ALU.mult, ALU.mult,
                            [PB[1 + c], gT.B, tot.B], [ckvn[p].B])

                rot = [0]

                def rb():
                    b = 5 + rot[0] % 3
                    rot[0] += 1
                    return b

                def stage3(i):
                    p = i % 2
                    c0 = 512 * i
                    for j in range(4):
                        b = rb()
                        pe([(bk(b), wk[:, kc, 128 * j:128 * j + 128], ckvn[p][:, kc, :], kc == 0, kc == 1) for kc in range(2)],
                           [wk.B, ckvn[p].B], [PB[b]])
                        act(kst[p][:, j, :], bk(b), AF.Copy, [PB[b]], [kst[p].B])
                    for s in range(4):
                        b = rb()
                        pe([(bk(b), ckvn[p][:, kc, 128 * s:128 * s + 128], wv[:, kc, :], kc == 0, kc == 1) for kc in range(2)],
                           [wv.B, ckvn[p].B], [PB[b]])
                        act(vst[p][:, :, s, 0:64], bk(b).rearrange("p (h d) -> p h d", d=64), AF.Copy, [PB[b]], [vst[p].B])
                    S.dma("sp", kSv[:, :, c0:c0 + 512], kst[p][:], dKs[p], reads=[kst[p].B], writes=[BscrK])
                    S.dma("sp", vSv[:, :, 4 * i:4 * i + 4, :], vst[p][:], dVs[p], reads=[vst[p].B], writes=[BscrV])

                def dbg_mark(nm):
                    if cut == nm:
                        S.barrier()
                        S.mark(nm)
                dbg_mark("Aw")
                pre(0)
                if NT > 1:
                    pre(1)
                stage1(0)
                load_q_weights()
                dbg_mark("A1")
                for i in range(NT):
                    stage2(i)
                    if i == 0:
                        dbg_mark("A2")
                    if i + 1 < NT:
                        stage1(i + 1)
                    if i + 2 < NT:
                        pre(i + 2)
                    stage3(i)
                    if i == 0:
                        dbg_mark("A3")
                S.barrier()
                S.mark("A")

            with ExitStack() as st:
                stg = [T(st, "stgq%d" % i, [128, 512], F32) for i in range(2)]
                xb = T(st, "xbq", [128, 8, 512], BF16)
                sq = T(st, "sqq", [128, 8, 512], BF16)
                r1 = T(st, "r1q", [128, 512], F32)
                a1 = T(st, "a1q", [128, 512], F32)
                uT = T(st, "uT", [128, 4, OWN + 16], F32)
                st1 = ExitStack()
                cqraw = T(st1, "cqraw", [128, 3, 512], F32)
                sqcq = T(st1, "sqcq", [128, 3, 512], BF16)
                totq = T(st1, "totq", [128, 512], F32)
                cqn = T(st1, "cqn", [128, 3, 512], BF16)
                ctq = T(st1, "ctq", [128, 512], F32)
                snq = T(st1, "snq", [128, 512], F32)
                tq1 = [T(st1, "tq1%d" % i, [128, 512], F32) for i in range(2)]
                tq2 = [T(st1, "tq2%d" % i, [128, 512], F32) for i in range(2)]
                dW = S.dsem(final=True)
                dX = S.dsem()
                dTab = S.dsem()
                dTabS = S.dsem()
                dIc = S.dsem()

                rot = [0]

                def rb():
                    b = 1 + rot[0] % 7
                    rot[0] += 1
                    return b

                def u_part(n, dst_fns):
                    for g in range(4):
                        b = rb()
                        pe([(bk(b)[:, 0:n], wu[:, k, 128 * g:128 * g + 128], xb[:, k, 0:n], k == 0, k == 7) for k in range(8)],
                           [wu.B, xb.B], [PB[b]])
                        for (lo, hi, dfn) in dst_fns:
                            tt(dfn(g), bk(b)[:, lo:hi], r1[:, lo:hi], ALU.mult, [PB[b], r1.B], [uT.B])

                def dbg_markq(nm):
                    if cut == nm:
                        S.barrier()
                        S.mark(nm)
                dbg_markq("Qw")
                prep_tile(xb, sq, r1, a1, xhv, 16, dX, need_a=False)
                u_part(16, [(0, 8, lambda g: uT[:, g, 0:8]), (8, 16, lambda g: uT[:, g, OWN + 8:OWN + 16])])

                dbg_markq("Qh")
                for i in range(4):
                    if i == 1:
                        dbg_markq("Q0")
                    c0 = 512 * i
                    prep_tile(xb, sq, r1, a1, xTv[:, :, c0:c0 + 512], 512, dX)
                    S.dma("sp", ctq[64:96, :], cosS[:, c0:c0 + 512], dTab, reads=[BtabS], writes=[ctq.B])
                    S.dma("sp", snq[64:96, :], sinS[:, c0:c0 + 512], dTabS, reads=[BtabS], writes=[snq.B])
                    u_part(512, [(0, 512, lambda g, c0=c0: uT[:, g, 8 + c0:8 + c0 + 512])])
                    for c in range(3):
                        b = rb()
                        pe([(bk(b), wcq[:, k, 128 * c:128 * c + 128], xb[:, k, :], k == 0, k == 7) for k in range(8)],
                           [wcq.B, xb.B], [PB[b]])
                        act(sqcq[:, c, :], bk(b), AF.Square, [PB[b]], [sqcq.B])
                        dve(lambda e, b=b, c=c: e.tensor_copy(cqraw[:, c, :], bk(b)), [PB[b], sqcq.B], [cqraw.B])
                    b = rb()
                    pe([(bk(b), ones[:], sqcq[:, c, :], c == 0, c == 2) for c in range(3)], [ones.B, sqcq.B], [PB[b]])
                    tot_ops(totq[:], totq.B, bk(b), PB[b], a1[:], a1.B, 384)
                    for c in range(3):
                        stt(cqn[:, c, :], cqraw[:, c, :], gcol(G_Q, c), totq[:], ALU.mult, ALU.mult,
                            [cqraw.B, gT.B, totq.B], [cqn.B])
                    for h in range(8):
                        b1 = rb()
                        pe([(bk(b1)[0:96, :], wq[:, c, 96 * h:96 * h + 96], cqn[:, c, :], c == 0, c == 2) for c in range(3)],
                           [wq.B, cqn.B], [PB[b1]])
                        b2 = rb()
                        pe([(bk(b2)[0:96, :], wqr[:, c, h, :], cqn[:, c, :], c == 0, c == 2) for c in range(3)],
                           [wqr.B, cqn.B], [PB[b2]])
                        x1_, x2_ = tq1[h % 2], tq2[h % 2]
                        act(x1_[0:96, :], bk(b1)[0:96, :], AF.Copy, [PB[b1]], [x1_.B])
                        act(x2_[0:96, :], bk(b2)[0:96, :], AF.Copy, [PB[b2]], [x2_.B])
                        act(QT[0:64, h, c0:c0 + 512], bk(b1)[0:64, :], AF.Copy, [PB[b1]], [QT.B])
                        tt(x1_[64:96, :], x1_[64:96, :], ctq[64:96, :], ALU.mult, [x1_.B, ctq.B], [x1_.B])
                        tt(x2_[64:96, :], x2_[64:96, :], snq[64:96, :], ALU.mult, [x2_.B, snq.B], [x2_.B])
                        tt(QT[64:96, h, c0:c0 + 512], x1_[64:96, :], x2_[64:96, :], ALU.add, [x1_.B, x2_.B], [QT.B])

                S.barrier()
                S.mark("Q1")
                st1.close()
                TA = T(st, "TA", [128, OWN + 16], F32)
                TB = T(st, "TB", [128, OWN + 16], F32)
                invc = T(st, "invc", [128, OWN], F32)
                pooled = T(st, "pooled", [128, 4, OWN], BF16)
                L = OWN + 16

                def sh(dst, src, lo, hi, d1, d2):
                    return lambda e: e.tensor_tensor(dst[:, lo:hi], src[:, lo + d1:hi + d1], src[:, lo + d2:hi + d2], ALU.add)

                for g in range(4):
                    ug = uT.t[:, g, :]
                    S.dma("sp", invc[:], bass.AP(invcnt.tensor, OWN * g, [[0, 128], [1, OWN]]), dIc, writes=[invc.B])
                    S.op("pool", sh(TA, ug, 1, L, -1, 0), [uT.B], [TA.B])
                    win = TA
                    if g >= 1:
                        S.op("pool", sh(TB, TA, 2, L - 1, -1, 1), [TA.B], [TB.B])
                        win = TB
                    if g >= 2:
                        S.op("pool", sh(TA, TB, 4, L - 3, -2, 2), [TB.B], [TA.B])
                        win = TA
                    if g >= 3:
                        S.op("pool", sh(TB, TA, 8, L - 7, -4, 4), [TA.B], [TB.B])
                        win = TB
                    other = TB if win is TA else TA
                    tt(other[:, 8:8 + OWN], win[:, 8:8 + OWN], invc[:], ALU.mult, [win.B, invc.B], [other.B])
                    tt(pooled[:, g, :], other[:, 8:8 + OWN], ug[:, 8:8 + OWN], ALU.subtract, [other.B, uT.B], [pooled.B])
                    for nt in range(4):
                        b = rb()
                        pe([(bk(b), wpg[:, g, :], pooled[:, g, 512 * nt:512 * nt + 512], True, True)], [wpg.B, pooled.B], [PB[b]])
                        act(PM[:, g, 512 * nt:512 * nt + 512], bk(b), AF.Identity, [PB[b], gT.B], [PM.B], scale=gcol(G_PS, g))
                if debug:
                    dD = S.dsem(final=True)
                    S.dma("sp", qtD[0:96], QT[0:96], dD, reads=[QT.B])
                    S.dma("sp", pmD, PM[:], dD, reads=[PM.B])
                S.barrier()
                S.mark("Q")
            qw.close()

            cw = ExitStack()
            stgc2 = [T(cw, "stgc2%d" % i, [128, 1024], F32, side="right") for i in range(2)]
            wg = T(cw, "wg", [128, 8, 2048], BF16, side="right")
            woa = T(cw, "woa", [64, 8, 1024], BF16, side="right")
            wob = T(cw, "wob", [128, 4, 1024], BF16, side="right")
            wout = T(cw, "wout", [128, 8, 1024], BF16, side="right")
            dWc = S.dsem(final=True)
            dStc = [S.dsem(), S.dsem()]

            def load_c1_weights():
                st = None
                stg = stgc2
                dW = dWc
                load_folded(st, stg, lambda k: wg[:, k, 0:1024], lambda k: w_inv[:, k, 1184:2208], 1024, G_PRE, q="pool", dsems=dStc)
                load_folded(st, stg, lambda k: wg[:, k, 1024:2048], lambda k: w_inv[:, k, 2208:3232], 1024, G_PRE, q="pool", dsems=dStc)
                wg.B.writers = [S.q["dve"][-1]]
                S.dma("pool", woa[:], w_o_attn.rearrange("(h d) n -> d h n", d=64), dW, writes=[woa.B])
                S.dma("pool", wob[:], w_o_pool.rearrange("(c p) n -> p c n", p=128), dW, writes=[wob.B])
                for k in range(8):
                    S.dma("pool", wout[:, k, :], w_out[128 * k:128 * k + 128, :], dW, writes=[wout.B])

            with ExitStack() as st:
                NSL = 4
                NSB = 3
                NPB = 4
                LOOK = 2
                kc = [T(st, "kc%d" % i, [128, 2048], BF16) for i in range(NSL)]
                vc = [T(st, "vc%d" % i, [128, 16, 65], BF16) for i in range(NSL)]
                Pb = [T(st, "P%d" % i, [128, 1024], BF16) for i in range(NPB)]
                osb = [T(st, "osb%d" % i, [128, 512], F32) for i in range(2)]
                rcb = [T(st, "rcb%d" % i, [64, 512], F32) for i in range(2)]
                dRs = [S.dsem(), S.dsem()]
                dRc = [S.dsem(), S.dsem()]
                BrsD = [Buf("rsD%d" % i) for i in range(64)]
                dSl = [S.dsem() for _ in range(NSL)]
                dSlV = [S.dsem() for _ in range(NSL)]
                scale = 96 ** -0.5
                chunks = [(h, qh, c) for h in range(8) for qh in range(2) for c in range(8)]
                if nchunks:
                    chunks = chunks[:nchunks]

                def load_chunk(n):
                    h, qh, c = chunks[n]
                    s = n % NSL
                    S.dma("sp", kc[s][0:64, :], kS[64 * h:64 * h + 64, 2048 * c:2048 * (c + 1)], dSl[s], reads=[BscrK], writes=[kc[s].B])
                    S.dma("sp", kc[s][64:96, :], krS[:, 2048 * c:2048 * (c + 1)], dSl[s], reads=[BscrR], writes=[kc[s].B])
                    S.dma("sp", vc[s][:], vS[h, :, 16 * c:16 * c + 16, :], dSlV[s], reads=[BscrV], writes=[vc[s].B])

                for n in range(min(3, len(chunks))):
                    load_chunk(n)
                load_c1_weights()
                step = [0]
                pend = []

                def emit_pv(pv):
                    (s, kt, pb, first, last) = pv
                    pe([(ps[0:65, 512 * q:512 * (q + 1)], vc[s][:, kt, :], Pb[pb][:, 512 * q:512 * q + 512], first, last)
                        for q in range(2)],
                       [vc[s].B, Pb[pb].B], [PB[0], PB[1]])

                for n, (h, qh, c) in enumerate(chunks):
                    s = n % NSL
                    q0 = 1024 * qh
                    for kt in range(16):
                        sb_ = step[0] % NSB
                        pb = step[0] % NPB
                        step[0] += 1
                        b0 = 2 + 2 * sb_
                        pe([(bk(b0 + q), kc[s][0:96, 128 * kt:128 * kt + 128], QT[0:96, h, q0 + 512 * q:q0 + 512 * q + 512], True, True)
                            for q in range(2)],
                           [kc[s].B, QT.B], [PB[b0], PB[b0 + 1]])
                        act(Pb[pb][:], ps[:, 512 * b0:512 * b0 + 1024], AF.Exp, [PB[b0], PB[b0 + 1]], [Pb[pb].B], scale=scale)
                        pend.append((s, kt, pb, (c == 0 and kt == 0), (c == 7 and kt == 15)))
                        if len(pend) > LOOK:
                            emit_pv(pend.pop(0))
                        if kt == LOOK and n + 3 < len(chunks):
                            load_chunk(n + 3)
                    if c == 7:
                        while pend:
                            emit_pv(pend.pop(0))
                        grp = 2 * h + qh
                        for qb in range(2):
                            o_ = osb[qb]
                            g_ = 2 * grp + qb
                            act(o_[0:65, :], bk(qb)[0:65, :], AF.Copy, [PB[qb]], [o_.B])
                            dve(lambda e, o_=o_: e.reciprocal(o_[64:65, :], o_[64:65, :]), [o_.B], [o_.B])
                            S.dma("sp", rsD[g_:g_ + 1, :], o_[64:65, :], dRs[qb], reads=[o_.B], writes=[BrsD[g_]])
                            S.dma("sp", rcb[qb][0:64, :], bass.AP(rsD.tensor, 512 * g_, [[0, 64], [1, 512]]), dRc[qb],
                                  reads=[BrsD[g_]], writes=[rcb[qb].B])
                            tt(AT[0:64, h, q0 + 512 * qb:q0 + 512 * qb + 512], o_[0:64, :], rcb[qb][0:64, :], ALU.mult,
                               [o_.B, rcb[qb].B], [AT.B])
                if debug:
                    dD2 = S.dsem(final=True)
                    S.dma("sp", atD[0:64], AT[0:64], dD2, reads=[AT.B])
                S.barrier()
                S.mark("B")

            qsc.close()
            with ExitStack() as st:
                xb = T(st, "xbc", [128, 8, 512], BF16)
                sq = T(st, "sqc", [128, 8, 512], BF16)
                xf = T(st, "xfc", [128, 8, 512], F32)
                r1 = T(st, "r1c", [128, 512], F32)
                r2 = T(st, "r2c", [128, 512], F32)
                zA = [T(st, "zA%d" % i, [128, 512], F32) for i in range(2)]
                zB = [T(st, "zB%d" % i, [128, 512], F32) for i in range(2)]
                m = T(st, "m", [128, 8, 512], BF16)
                y = T(st, "y", [128, 8, 512], F32)
                sqy = [T(st, "sqy%d" % i, [128, 512], BF16) for i in range(2)]
                dW = S.dsem(final=True)
                dX = S.dsem()
                dXf = S.dsem()
                dO = S.dsem()
                Bx1S = Buf("x1S")
                for i in range(4):
                    c0 = 512 * i
                    S.dma("sp", xf[:], xTv[:, :, c0:c0 + 512], dXf, writes=[xf.B])
                    prep_tile(xb, sq, r1, None, xTv[:, :, c0:c0 + 512], 512, dX, need_a=False)
                    for j in range(8):
                        bA, bB, ba, bb = (1, 2, 3, 4) if j % 2 == 0 else (5, 6, 7, 4)
                        z1, z2 = zA[j % 2], zB[j % 2]
                        pe([(bk(bA), wg[:, k, 128 * j:128 * j + 128], xb[:, k, :], k == 0, k == 7) for k in range(8)],
                           [wg.B, xb.B], [PB[bA]])
                        pe([(bk(bB), wg[:, k, 1024 + 128 * j:1024 + 128 * j + 128], xb[:, k, :], k == 0, k == 7) for k in range(8)],
                           [wg.B, xb.B], [PB[bB]])
                        pe([(bk(ba), woa[0:64, h, 128 * j:128 * j + 128], AT[0:64, h, c0:c0 + 512], h == 0, h == 7) for h in range(8)],
                           [woa.B, AT.B], [PB[ba]])
                        tt(z1[:], bk(bA), r1[:], ALU.mult, [PB[bA], r1.B], [z1.B])
                        act(z1[:], z1[:], AF.Sigmoid, [z1.B], [z1.B])
                        tt(z1[:], z1[:], bk(ba), ALU.mult, [z1.B, PB[ba]], [z1.B])
                        pe([(bk(bb), wob[:, c, 128 * j:128 * j + 128], PM[:, c, c0:c0 + 512], c == 0, c == 3) for c in range(4)],
                           [wob.B, PM.B], [PB[bb]])
                        tt(z2[:], bk(bB), r1[:], ALU.mult, [PB[bB], r1.B], [z2.B])
                        act(z2[:], z2[:], AF.Sigmoid, [z2.B], [z2.B])
                        tt(z2[:], z2[:], bk(bb), ALU.mult, [z2.B, PB[bb]], [z2.B])
                        tt(m[:, j, :], z1[:], z2[:], ALU.add, [z1.B, z2.B], [m.B])
                    for j in range(8):
                        b = 1 + j % 6
                        pe([(bk(b), wout[:, k, 128 * j:128 * j + 128], m[:, k, :], k == 0, k == 7) for k in range(8)],
                           [wout.B, m.B], [PB[b]])
                        act(y[:, j, :], bk(b), AF.Copy, [PB[b]], [y.B])
                        sy = sqy[j % 2]
                        act(sy[:], bk(b), AF.Square, [PB[b]], [sy.B])
                        pe([(bk(7), ones[:], sy[:], j == 0, j == 7)], [ones.B, sy.B], [PB[7]])
                    rstd_ops(r2[:], r2.B, bk(7), PB[7], 1.0 / 1024)
                    for j in range(8):
                        stt(y[:, j, :], y[:, j, :], gcol(G_POST, j), r2[:], ALU.mult, ALU.mult, [y.B, gT.B, r2.B], [y.B])
                        tt(y[:, j, :], y[:, j, :], xf[:, j, :], ALU.add, [y.B, xf.B], [y.B])
                    S.dma("sp", x1Sv[:, :, c0:c0 + 512], y[:], dO, reads=[y.B], writes=[Bx1S])
                S.barrier()
                S.mark("C1")
            cw.close()

        with ExitStack() as st:
            x1h = [T(st, "x1h%d" % i, [128, 8, 1024], F32) for i in range(2)]
            hfb = T(st, "hfb", [128, 8, 1024], BF16)
            actT = T(st, "actT", [128, 22, 1024], BF16)
            y2 = T(st, "y2", [128, 8, 1024], F32)
            wgu = [T(st, "wgu%d" % i, [128, 8, 256], BF16) for i in range(3)]
            wd = [T(st, "wd%d" % i, [128, 22, 128], BF16) for i in range(2)]
            sqc = [T(st, "sqf%d" % i, [128, 512], BF16) for i in range(2)]
            sg = [T(st, "sg%d" % i, [128, 512], F32) for i in range(2)]
            r3 = T(st, "r3", [128, 1024], F32)
            r4 = T(st, "r4", [128, 1024], F32)
            dX1 = [S.dsem(), S.dsem()]
            dGU = [S.dsem() for _ in range(3)]
            dWD = [S.dsem() for _ in range(2)]
            dOut = S.dsem()
            wguv = w_gate_up.rearrange("(k p) n -> p k n", p=128)
            wdv = w_down.rearrange("(k p) n -> p k n", p=128)

            def load_gu(hf, j):
                s = (hf * 22 + j) % 3
                S.dma("pool", wgu[s][:, :, 0:128], wguv[:, :, 128 * j:128 * j + 128], dGU[s], writes=[wgu[s].B])
                S.dma("pool", wgu[s][:, :, 128:256], wguv[:, :, 2816 + 128 * j:2816 + 128 * j + 128], dGU[s], writes=[wgu[s].B])

            def load_wd(hf, i):
                s = (hf * 8 + i) % 2
                S.dma("pool", wd[s][:], wdv[:, :, 128 * i:128 * i + 128], dWD[s], writes=[wd[s].B])

            def prologue(hf):
                xh_ = x1h[hf]
                for nt in range(2):
                    n0 = 512 * nt
                    for k in range(8):
                        sc_ = sqc[k % 2]
                        act(sc_[:], xh_[:, k, n0:n0 + 512], AF.Square, [xh_.B], [sc_.B])
                        pe([(bk(6 + nt), ones[:], sc_[:], k == 0, k == 7)], [ones.B, sc_.B], [PB[6 + nt]])
                    rstd_ops(r3[:, n0:n0 + 512], r3.B, bk(6 + nt), PB[6 + nt], 1.0 / 1024)
                    for k in range(8):
                        stt(hfb[:, k, n0:n0 + 512], xh_[:, k, n0:n0 + 512], gcol(G_FPRE, k), r3[:, n0:n0 + 512],
                            ALU.mult, ALU.mult, [xh_.B, gT.B, r3.B], [hfb.B])

            def epi_ops(hf):
                xh_ = x1h[hf]
                ops = []
                for nt in range(2):
                    n0 = 512 * nt
                    for i in range(8):
                        def f(i=i, n0=n0):
                            stt(y2[:, i, n0:n0 + 512], y2[:, i, n0:n0 + 512], gcol(G_FPOST, i), r4[:, n0:n0 + 512],
                                ALU.mult, ALU.mult, [y2.B, gT.B, r4.B], [y2.B])
                            tt(y2[:, i, n0:n0 + 512], y2[:, i, n0:n0 + 512], xh_[:, i, n0:n0 + 512], ALU.add, [y2.B, xh_.B], [y2.B])
                        ops.append(f)
                return ops

            def gu_phase(hf, inter):
                cnt = 0
                for j in range(22):
                    if j + 2 < 22:
                        load_gu(hf, j + 2)
                    if j == 20:
                        load_wd(hf, 0)
                    if j == 21:
                        load_wd(hf, 1)
                    s = (hf * 22 + j) % 3
                    for nt in range(2):
                        n0 = 512 * nt
                        bg, bu = (0, 1) if cnt % 2 == 0 else (2, 3)
                        sg_ = sg[cnt % 2]
                        cnt += 1
                        pe([(bk(bg), wgu[s][:, k, 0:128], hfb[:, k, n0:n0 + 512], k == 0, k == 7) for k in range(8)],
                           [wgu[s].B, hfb.B], [PB[bg]])
                        pe([(bk(bu), wgu[s][:, k, 128:256], hfb[:, k, n0:n0 + 512], k == 0, k == 7) for k in range(8)],
                           [wgu[s].B, hfb.B], [PB[bu]])
                        act(sg_[:], bk(bg), AF.Silu, [PB[bg]], [sg_.B])
                        tt(actT[:, j, n0:n0 + 512], sg_[:], bk(bu), ALU.mult, [sg_.B, PB[bu]], [actT.B])
                    if inter:
                        inter.pop(0)()
                while inter:
                    inter.pop(0)()

            def down_phase(hf):
                cnt = 0
                for i in range(8):
                    s = (hf * 8 + i) % 2
                    for nt in range(2):
                        n0 = 512 * nt
                        b = cnt % 4
                        sc_ = sqc[cnt % 2]
                        cnt += 1
                        pe([(bk(b), wd[s][:, k, :], actT[:, k, n0:n0 + 512], k == 0, k == 21) for k in range(22)],
                           [wd[s].B, actT.B], [PB[b]])
                        act(y2[:, i, n0:n0 + 512], bk(b), AF.Copy, [PB[b]], [y2.B])
                        act(sc_[:], bk(b), AF.Square, [PB[b]], [sc_.B])
                        pe([(bk(4 + nt), ones[:], sc_[:], i == 0, i == 7)], [ones.B, sc_.B], [PB[4 + nt]])
                    if i + 2 < 8:
                        load_wd(hf, i + 2)
                for nt in range(2):
                    n0 = 512 * nt
                    rstd_ops(r4[:, n0:n0 + 512], r4.B, bk(4 + nt), PB[4 + nt], 1.0 / 1024)

            for hf in range(2):
                S.dma("sp", x1h[hf][:], x1Sv[:, :, 1024 * hf:1024 * hf + 1024], dX1[hf], reads=[Bx1S], writes=[x1h[hf].B])
            load_gu(0, 0)
            load_gu(0, 1)
            prologue(0)
            gu_phase(0, [])
            load_gu(1, 0)
            load_gu(1, 1)
            prologue(1)
            down_phase(0)
            gu_phase(1, epi_ops(0))
            S.dma("sp", outTv[:, :, 0:1024], y2[:], dOut, reads=[y2.B])
            down_phase(1)
            for f in epi_ops(1):
                f()
            S.dma("sp", outTv[:, :, 1024:2048], y2[:], dOut, reads=[y2.B])
            S.barrier()
            S.mark("C2")

        if cut:
            S.cut(cut)
        S.emit(None)
        S.check()
        with nc.Block() as block:
            @block.tensor
            def _(e):
                S.play("pe", e)

            @block.scalar
            def _(e):
                S.play("act", e)

            @block.vector
            def _(e):
                S.play("dve", e)

            @block.gpsimd
            def _(e):
                S.play("pool", e)

            @block.sync
            def _(e):
                S.play("sp", e)
    return nc


def make_inputs(inputs, core):
    x = np.asarray(inputs["x"], np.float32)[0]
    pos = np.asarray(inputs["positions"], np.int32)[0]
    o0 = core * OWN
    xr = np.roll(x, -o0, axis=0)
    xTc = np.ascontiguousarray(xr.T)
    posr = np.ascontiguousarray(np.roll(pos, -o0)[None, :])
    xh = np.zeros((16, 1024), np.float32)
    if o0 - 8 >= 0:
        xh[0:8] = x[o0 - 8:o0]
    if o0 + OWN + 8 <= S_TOK:
        xh[8:16] = x[o0 + OWN:o0 + OWN + 8]
    xhT = np.ascontiguousarray(xh.T)
    return xTc, xhT, posr


_CONSTS = {}


def const_tables(core):
    if core in _CONSTS:
        return _CONSTS[core]
    inv = (1.0 / (10000.0 ** (np.arange(0, 32, 2, dtype=np.float32) / np.float32(32)))).astype(np.float32)
    cst = np.zeros((128, 4), np.float32)
    for p in range(128):
        cst[p, 0] = inv[p % 16]
        cst[p, 1] = -1.0 if (p % 32) < 16 else 1.0
        cst[p, 2] = 0.0
        cst[p, 3] = EPS
    t = np.arange(core * OWN, (core + 1) * OWN)
    invcnt = np.zeros((4, OWN), np.float32)
    for g, w in enumerate(POOL_W):
        left = w // 2
        right = w - left - 1
        cnt = (np.minimum(t + right, S_TOK - 1) - np.maximum(t - left, 0) + 1).astype(np.float32)
        invcnt[g] = (1.0 / cnt).astype(np.float32)
    _CONSTS[core] = (cst, invcnt)
    return _CONSTS[core]


def chunk_cols(v):
    v = np.asarray(v, np.float32)
    return np.ascontiguousarray(v.reshape(-1, 128).T)


_NC = {}


def kernel(x, positions, g_mix_pre, w_in, g_q_lat, w_uq, g_kv_lat, w_ukv, w_o_attn,
           w_pool_group, pool_scale, w_o_pool, w_out, g_mix_post, g_ffn_pre,
           w_gate_up, w_down, g_ffn_post, _debug=False):
    inputs = {"x": x, "positions": positions}
    gains = np.concatenate([chunk_cols(g_mix_pre), chunk_cols(g_q_lat), chunk_cols(g_kv_lat),
                            chunk_cols(pool_scale), chunk_cols(g_mix_post), chunk_cols(g_ffn_pre),
                            chunk_cols(g_ffn_post)], axis=1)
    gains = np.ascontiguousarray(gains, dtype=np.float32)
    assert gains.shape == (128, 41)
    f = lambda a: np.ascontiguousarray(np.asarray(a, np.float32))
    common = dict(gains=gains, w_in=f(w_in), w_uq=f(w_uq), w_ukv=f(w_ukv), w_o_attn=f(w_o_attn),
                  w_pool_group=f(w_pool_group), w_o_pool=f(w_o_pool), w_out=f(w_out),
                  w_gate_up=f(w_gate_up), w_down=f(w_down))
    in_maps = []
    for c in range(NCORES):
        xTc, xhT, posr = make_inputs(inputs, c)
        cst, invcnt = const_tables(c)
        m = dict(common)
        m.update(xT=xTc, xh=xhT, pos=posr, cst=cst, invcnt=invcnt)
        in_maps.append(m)
    key = bool(_debug)
    if key not in _NC:
        _NC[key] = build_program(debug=key)
    nc = _NC[key]
    res = run_bass_kernel_spmd(nc, in_maps, core_ids=list(range(NCORES)))
    if _debug:
        return res
    outT = np.concatenate([np.asarray(r["outT"]) for r in res.results], axis=1)
    return np.ascontiguousarray(outT.T)[None, :, :].astype(np.float32)
```

```python
import math
from contextlib import ExitStack

import numpy as np
import concourse.bass as bass
import concourse.mybir as mybir
from concourse.bass_utils import run_bass_kernel_spmd

ENGS = ("pe", "act", "dve", "pool", "sp")


class Buf:
    __slots__ = ("name", "writers", "readers")

    def __init__(self, name):
        self.name = name
        self.writers = []
        self.readers = []


class DSem:
    __slots__ = ("sem", "count", "final")

    def __init__(self, sem, final=False):
        self.sem = sem
        self.count = 0
        self.final = final


class Op:
    __slots__ = ("eng", "fn", "deps", "signal", "sigval", "dsem", "dval", "idx")

    def __init__(self, eng, fn):
        self.eng = eng
        self.fn = fn
        self.deps = []
        self.signal = False
        self.sigval = None
        self.dsem = None
        self.dval = None


class Sched:
    def __init__(self, nc, stack):
        self.nc = nc
        self.stack = stack
        self.q = {e: [] for e in ENGS}
        self.esem = {e: stack.enter_context(nc.semaphore("es_" + e)) for e in ENGS}
        self.dmas = []
        self.marks = {}
        self.nds = 0

    def mark(self, name):
        self.marks[name] = {e: len(self.q[e]) for e in ENGS}

    def cut(self, name):
        for e in ENGS:
            del self.q[e][self.marks[name][e]:]

    def dsem(self, final=False):
        self.nds += 1
        return DSem(self.stack.enter_context(self.nc.semaphore("ds%d" % self.nds)), final)

    def _track(self, op, reads, writes, group_dma_writes=False):
        deps = set()
        for b in reads:
            for w in b.writers:
                deps.add((w, True))
        for b in writes:
            for w in b.writers:
                if group_dma_writes and w.dsem is not None and not b.readers and w.dsem is op.dsem:
                    continue
                deps.add((w, False))
            for r in b.readers:
                deps.add((r, False))
        for d, raw in deps:
            if d is op:
                continue
            if d.dsem is None and op.dsem is None and d.eng == op.eng:
                if not (raw and op.eng in ("act", "dve", "pool")):
                    continue
            op.deps.append(d)
            if d.dsem is None:
                d.signal = True
        for b in reads:
            b.readers.append(op)
        for b in writes:
            if group_dma_writes and b.writers and all(
                    w.dsem is not None and w.dsem is op.dsem for w in b.writers) and not b.readers:
                b.writers.append(op)
            else:
                b.writers = [op]
                b.readers = []

    def op(self, eng, fn, reads=(), writes=()):
        o = Op(eng, fn)
        self._track(o, reads, writes)
        self.q[eng].append(o)
        return o

    def dma(self, eng, out, in_, dsem, reads=(), writes=()):
        def fn(e, out=out, in_=in_):
            return e.dma_start(out=out, in_=in_)
        o = Op(eng, fn)
        o.dsem = dsem
        dsem.count += 16
        o.dval = dsem.count
        self._track(o, reads, writes, group_dma_writes=True)
        self.q[eng].append(o)
        self.dmas.append(o)
        return o

    def barrier(self):
        lasts = []
        for e in ENGS:
            for o in reversed(self.q[e]):
                if o.dsem is None and o.fn is not None:
                    o.signal = True
                    lasts.append(o)
                    break
        dm = {}
        for o in self.dmas:
            dm[id(o.dsem)] = o
        self.dmas = []
        for e in ENGS:
            def fn(eng):
                return None
            b = Op(e, None)
            for l in lasts:
                if l.eng != e:
                    b.deps.append(l)
            b.deps.extend(dm.values())
            self.q[e].append(b)

    def emit(self, engines):
        for e in ENGS:
            c = 0
            for o in self.q[e]:
                if o.dsem is None and o.signal and o.fn is not None:
                    c += 1
                    o.sigval = c

    def _waits(self, o):
        waits = {}
        for d in o.deps:
            if d.dsem is not None:
                key = id(d.dsem)
                val = d.dsem.count if d.dsem.final else d.dval
                sem = d.dsem.sem
            else:
                key = d.eng
                val = d.sigval
                sem = self.esem[d.eng]
            assert val is not None, (o.eng, d.eng, d.fn)
            if key not in waits or waits[key][1] < val:
                waits[key] = (sem, val)
        return waits

    def check(self):
        cnt = {}
        ptr = {e: 0 for e in ENGS}
        total = sum(len(self.q[e]) for e in ENGS)
        done = 0
        while done < total:
            prog = False
            for e in ENGS:
                while ptr[e] < len(self.q[e]):
                    o = self.q[e][ptr[e]]
                    ok = all(cnt.get(k, 0) >= v for k, (s_, v) in self._waits(o).items())
                    if not ok:
                        break
                    if o.fn is not None:
                        if o.dsem is not None:
                            cnt[id(o.dsem)] = cnt.get(id(o.dsem), 0) + 16
                        elif o.signal:
                            cnt[e] = cnt.get(e, 0) + 1
                    ptr[e] += 1
                    done += 1
                    prog = True
            if not prog:
                st = {e: (ptr[e], len(self.q[e])) for e in ENGS}
                raise RuntimeError("schedule deadlock: %s" % st)
        return True

    def play(self, e, eng):
        seen = {}
        last_real = None
        for o in self.q[e]:
            waits = {}
            for d in o.deps:
                if d.dsem is not None:
                    key = id(d.dsem)
                    val = d.dsem.count if d.dsem.final else d.dval
                    sem = d.dsem.sem
                else:
                    key = d.eng
                    val = d.sigval
                    sem = self.esem[d.eng]
                if key not in waits or waits[key][1] < val:
                    waits[key] = (sem, val)
            for key, (sem, val) in waits.items():
                if seen.get(key, 0) >= val:
                    continue
                seen[key] = val
                eng.wait_ge(sem, val)
            if o.fn is None:
                continue
            ins = o.fn(eng)
            if o.dsem is not None:
                ins.then_inc(o.dsem.sem, 16)
            elif o.signal:
                ins.then_inc(self.esem[e], 1)


F32 = mybir.dt.float32
BF16 = mybir.dt.bfloat16
I32 = mybir.dt.int32
AF = mybir.ActivationFunctionType
ALU = mybir.AluOpType

S_TOK = 16384
OWN = 2048
NCORES = 8
EPS = 1e-6
G_PRE, G_Q, G_KV, G_PS, G_POST, G_FPRE, G_FPOST = 0, 8, 11, 13, 17, 25, 33
POOL_W = (2, 4, 8, 16)


def build_program(debug=False, cut=None, nta=None, nchunks=None):
    nc = bass.Bass("TRN2", target_bir_lowering=False)

    def din(name, shape, dt=F32):
        return nc.dram_tensor(name, shape, dt, kind="ExternalInput").ap()

    xT = din("xT", [1024, S_TOK])
    xh = din("xh", [1024, 16])
    pos = din("pos", [1, S_TOK], I32)
    cst = din("cst", [128, 4])
    gains = din("gains", [128, 41])
    invcnt = din("invcnt", [4, OWN])
    w_in = din("w_in", [1024, 3232])
    w_uq = din("w_uq", [384, 768])
    w_ukv = din("w_ukv", [256, 1024])
    w_o_attn = din("w_o_attn", [512, 1024])
    w_pool_group = din("w_pool_group", [4, 128, 128])
    w_o_pool = din("w_o_pool", [512, 1024])
    w_out = din("w_out", [1024, 1024])
    w_gate_up = din("w_gate_up", [1024, 5632])
    w_down = din("w_down", [2816, 1024])
    outT = nc.dram_tensor("outT", [1024, OWN], F32, kind="ExternalOutput").ap()
    skind = dict(kind="ExternalOutput") if debug else {}
    kS = nc.dram_tensor("kS", [512, S_TOK], BF16, **skind).ap()
    krS = nc.dram_tensor("krS", [32, S_TOK], BF16, **skind).ap()
    vS = nc.dram_tensor("vS", [8, 128, 128, 65], BF16, **skind).ap()
    cosS = nc.dram_tensor("cosS", [32, S_TOK], F32, **skind).ap()
    sinS = nc.dram_tensor("sinS", [32, S_TOK], F32, **skind).ap()
    x1S = nc.dram_tensor("x1S", [1024, OWN], F32, **skind).ap()
    rsD = nc.dram_tensor("rsD", [64, 512], F32).ap()
    if debug:
        qtD = nc.dram_tensor("qtD", [128, 8, OWN], BF16, kind="ExternalOutput").ap()
        atD = nc.dram_tensor("atD", [128, 8, OWN], BF16, kind="ExternalOutput").ap()
        pmD = nc.dram_tensor("pmD", [128, 4, OWN], BF16, kind="ExternalOutput").ap()

    xTv = xT.rearrange("(k p) n -> p k n", p=128)
    xhv = xh.rearrange("(k p) n -> p k n", p=128)
    w_inv = w_in.rearrange("(k p) n -> p k n", p=128)
    outTv = outT.rearrange("(k p) n -> p k n", p=128)
    x1Sv = x1S.rearrange("(k p) n -> p k n", p=128)
    kSv = kS.rearrange("(j q) t -> q j t", q=128)
    vSv = vS.rearrange("h p t d -> p h t d")

    with ExitStack() as gst:
        S = Sched(nc, gst)
        ps = gst.enter_context(nc.psum_tensor("ps", [128, 4096], F32))
        PB = [Buf("pb%d" % b) for b in range(8)]

        def bk(b):
            return ps[:, 512 * b:512 * (b + 1)]

        class T:
            def __init__(self, st, name, shape, dt, nb=1, side=None):
                if side:
                    self.t = st.enter_context(nc.sbuf_tensor("sb_" + name, shape, dt, side=side))
                else:
                    self.t = st.enter_context(nc.sbuf_tensor("sb_" + name, shape, dt))
                self.b = [Buf(name + str(i)) for i in range(nb)]

            def __getitem__(self, k):
                return self.t[k]

            @property
            def B(self):
                return self.b[0]

        def pe(mms, reads, writes):
            def fn(e, mms=mms):
                ins = None
                for (o, l, r, st_, sp_) in mms:
                    ins = e.matmul(o, l, r, start=st_, stop=sp_)
                return ins
            return S.op("pe", fn, reads, writes)

        def act(out, in_, func, reads, writes, scale=1.0, bias=0.0):
            return S.op("act", lambda e: e.activation(out, in_, func, bias=bias, scale=scale), reads, writes)

        def dve(fn, reads, writes):
            return S.op("dve", fn, reads, writes)

        def tt(out, a, b, op, reads, writes, eng="dve"):
            return S.op(eng, lambda e: e.tensor_tensor(out, a, b, op), reads, writes)

        def stt(out, in0, sc, in1, op0, op1, reads, writes):
            return S.op("dve", lambda e: e.scalar_tensor_tensor(out, in0, sc, in1, op0, op1), reads, writes)

        def ts(out, in0, s1, s2, op0, op1, reads, writes):
            if s2 is None:
                return S.op("dve", lambda e: e.tensor_scalar(out, in0, s1, None, op0), reads, writes)
            return S.op("dve", lambda e: e.tensor_scalar(out, in0, s1, s2, op0, op1), reads, writes)

        cstT = T(gst, "cst", [128, 4], F32)
        gT = T(gst, "gains", [128, 41], F32)
        ones = T(gst, "ones", [128, 128], BF16)
        onesf = T(gst, "onesf", [128, 128], F32)
        dC = S.dsem(final=True)
        S.dma("sp", cstT[:], cst, dC, writes=[cstT.B])
        S.dma("sp", gT[:], gains, dC, writes=[gT.B])
        S.op("pool", lambda e: e.memset(ones[:], 1.0), writes=[ones.B])
        S.op("pool", lambda e: e.memset(onesf[:], 1.0), writes=[onesf.B])
        epsb = cstT[:, 3:4]

        def gcol(off, k, p0=0, p1=128):
            return gT[p0:p1, off + k:off + k + 1]

        def rstd_ops(out_ap, out_B, ss_ap, ss_B, sc):
            act(out_ap, ss_ap, AF.Ln, [ss_B, cstT.B], [out_B], scale=sc, bias=epsb)
            act(out_ap, out_ap, AF.Exp, [out_B], [out_B], scale=-0.5)

        def tot_ops(out_ap, out_B, ss_ap, ss_B, a_ap, a_B, nf):
            stt(out_ap, ss_ap, 1.0 / nf, a_ap, ALU.mult, ALU.add, [ss_B, a_B], [out_B])
            act(out_ap, out_ap, AF.Ln, [out_B], [out_B])
            act(out_ap, out_ap, AF.Exp, [out_B], [out_B], scale=-0.5)

        dStage = [S.dsem(), S.dsem()]
        stage_ctr = [0]

        def load_folded(st, stg, dst_fn, src_fn, ncols, goff, q="sp", dsems=None):
            dsems = dsems or dStage
            for k in range(8):
                i = stage_ctr[0] % 2
                stage_ctr[0] += 1
                S.dma(q, stg[i][:, 0:ncols], src_fn(k), dsems[i], writes=[stg[i].B])
                ts(dst_fn(k), stg[i][:, 0:ncols], gcol(goff, k), None, ALU.mult, None,
                   [stg[i].B, gT.B], [])

        with ExitStack() as st:
            I = T(st, "ti", [128, 4096], I32)
            X = T(st, "tx", [128, 4096], F32)
            Y = T(st, "ty", [128, 4096], F32)
            Z = T(st, "tz", [128, 4096], F32)
            W = T(st, "tw", [128, 4096], F32)
            d0 = S.dsem(final=True)
            for s in range(4):
                S.dma("sp", I[32 * s:32 * s + 32, :],
                      bass.AP(pos.tensor, 4096 * s, [[0, 32], [1, 4096]]),
                      d0, writes=[I.B])
            C1 = 6.28125
            C2 = 2 * math.pi - C1
            dve(lambda e: e.tensor_copy(X[:], I[:]), [I.B], [X.B])
            ts(X[:], X[:], cstT[:, 0:1], None, ALU.mult, None, [X.B, cstT.B], [X.B])
            ts(I[:], X[:], 1.0 / (2 * math.pi), None, ALU.mult, None, [X.B], [I.B])
            dve(lambda e: e.tensor_copy(Y[:], I[:]), [I.B], [Y.B])
            stt(Z[:], Y[:], -C1, X[:], ALU.mult, ALU.add, [Y.B, X.B], [Z.B])
            stt(Z[:], Y[:], -C2, Z[:], ALU.mult, ALU.add, [Y.B, Z.B], [Z.B])
            ts(Y[:], Z[:], math.pi, -2 * math.pi, ALU.is_gt, ALU.mult, [Z.B], [Y.B])
            tt(Z[:], Z[:], Y[:], ALU.add, [Z.B, Y.B], [Z.B])
            act(X[:], Z[:], AF.Sin, [Z.B, cstT.B], [X.B], scale=cstT[:, 1:2])
            dT = S.dsem(final=True)
            BtabS = Buf("tabS")
            for s in range(4):
                S.dma("sp", sinS[:, 4096 * s:4096 * (s + 1)], X[32 * s:32 * s + 32, :], dT, reads=[X.B], writes=[BtabS])
            ts(Y[:], Z[:], math.pi / 2, None, ALU.add, None, [Z.B], [Y.B])
            ts(W[:], Y[:], math.pi, -2 * math.pi, ALU.is_gt, ALU.mult, [Y.B], [W.B])
            tt(Y[:], Y[:], W[:], ALU.add, [Y.B, W.B], [Y.B])
            act(W[:], Y[:], AF.Sin, [Y.B], [W.B])
            for s in range(4):
                S.dma("sp", cosS[:, 4096 * s:4096 * (s + 1)], W[32 * s:32 * s + 32, :], dT, reads=[W.B], writes=[BtabS])
            S.barrier()
            S.mark("p0")

        def prep_tile(xb, sq, r1, a1, src_ap, n, dX, need_a=True):
            S.dma("pool", xb[:, :, 0:n], src_ap, dX, writes=[xb.B])
            act(sq[:, :, 0:n], xb[:, :, 0:n], AF.Square, [xb.B], [sq.B])
            pe([(bk(0)[:, 0:n], ones[:], sq[:, k, 0:n], k == 0, k == 7) for k in range(8)],
               [sq.B, ones.B], [PB[0]])
            rstd_ops(r1[:, 0:n], r1.B, bk(0)[:, 0:n], PB[0], 1.0 / 1024)
            if need_a:
                ts(a1[:, 0:n], bk(0)[:, 0:n], EPS / 1024, EPS * EPS, ALU.mult, ALU.add, [PB[0], r1.B], [a1.B])

        with ExitStack() as bq:
            AT = T(bq, "AT", [128, 8, OWN], BF16)
            PM = T(bq, "PM", [128, 4, OWN], BF16)
            qsc = ExitStack()
            QT = T(qsc, "QT", [128, 8, OWN], BF16)

            qw = ExitStack()
            stgq2 = [T(qw, "stgq2%d" % i, [128, 512], F32, side="right") for i in range(2)]
            wcq = T(qw, "wcq", [128, 8, 384], BF16, side="right")
            wu = T(qw, "wu", [128, 8, 512], BF16, side="right")
            wq = T(qw, "wq", [128, 3, 768], BF16, side="right")
            wqr = T(qw, "wqr", [128, 3, 8, 96], BF16, side="right")
            wpg = T(qw, "wpg", [128, 4, 128], BF16, side="right")
            dWq = S.dsem(final=True)
            dStq = [S.dsem(), S.dsem()]

            def load_q_weights():
                st = None
                stg = stgq2
                dW = dWq
                load_folded(st, stg, lambda k: wcq[:, k, :], lambda k: w_inv[:, k, 0:384], 384, G_PRE, dsems=dStq)
                wcq.B.writers = [S.q["dve"][-1]]
                load_folded(st, stg, lambda k: wu[:, k, :], lambda k: w_inv[:, k, 672:1184], 512, G_PRE, dsems=dStq)
                wu.B.writers = [S.q["dve"][-1]]
                S.dma("pool", wq[:], w_uq.rearrange("(k p) n -> p k n", p=128), dW, writes=[wq.B])
                S.op("pool", lambda e: e.memset(wqr[:], 0.0), writes=[wqr.B])
                uqv = w_uq.rearrange("(k p) (h d) -> p k h d", p=128, d=96)
                for c in range(3):
                    S.dma("pool", wqr[:, c, :, 64:80], uqv[:, c, :, 80:96], dW, writes=[wqr.B])
                    S.dma("pool", wqr[:, c, :, 80:96], uqv[:, c, :, 64:80], dW, writes=[wqr.B])
                S.dma("pool", wpg[:], w_pool_group.rearrange("g c d -> c g d"), dW, writes=[wpg.B])


            with ExitStack() as st:
                stg = [T(st, "stg%d" % i, [128, 512], F32) for i in range(2)]
                wkv = T(st, "wkv", [128, 8, 320], BF16)
                wk = T(st, "wk", [128, 2, 512], BF16)
                wv = T(st, "wv", [128, 2, 512], BF16)
                xb = [T(st, "xb%d" % i, [128, 8, 512], BF16) for i in range(2)]
                sq = [T(st, "sq%d" % i, [128, 8, 512], BF16) for i in range(2)]
                r1 = [T(st, "r1%d" % i, [128, 512], F32) for i in range(2)]
                a1 = [T(st, "a1%d" % i, [128, 512], F32) for i in range(2)]
                sqr = [T(st, "sqr%d" % i, [128, 2, 512], BF16) for i in range(2)]
                tot = T(st, "tot", [128, 512], F32)
                ckvn = [T(st, "ckvn%d" % i, [128, 2, 512], BF16) for i in range(2)]
                kst = [T(st, "kst%d" % i, [128, 4, 512], BF16) for i in range(2)]
                vst = [T(st, "vst%d" % i, [128, 8, 4, 65], BF16) for i in range(2)]
                ct = [T(st, "ct%d" % i, [32, 512], F32) for i in range(2)]
                sn = [T(st, "sn%d" % i, [32, 512], F32) for i in range(2)]
                krs = [T(st, "krs%d" % i, [32, 512], BF16) for i in range(2)]
                t1 = T(st, "t1", [32, 512], F32)
                t2 = T(st, "t2", [32, 512], F32)
                dW = S.dsem(final=True)
                dX = [S.dsem(), S.dsem()]
                dTab = [S.dsem(), S.dsem()]
                dTabS = [S.dsem(), S.dsem()]
                dKs = [S.dsem(), S.dsem()]
                dVs = [S.dsem(), S.dsem()]
                dKr = [S.dsem(), S.dsem()]
                BscrK, BscrV, BscrR = Buf("scrK"), Buf("scrV"), Buf("scrR")

                load_folded(st, stg, lambda k: wkv[:, k, 0:288], lambda k: w_inv[:, k, 384:672], 288, G_PRE)
                load_folded(st, stg, lambda k: wkv[:, k, 288:304], lambda k: w_inv[:, k, 656:672], 16, G_PRE)
                load_folded(st, stg, lambda k: wkv[:, k, 304:320], lambda k: w_inv[:, k, 640:656], 16, G_PRE)
                wkv.B.writers = [S.q["dve"][-1]]
                ukv = w_ukv.rearrange("(k p) (h two d) -> p k h two d", p=128, two=2, d=64)
                for kc in range(2):
                    S.dma("pool", wk[:, kc, :].rearrange("p (h d) -> p h d", d=64), ukv[:, kc, :, 0, :], dW, writes=[wk.B])
                    S.dma("pool", wv[:, kc, :].rearrange("p (h d) -> p h d", d=64), ukv[:, kc, :, 1, :], dW, writes=[wv.B])
                for i in range(2):
                    S.op("pool", lambda e, i=i: e.memset(vst[i][:], 1.0), writes=[vst[i].B])

                NT = nta or (S_TOK // 512)

                def pre(i):
                    p = i % 2
                    c0 = 512 * i
                    S.dma("pool", xb[p][:], xTv[:, :, c0:c0 + 512], dX[p], writes=[xb[p].B])
                    act(sq[p][:], xb[p][:], AF.Square, [xb[p].B], [sq[p].B])

                def stage1(i):
                    p = i % 2
                    c0 = 512 * i
                    pe([(bk(0), ones[:], sq[p][:, k, :], k == 0, k == 7) for k in range(8)],
                       [sq[p].B, ones.B], [PB[0]])
                    rstd_ops(r1[p][:], r1[p].B, bk(0), PB[0], 1.0 / 1024)
                    ts(a1[p][:], bk(0), EPS / 1024, EPS * EPS, ALU.mult, ALU.add, [PB[0], r1[p].B], [a1[p].B])
                    S.dma("sp", ct[p][:], cosS[:, c0:c0 + 512], dTab[p], reads=[BtabS], writes=[ct[p].B])
                    S.dma("sp", sn[p][:], sinS[:, c0:c0 + 512], dTabS[p], reads=[BtabS], writes=[sn[p].B])
                    for c in range(2):
                        pe([(bk(1 + c), wkv[:, k, 128 * c:128 * c + 128], xb[p][:, k, :], k == 0, k == 7) for k in range(8)],
                           [wkv.B, xb[p].B], [PB[1 + c]])
                        act(sqr[p][:, c, :], bk(1 + c), AF.Square, [PB[1 + c]], [sqr[p].B])
                    pe([(bk(3)[0:32, :], wkv[:, k, 256:288], xb[p][:, k, :], k == 0, k == 7) for k in range(8)],
                       [wkv.B, xb[p].B], [PB[3]])
                    pe([(bk(4)[0:32, :], wkv[:, k, 288:320], xb[p][:, k, :], k == 0, k == 7) for k in range(8)],
                       [wkv.B, xb[p].B], [PB[4]])
                    tt(t1[:], bk(3)[0:32, :], ct[p][:], ALU.mult, [PB[3], ct[p].B], [t1.B])
                    tt(t2[:], bk(4)[0:32, :], sn[p][:], ALU.mult, [PB[4], sn[p].B], [t2.B])
                    tt(t1[:], t1[:], t2[:], ALU.add, [t1.B, t2.B], [t1.B])
                    tt(krs[p][:], t1[:], r1[p][0:32, :], ALU.mult, [t1.B, r1[p].B], [krs[p].B])
                    S.dma("sp", krS[:, c0:c0 + 512], krs[p][:], dKr[p], reads=[krs[p].B], writes=[BscrR])

                def stage2(i):
                    p = i % 2
                    pe([(bk(0), ones[:], sqr[p][:, c, :], c == 0, c == 1) for c in range(2)],
                       [ones.B, sqr[p].B], [PB[0]])
                    tot_ops(tot[:], tot.B, bk(0), PB[0], a1[p][:], a1[p].B, 256)
                    for c in range(2):
                        stt(ckvn[p][:, c, :], bk(1 + c), gcol(G_KV, c), tot[:], ALU.mult, ALU.mult,
                            [PB[1 + c], gT.B, tot.B], [ckvn[p].B])

                rot = [0]

                def rb():
                    b = 5 + rot[0] % 3
                    rot[0] += 1
                    return b

                def stage3(i):
                    p = i % 2
                    c0 = 512 * i
                    for j in range(4):
                        b = rb()
                        pe([(bk(b), wk[:, kc, 128 * j:128 * j + 128], ckvn[p][:, kc, :], kc == 0, kc == 1) for kc in range(2)],
                           [wk.B, ckvn[p].B], [PB[b]])
                        act(kst[p][:, j, :], bk(b), AF.Copy, [PB[b]], [kst[p].B])
                    for s in range(4):
                        b = rb()
                        pe([(bk(b), ckvn[p][:, kc, 128 * s:128 * s + 128], wv[:, kc, :], kc == 0, kc == 1) for kc in range(2)],
                           [wv.B, ckvn[p].B], [PB[b]])
                        act(vst[p][:, :, s, 0:64], bk(b).rearrange("p (h d) -> p h d", d=64), AF.Copy, [PB[b]], [vst[p].B])
                    S.dma("sp", kSv[:, :, c0:c0 + 512], kst[p][:], dKs[p], reads=[kst[p].B], writes=[BscrK])
                    S.dma("sp", vSv[:, :, 4 * i:4 * i + 4, :], vst[p][:], dVs[p], reads=[vst[p].B], writes=[BscrV])

                def dbg_mark(nm):
                    if cut == nm:
                        S.barrier()
                        S.mark(nm)
                dbg_mark("Aw")
                pre(0)
                if NT > 1:
                    pre(1)
                stage1(0)
                load_q_weights()
                dbg_mark("A1")
                for i in range(NT):
                    stage2(i)
                    if i == 0:
                        dbg_mark("A2")
                    if i + 1 < NT:
                        stage1(i + 1)
                    if i + 2 < NT:
                        pre(i + 2)
                    stage3(i)
                    if i == 0:
                        dbg_mark("A3")
                S.barrier()
                S.mark("A")

            with ExitStack() as st:
                stg = [T(st, "stgq%d" % i, [128, 512], F32) for i in range(2)]
                xb = T(st, "xbq", [128, 8, 512], BF16)
                sq = T(st, "sqq", [128, 8, 512], BF16)
                r1 = T(st, "r1q", [128, 512], F32)
                a1 = T(st, "a1q", [128, 512], F32)
                uT = T(st, "uT", [128, 4, OWN + 16], F32)
                st1 = ExitStack()
                cqraw = T(st1, "cqraw", [128, 3, 512], F32)
                sqcq = T(st1, "sqcq", [128, 3, 512], BF16)
                totq = T(st1, "totq", [128, 512], F32)
                cqn = T(st1, "cqn", [128, 3, 512], BF16)
                ctq = T(st1, "ctq", [128, 512], F32)
                snq = T(st1, "snq", [128, 512], F32)
                tq1 = [T(st1, "tq1%d" % i, [128, 512], F32) for i in range(2)]
                tq2 = [T(st1, "tq2%d" % i, [128, 512], F32) for i in range(2)]
                dW = S.dsem(final=True)
                dX = S.dsem()
                dTab = S.dsem()
                dTabS = S.dsem()
                dIc = S.dsem()

                rot = [0]

                def rb():
                    b = 1 + rot[0] % 7
                    rot[0] += 1
                    return b

                def u_part(n, dst_fns):
                    for g in range(4):
                        b = rb()
                        pe([(bk(b)[:, 0:n], wu[:, k, 128 * g:128 * g + 128], xb[:, k, 0:n], k == 0, k == 7) for k in range(8)],
                           [wu.B, xb.B], [PB[b]])
                        for (lo, hi, dfn) in dst_fns:
                            tt(dfn(g), bk(b)[:, lo:hi], r1[:, lo:hi], ALU.mult, [PB[b], r1.B], [uT.B])

                def dbg_markq(nm):
                    if cut == nm:
                        S.barrier()
                        S.mark(nm)
                dbg_markq("Qw")
                prep_tile(xb, sq, r1, a1, xhv, 16, dX, need_a=False)
                u_part(16, [(0, 8, lambda g: uT[:, g, 0:8]), (8, 16, lambda g: uT[:, g, OWN + 8:OWN + 16])])

                dbg_markq("Qh")
                for i in range(4):
                    if i == 1:
                        dbg_markq("Q0")
                    c0 = 512 * i
                    prep_tile(xb, sq, r1, a1, xTv[:, :, c0:c0 + 512], 512, dX)
                    S.dma("sp", ctq[64:96, :], cosS[:, c0:c0 + 512], dTab, reads=[BtabS], writes=[ctq.B])
                    S.dma("sp", snq[64:96, :], sinS[:, c0:c0 + 512], dTabS, reads=[BtabS], writes=[snq.B])
                    u_part(512, [(0, 512, lambda g, c0=c0: uT[:, g, 8 + c0:8 + c0 + 512])])
                    for c in range(3):
                        b = rb()
                        pe([(bk(b), wcq[:, k, 128 * c:128 * c + 128], xb[:, k, :], k == 0, k == 7) for k in range(8)],
                           [wcq.B, xb.B], [PB[b]])
                        act(sqcq[:, c, :], bk(b), AF.Square, [PB[b]], [sqcq.B])
                        dve(lambda e, b=b, c=c: e.tensor_copy(cqraw[:, c, :], bk(b)), [PB[b], sqcq.B], [cqraw.B])
                    b = rb()
                    pe([(bk(b), ones[:], sqcq[:, c, :], c == 0, c == 2) for c in range(3)], [ones.B, sqcq.B], [PB[b]])
                    tot_ops(totq[:], totq.B, bk(b), PB[b], a1[:], a1.B, 384)
                    for c in range(3):
                        stt(cqn[:, c, :], cqraw[:, c, :], gcol(G_Q, c), totq[:], ALU.mult, ALU.mult,
                            [cqraw.B, gT.B, totq.B], [cqn.B])
                    for h in range(8):
                        b1 = rb()
                        pe([(bk(b1)[0:96, :], wq[:, c, 96 * h:96 * h + 96], cqn[:, c, :], c == 0, c == 2) for c in range(3)],
                           [wq.B, cqn.B], [PB[b1]])
                        b2 = rb()
                        pe([(bk(b2)[0:96, :], wqr[:, c, h, :], cqn[:, c, :], c == 0, c == 2) for c in range(3)],
                           [wqr.B, cqn.B], [PB[b2]])
                        x1_, x2_ = tq1[h % 2], tq2[h % 2]
                        act(x1_[0:96, :], bk(b1)[0:96, :], AF.Copy, [PB[b1]], [x1_.B])
                        act(x2_[0:96, :], bk(b2)[0:96, :], AF.Copy, [PB[b2]], [x2_.B])
                        act(QT[0:64, h, c0:c0 + 512], bk(b1)[0:64, :], AF.Copy, [PB[b1]], [QT.B])
                        tt(x1_[64:96, :], x1_[64:96, :], ctq[64:96, :], ALU.mult, [x1_.B, ctq.B], [x1_.B])
                        tt(x2_[64:96, :], x2_[64:96, :], snq[64:96, :], ALU.mult, [x2_.B, snq.B], [x2_.B])
                        tt(QT[64:96, h, c0:c0 + 512], x1_[64:96, :], x2_[64:96, :], ALU.add, [x1_.B, x2_.B], [QT.B])

                S.barrier()
                S.mark("Q1")
                st1.close()
                TA = T(st, "TA", [128, OWN + 16], F32)
                TB = T(st, "TB", [128, OWN + 16], F32)
                invc = T(st, "invc", [128, OWN], F32)
                pooled = T(st, "pooled", [128, 4, OWN], BF16)
                L = OWN + 16

                def sh(dst, src, lo, hi, d1, d2):
                    return lambda e: e.tensor_tensor(dst[:, lo:hi], src[:, lo + d1:hi + d1], src[:, lo + d2:hi + d2], ALU.add)

                for g in range(4):
                    ug = uT.t[:, g, :]
                    S.dma("sp", invc[:], bass.AP(invcnt.tensor, OWN * g, [[0, 128], [1, OWN]]), dIc, writes=[invc.B])
                    S.op("pool", sh(TA, ug, 1, L, -1, 0), [uT.B], [TA.B])
                    win = TA
                    if g >= 1:
                        S.op("pool", sh(TB, TA, 2, L - 1, -1, 1), [TA.B], [TB.B])
                        win = TB
                    if g >= 2:
                        S.op("pool", sh(TA, TB, 4, L - 3, -2, 2), [TB.B], [TA.B])
                        win = TA
                    if g >= 3:
                        S.op("pool", sh(TB, TA, 8, L - 7, -4, 4), [TA.B], [TB.B])
                        win = TB
                    other = TB if win is TA else TA
                    tt(other[:, 8:8 + OWN], win[:, 8:8 + OWN], invc[:], ALU.mult, [win.B, invc.B], [other.B])
                    tt(pooled[:, g, :], other[:, 8:8 + OWN], ug[:, 8:8 + OWN], ALU.subtract, [other.B, uT.B], [pooled.B])
                    for nt in range(4):
                        b = rb()
                        pe([(bk(b), wpg[:, g, :], pooled[:, g, 512 * nt:512 * nt + 512], True, True)], [wpg.B, pooled.B], [PB[b]])
                        act(PM[:, g, 512 * nt:512 * nt + 512], bk(b), AF.Identity, [PB[b], gT.B], [PM.B], scale=gcol(G_PS, g))
                if debug:
                    dD = S.dsem(final=True)
                    S.dma("sp", qtD[0:96], QT[0:96], dD, reads=[QT.B])
                    S.dma("sp", pmD, PM[:], dD, reads=[PM.B])
                S.barrier()
                S.mark("Q")
            qw.close()

            cw = ExitStack()
            stgc2 = [T(cw, "stgc2%d" % i, [128, 1024], F32, side="right") for i in range(2)]
            wg = T(cw, "wg", [128, 8, 2048], BF16, side="right")
            woa = T(cw, "woa", [64, 8, 1024], BF16, side="right")
            wob = T(cw, "wob", [128, 4, 1024], BF16, side="right")
            wout = T(cw, "wout", [128, 8, 1024], BF16, side="right")
            dWc = S.dsem(final=True)
            dStc = [S.dsem(), S.dsem()]

            def load_c1_weights():
                st = None
                stg = stgc2
                dW = dWc
                load_folded(st, stg, lambda k: wg[:, k, 0:1024], lambda k: w_inv[:, k, 1184:2208], 1024, G_PRE, q="pool", dsems=dStc)
                load_folded(st, stg, lambda k: wg[:, k, 1024:2048], lambda k: w_inv[:, k, 2208:3232], 1024, G_PRE, q="pool", dsems=dStc)
                wg.B.writers = [S.q["dve"][-1]]
                S.dma("pool", woa[:], w_o_attn.rearrange("(h d) n -> d h n", d=64), dW, writes=[woa.B])
                S.dma("pool", wob[:], w_o_pool.rearrange("(c p) n -> p c n", p=128), dW, writes=[wob.B])
                for k in range(8):
                    S.dma("pool", wout[:, k, :], w_out[128 * k:128 * k + 128, :], dW, writes=[wout.B])

            with ExitStack() as st:
                NSL = 4
                NSB = 3
                NPB = 4
                LOOK = 2
                kc = [T(st, "kc%d" % i, [128, 2048], BF16) for i in range(NSL)]
                vc = [T(st, "vc%d" % i, [128, 16, 65], BF16) for i in range(NSL)]
                Pb = [T(st, "P%d" % i, [128, 1024], BF16) for i in range(NPB)]
                osb = [T(st, "osb%d" % i, [128, 512], F32) for i in range(2)]
                rcb = [T(st, "rcb%d" % i, [64, 512], F32) for i in range(2)]
                dRs = [S.dsem(), S.dsem()]
                dRc = [S.dsem(), S.dsem()]
                BrsD = [Buf("rsD%d" % i) for i in range(64)]
                dSl = [S.dsem() for _ in range(NSL)]
                dSlV = [S.dsem() for _ in range(NSL)]
                scale = 96 ** -0.5
                chunks = [(h, qh, c) for h in range(8) for qh in range(2) for c in range(8)]
                if nchunks:
                    chunks = chunks[:nchunks]

                def load_chunk(n):
                    h, qh, c = chunks[n]
                    s = n % NSL
                    S.dma("sp", kc[s][0:64, :], kS[64 * h:64 * h + 64, 2048 * c:2048 * (c + 1)], dSl[s], reads=[BscrK], writes=[kc[s].B])
                    S.dma("sp", kc[s][64:96, :], krS[:, 2048 * c:2048 * (c + 1)], dSl[s], reads=[BscrR], writes=[kc[s].B])
                    S.dma("sp", vc[s][:], vS[h, :, 16 * c:16 * c + 16, :], dSlV[s], reads=[BscrV], writes=[vc[s].B])

                for n in range(min(3, len(chunks))):
                    load_chunk(n)
                load_c1_weights()
                step = [0]
                pend = []

                def emit_pv(pv):
                    (s, kt, pb, first, last) = pv
                    pe([(ps[0:65, 512 * q:512 * (q + 1)], vc[s][:, kt, :], Pb[pb][:, 512 * q:512 * q + 512], first, last)
                        for q in range(2)],
                       [vc[s].B, Pb[pb].B], [PB[0], PB[1]])

                for n, (h, qh, c) in enumerate(chunks):
                    s = n % NSL
                    q0 = 1024 * qh
                    for kt in range(16):
                        sb_ = step[0] % NSB
                        pb = step[0] % NPB
                        step[0] += 1
                        b0 = 2 + 2 * sb_
                        pe([(bk(b0 + q), kc[s][0:96, 128 * kt:128 * kt + 128], QT[0:96, h, q0 + 512 * q:q0 + 512 * q + 512], True, True)
                            for q in range(2)],
                           [kc[s].B, QT.B], [PB[b0], PB[b0 + 1]])
                        act(Pb[pb][:], ps[:, 512 * b0:512 * b0 + 1024], AF.Exp, [PB[b0], PB[b0 + 1]], [Pb[pb].B], scale=scale)
                        pend.append((s, kt, pb, (c == 0 and kt == 0), (c == 7 and kt == 15)))
                        if len(pend) > LOOK:
                            emit_pv(pend.pop(0))
                        if kt == LOOK and n + 3 < len(chunks):
                            load_chunk(n + 3)
                    if c == 7:
                        while pend:
                            emit_pv(pend.pop(0))
                        grp = 2 * h + qh
                        for qb in range(2):
                            o_ = osb[qb]
                            g_ = 2 * grp + qb
                            act(o_[0:65, :], bk(qb)[0:65, :], AF.Copy, [PB[qb]], [o_.B])
                            dve(lambda e, o_=o_: e.reciprocal(o_[64:65, :], o_[64:65, :]), [o_.B], [o_.B])
                            S.dma("sp", rsD[g_:g_ + 1, :], o_[64:65, :], dRs[qb], reads=[o_.B], writes=[BrsD[g_]])
                            S.dma("sp", rcb[qb][0:64, :], bass.AP(rsD.tensor, 512 * g_, [[0, 64], [1, 512]]), dRc[qb],
                                  reads=[BrsD[g_]], writes=[rcb[qb].B])
                            tt(AT[0:64, h, q0 + 512 * qb:q0 + 512 * qb + 512], o_[0:64, :], rcb[qb][0:64, :], ALU.mult,
                               [o_.B, rcb[qb].B], [AT.B])
                if debug:
                    dD2 = S.dsem(final=True)
                    S.dma("sp", atD[0:64], AT[0:64], dD2, reads=[AT.B])
                S.barrier()
                S.mark("B")

            qsc.close()
            with ExitStack() as st:
                xbl = [T(st, "xbc%d" % i, [128, 8, 512], BF16) for i in range(2)]
                sq = T(st, "sqc", [128, 8, 512], BF16)
                xf = T(st, "xfc", [128, 8, 512], F32)
                r1 = T(st, "r1c", [128, 512], F32)
                r2 = T(st, "r2c", [128, 512], F32)
                zA = [T(st, "zA%d" % i, [128, 512], F32) for i in range(2)]
                zB = [T(st, "zB%d" % i, [128, 512], F32) for i in range(2)]
                m = T(st, "m", [128, 8, 512], BF16)
                y = T(st, "y", [128, 8, 512], F32)
                sqy = [T(st, "sqy%d" % i, [128, 512], BF16) for i in range(2)]
                dW = S.dsem(final=True)
                dX = S.dsem()
                dXf = S.dsem()
                dXl = [S.dsem(), S.dsem()]
                dO = S.dsem()
                Bx1S = Buf("x1S")
                for i in range(4):
                    c0 = 512 * i
                    S.dma("sp", xf[:], xTv[:, :, c0:c0 + 512], dXf, writes=[xf.B])
                    xb = xbl[i % 2]
                    if i == 0:
                        S.dma("pool", xb[:], xTv[:, :, 0:512], dXl[0], writes=[xb.B])
                        act(sq[:], xb[:], AF.Square, [xb.B], [sq.B])
                    pe([(bk(0), ones[:], sq[:, k, :], k == 0, k == 7) for k in range(8)], [sq.B, ones.B], [PB[0]])
                    rstd_ops(r1[:], r1.B, bk(0), PB[0], 1.0 / 1024)
                    if i + 1 < 4:
                        xn = xbl[(i + 1) % 2]
                        S.dma("pool", xn[:], xTv[:, :, c0 + 512:c0 + 1024], dXl[(i + 1) % 2], writes=[xn.B])
                    for j in range(8):
                        if j == 4 and i + 1 < 4:
                            act(sq[:], xn[:], AF.Square, [xn.B], [sq.B])
                        bA, bB, ba, bb = (1, 2, 3, 4) if j % 2 == 0 else (5, 6, 7, 4)
                        z1, z2 = zA[j % 2], zB[j % 2]
                        pe([(bk(bA), wg[:, k, 128 * j:128 * j + 128], xb[:, k, :], k == 0, k == 7) for k in range(8)],
                           [wg.B, xb.B], [PB[bA]])
                        pe([(bk(bB), wg[:, k, 1024 + 128 * j:1024 + 128 * j + 128], xb[:, k, :], k == 0, k == 7) for k in range(8)],
                           [wg.B, xb.B], [PB[bB]])
                        pe([(bk(ba), woa[0:64, h, 128 * j:128 * j + 128], AT[0:64, h, c0:c0 + 512], h == 0, h == 7) for h in range(8)],
                           [woa.B, AT.B], [PB[ba]])
                        tt(z1[:], bk(bA), r1[:], ALU.mult, [PB[bA], r1.B], [z1.B])
                        act(z1[:], z1[:], AF.Sigmoid, [z1.B], [z1.B])
                        tt(z1[:], z1[:], bk(ba), ALU.mult, [z1.B, PB[ba]], [z1.B])
                        pe([(bk(bb), wob[:, c, 128 * j:128 * j + 128], PM[:, c, c0:c0 + 512], c == 0, c == 3) for c in range(4)],
                           [wob.B, PM.B], [PB[bb]])
                        tt(z2[:], bk(bB), r1[:], ALU.mult, [PB[bB], r1.B], [z2.B])
                        act(z2[:], z2[:], AF.Sigmoid, [z2.B], [z2.B])
                        tt(z2[:], z2[:], bk(bb), ALU.mult, [z2.B, PB[bb]], [z2.B])
                        tt(m[:, j, :], z1[:], z2[:], ALU.add, [z1.B, z2.B], [m.B])
                    for j in range(8):
                        b = 1 + j % 6
                        pe([(bk(b), wout[:, k, 128 * j:128 * j + 128], m[:, k, :], k == 0, k == 7) for k in range(8)],
                           [wout.B, m.B], [PB[b]])
                        act(y[:, j, :], bk(b), AF.Copy, [PB[b]], [y.B])
                        sy = sqy[j % 2]
                        act(sy[:], bk(b), AF.Square, [PB[b]], [sy.B])
                        if j > 0:
                            sp_ = sqy[(j - 1) % 2]
                            pe([(bk(7), ones[:], sp_[:], j - 1 == 0, False)], [ones.B, sp_.B], [PB[7]])
                    sp_ = sqy[7 % 2]
                    pe([(bk(7), ones[:], sp_[:], False, True)], [ones.B, sp_.B], [PB[7]])
                    rstd_ops(r2[:], r2.B, bk(7), PB[7], 1.0 / 1024)
                    for j in range(8):
                        stt(y[:, j, :], y[:, j, :], gcol(G_POST, j), r2[:], ALU.mult, ALU.mult, [y.B, gT.B, r2.B], [y.B])
                        tt(y[:, j, :], y[:, j, :], xf[:, j, :], ALU.add, [y.B, xf.B], [y.B])
                    S.dma("sp", x1Sv[:, :, c0:c0 + 512], y[:], dO, reads=[y.B], writes=[Bx1S])
                S.barrier()
                S.mark("C1")
            cw.close()

        with ExitStack() as st:
            x1h = [T(st, "x1h%d" % i, [128, 8, 1024], F32) for i in range(2)]
            hfb = T(st, "hfb", [128, 8, 1024], BF16)
            actT = T(st, "actT", [128, 22, 1024], BF16)
            y2 = T(st, "y2", [128, 8, 1024], F32)
            wgu = [T(st, "wgu%d" % i, [128, 8, 256], BF16) for i in range(3)]
            wd = [T(st, "wd%d" % i, [128, 22, 128], BF16) for i in range(2)]
            sqc = [T(st, "sqf%d" % i, [128, 512], BF16) for i in range(2)]
            sg = [T(st, "sg%d" % i, [128, 512], F32) for i in range(2)]
            r3 = T(st, "r3", [128, 1024], F32)
            r4 = T(st, "r4", [128, 1024], F32)
            dX1 = [S.dsem(), S.dsem()]
            dGU = [S.dsem() for _ in range(3)]
            dWD = [S.dsem() for _ in range(2)]
            dOut = S.dsem()
            wguv = w_gate_up.rearrange("(k p) n -> p k n", p=128)
            wdv = w_down.rearrange("(k p) n -> p k n", p=128)

            def load_gu(hf, j):
                s = (hf * 22 + j) % 3
                S.dma("pool", wgu[s][:, :, 0:128], wguv[:, :, 128 * j:128 * j + 128], dGU[s], writes=[wgu[s].B])
                S.dma("pool", wgu[s][:, :, 128:256], wguv[:, :, 2816 + 128 * j:2816 + 128 * j + 128], dGU[s], writes=[wgu[s].B])

            def load_wd(hf, i):
                s = (hf * 8 + i) % 2
                S.dma("pool", wd[s][:], wdv[:, :, 128 * i:128 * i + 128], dWD[s], writes=[wd[s].B])

            def prologue_units(hf):
                xh_ = x1h[hf]
                units = []
                for nt in range(2):
                    n0 = 512 * nt
                    for k in range(8):
                        def f(nt=nt, n0=n0, k=k):
                            sc_ = sqc[k % 2]
                            act(sc_[:], xh_[:, k, n0:n0 + 512], AF.Square, [xh_.B], [sc_.B])
                            pe([(bk(6 + nt), ones[:], sc_[:], k == 0, k == 7)], [ones.B, sc_.B], [PB[6 + nt]])
                        units.append(f)
                return units

            def prologue_finish(hf):
                xh_ = x1h[hf]
                for nt in range(2):
                    n0 = 512 * nt
                    rstd_ops(r3[:, n0:n0 + 512], r3.B, bk(6 + nt), PB[6 + nt], 1.0 / 1024)
                    for k in range(8):
                        stt(hfb[:, k, n0:n0 + 512], xh_[:, k, n0:n0 + 512], gcol(G_FPRE, k), r3[:, n0:n0 + 512],
                            ALU.mult, ALU.mult, [xh_.B, gT.B, r3.B], [hfb.B])

            def prologue(hf):
                for f in prologue_units(hf):
                    f()
                prologue_finish(hf)

            def epi_ops(hf):
                xh_ = x1h[hf]
                ops = []
                for nt in range(2):
                    n0 = 512 * nt
                    for i in range(8):
                        def f(i=i, n0=n0):
                            stt(y2[:, i, n0:n0 + 512], y2[:, i, n0:n0 + 512], gcol(G_FPOST, i), r4[:, n0:n0 + 512],
                                ALU.mult, ALU.mult, [y2.B, gT.B, r4.B], [y2.B])
                            tt(y2[:, i, n0:n0 + 512], y2[:, i, n0:n0 + 512], xh_[:, i, n0:n0 + 512], ALU.add, [y2.B, xh_.B], [y2.B])
                        ops.append(f)
                return ops

            def gu_phase(hf, inter):
                cnt = 0
                for j in range(22):
                    if j + 2 < 22:
                        load_gu(hf, j + 2)
                    if j == 20:
                        load_wd(hf, 0)
                    if j == 21:
                        load_wd(hf, 1)
                    s = (hf * 22 + j) % 3
                    for nt in range(2):
                        n0 = 512 * nt
                        bg, bu = (0, 1) if cnt % 2 == 0 else (2, 3)
                        sg_ = sg[cnt % 2]
                        cnt += 1
                        pe([(bk(bg), wgu[s][:, k, 0:128], hfb[:, k, n0:n0 + 512], k == 0, k == 7) for k in range(8)],
                           [wgu[s].B, hfb.B], [PB[bg]])
                        pe([(bk(bu), wgu[s][:, k, 128:256], hfb[:, k, n0:n0 + 512], k == 0, k == 7) for k in range(8)],
                           [wgu[s].B, hfb.B], [PB[bu]])
                        act(sg_[:], bk(bg), AF.Silu, [PB[bg]], [sg_.B])
                        tt(actT[:, j, n0:n0 + 512], sg_[:], bk(bu), ALU.mult, [sg_.B, PB[bu]], [actT.B])
                    if inter:
                        inter.pop(0)()
                while inter:
                    inter.pop(0)()

            def down_phase(hf):
                cnt = 0
                pend_ss = []
                for i in range(8):
                    s = (hf * 8 + i) % 2
                    for nt in range(2):
                        n0 = 512 * nt
                        b = cnt % 4
                        sc_ = sqc[cnt % 2]
                        cnt += 1
                        pe([(bk(b), wd[s][:, k, :], actT[:, k, n0:n0 + 512], k == 0, k == 21) for k in range(22)],
                           [wd[s].B, actT.B], [PB[b]])
                        act(y2[:, i, n0:n0 + 512], bk(b), AF.Copy, [PB[b]], [y2.B])
                        act(sc_[:], bk(b), AF.Square, [PB[b]], [sc_.B])
                        if pend_ss:
                            (pb_, ps_, pf_, pl_) = pend_ss.pop(0)
                            pe([(bk(pb_), ones[:], ps_[:], pf_, pl_)], [ones.B, ps_.B], [PB[pb_]])
                        pend_ss.append((4 + nt, sc_, i == 0, i == 7))
                    if i + 2 < 8:
                        load_wd(hf, i + 2)
                while pend_ss:
                    (pb_, ps_, pf_, pl_) = pend_ss.pop(0)
                    pe([(bk(pb_), ones[:], ps_[:], pf_, pl_)], [ones.B, ps_.B], [PB[pb_]])
                for nt in range(2):
                    n0 = 512 * nt
                    rstd_ops(r4[:, n0:n0 + 512], r4.B, bk(4 + nt), PB[4 + nt], 1.0 / 1024)

            for hf in range(2):
                S.dma("sp", x1h[hf][:], x1Sv[:, :, 1024 * hf:1024 * hf + 1024], dX1[hf], reads=[Bx1S], writes=[x1h[hf].B])
            load_gu(0, 0)
            load_gu(0, 1)
            prologue(0)
            gu_phase(0, prologue_units(1))
            load_gu(1, 0)
            load_gu(1, 1)
            prologue_finish(1)
            down_phase(0)
            gu_phase(1, epi_ops(0))
            S.dma("sp", outTv[:, :, 0:1024], y2[:], dOut, reads=[y2.B])
            down_phase(1)
            for f in epi_ops(1):
                f()
            S.dma("sp", outTv[:, :, 1024:2048], y2[:], dOut, reads=[y2.B])
            S.barrier()
            S.mark("C2")

        if cut:
            S.cut(cut)
        S.emit(None)
        S.check()
        with nc.Block() as block:
            @block.tensor
            def _(e):
                S.play("pe", e)

            @block.scalar
            def _(e):
                S.play("act", e)

            @block.vector
            def _(e):
                S.play("dve", e)

            @block.gpsimd
            def _(e):
                S.play("pool", e)

            @block.sync
            def _(e):
                S.play("sp", e)
    return nc


def make_inputs(inputs, core):
    x = np.asarray(inputs["x"], np.float32)[0]
    pos = np.asarray(inputs["positions"], np.int32)[0]
    o0 = core * OWN
    xr = np.roll(x, -o0, axis=0)
    xTc = np.ascontiguousarray(xr.T)
    posr = np.ascontiguousarray(np.roll(pos, -o0)[None, :])
    xh = np.zeros((16, 1024), np.float32)
    if o0 - 8 >= 0:
        xh[0:8] = x[o0 - 8:o0]
    if o0 + OWN + 8 <= S_TOK:
        xh[8:16] = x[o0 + OWN:o0 + OWN + 8]
    xhT = np.ascontiguousarray(xh.T)
    return xTc, xhT, posr


_CONSTS = {}


def const_tables(core):
    if core in _CONSTS:
        return _CONSTS[core]
    inv = (1.0 / (10000.0 ** (np.arange(0, 32, 2, dtype=np.float32) / np.float32(32)))).astype(np.float32)
    cst = np.zeros((128, 4), np.float32)
    for p in range(128):
        cst[p, 0] = inv[p % 16]
        cst[p, 1] = -1.0 if (p % 32) < 16 else 1.0
        cst[p, 2] = 0.0
        cst[p, 3] = EPS
    t = np.arange(core * OWN, (core + 1) * OWN)
    invcnt = np.zeros((4, OWN), np.float32)
    for g, w in enumerate(POOL_W):
        left = w // 2
        right = w - left - 1
        cnt = (np.minimum(t + right, S_TOK - 1) - np.maximum(t - left, 0) + 1).astype(np.float32)
        invcnt[g] = (1.0 / cnt).astype(np.float32)
    _CONSTS[core] = (cst, invcnt)
    return _CONSTS[core]


def chunk_cols(v):
    v = np.asarray(v, np.float32)
    return np.ascontiguousarray(v.reshape(-1, 128).T)


_NC = {}


def kernel(x, positions, g_mix_pre, w_in, g_q_lat, w_uq, g_kv_lat, w_ukv, w_o_attn,
           w_pool_group, pool_scale, w_o_pool, w_out, g_mix_post, g_ffn_pre,
           w_gate_up, w_down, g_ffn_post, _debug=False):
    inputs = {"x": x, "positions": positions}
    gains = np.concatenate([chunk_cols(g_mix_pre), chunk_cols(g_q_lat), chunk_cols(g_kv_lat),
                            chunk_cols(pool_scale), chunk_cols(g_mix_post), chunk_cols(g_ffn_pre),
                            chunk_cols(g_ffn_post)], axis=1)
    gains = np.ascontiguousarray(gains, dtype=np.float32)
    assert gains.shape == (128, 41)
    f = lambda a: np.ascontiguousarray(np.asarray(a, np.float32))
    common = dict(gains=gains, w_in=f(w_in), w_uq=f(w_uq), w_ukv=f(w_ukv), w_o_attn=f(w_o_attn),
                  w_pool_group=f(w_pool_group), w_o_pool=f(w_o_pool), w_out=f(w_out),
                  w_gate_up=f(w_gate_up), w_down=f(w_down))
    in_maps = []
    for c in range(NCORES):
        xTc, xhT, posr = make_inputs(inputs, c)
        cst, invcnt = const_tables(c)
        m = dict(common)
        m.update(xT=xTc, xh=xhT, pos=posr, cst=cst, invcnt=invcnt)
        in_maps.append(m)
    key = bool(_debug)
    if key not in _NC:
        _NC[key] = build_program(debug=key)
    nc = _NC[key]
    res = run_bass_kernel_spmd(nc, in_maps, core_ids=list(range(NCORES)))
    if _debug:
        return res
    outT = np.concatenate([np.asarray(r["outT"]) for r in res.results], axis=1)
    return np.ascontiguousarray(outT.T)[None, :, :].astype(np.float32)
```
